# Optimizing a Trainium2 kernel written in Bass

```python
import math
import jax, jax.numpy as jnp
from jax import lax
import numpy as np

D_MODEL = 1024
BATCH = 8
SEQ = 2048
DEPTH = 1

DA_HEADS = 8
DA_HEAD_DIM = 64
DA_V_DIM = 2 * DA_HEAD_DIM
DA_QK_WIDTH = DA_HEADS * 2 * DA_HEAD_DIM
DA_WIDTH = DA_HEADS * DA_V_DIM
SUBLN_EPS = 1e-5
DL_HEADS = 16
DL_HEAD_DIM = 64
DL_WIDTH = DL_HEADS * DL_HEAD_DIM
DL_DILATIONS = (1, 4, 16)
DL_OFFSETS = 128
BLOCK = 128
REL_BUCKETS = 32
REL_MAX_DISTANCE = 2048
N_REL_HEADS = DA_HEADS + DL_HEADS
DEEPNORM_ALPHA = (2.0 * DEPTH) ** 0.25
DEEPNORM_BETA = (8.0 * DEPTH) ** -0.25
LN_EPS = 1e-5
IN_SPLIT_SIZES = (DA_QK_WIDTH, DA_QK_WIDTH, DA_WIDTH, DA_WIDTH, DL_WIDTH, DL_WIDTH, DL_WIDTH, DL_WIDTH, D_MODEL, D_MODEL)
IN_COL_SCALES = (1.0, 1.0, DEEPNORM_BETA, 1.0, 1.0, 1.0, DEEPNORM_BETA, 1.0, 1.0, 1.0)
N_IN = sum(IN_SPLIT_SIZES)

kernel_name = "hybrid_diffattn_dilated_gated_merge"


def rel_bucket(dist):
    dist = jnp.maximum(dist, 0).astype(jnp.int32)
    max_exact = REL_BUCKETS // 2
    distf = jnp.maximum(dist, 1).astype(jnp.float32)
    large = max_exact + (jnp.log(distf / max_exact) / math.log(REL_MAX_DISTANCE / max_exact)
                         * (REL_BUCKETS - max_exact)).astype(jnp.int32)
    large = jnp.minimum(large, REL_BUCKETS - 1)
    return jnp.where(dist < max_exact, dist, large)


def layer_norm(x, g, b):
    xf = x.astype(jnp.float32)
    mu = jnp.mean(xf, -1, keepdims=True)
    var = jnp.mean(jnp.square(xf - mu), -1, keepdims=True)
    return ((xf - mu) * lax.rsqrt(var + LN_EPS) * g + b).astype(x.dtype)


def diff_attention(q, k, v, table_a, lam, subln_w, lam_init):
    B_, S_ = q.shape[:2]
    nb = S_ // BLOCK
    scale = DA_HEAD_DIM ** -0.5
    qb = jnp.moveaxis(q.reshape(B_, nb, BLOCK, DA_HEADS, 2, DA_HEAD_DIM), 1, 0)
    kpos = jnp.arange(S_)
    vf = v.astype(jnp.float32)

    def one_block(args):
        qblk, bi = args
        qpos = bi * BLOCK + jnp.arange(BLOCK)
        rel = qpos[:, None] - kpos[None, :]
        bias = table_a[rel_bucket(rel)].transpose(2, 0, 1)
        s = jnp.einsum('bqhmd,bkhmd->bhmqk', qblk, k).astype(jnp.float32) * scale + bias[:, None]
        s = jnp.where(rel[None, None, None] >= 0, s, -jnp.inf)
        p = jax.nn.softmax(s, axis=-1)
        a = p[:, :, 0] - lam * p[:, :, 1]
        return jnp.einsum('bhqk,bkhe->bqhe', a, vf)

    o = lax.map(one_block, (qb, jnp.arange(nb)))
    o = jnp.moveaxis(o, 0, 1).reshape(B_, S_, DA_HEADS, DA_V_DIM)
    o = o * lax.rsqrt(jnp.mean(jnp.square(o), -1, keepdims=True) + SUBLN_EPS) * subln_w
    o = o * (1.0 - lam_init)
    return o.reshape(B_, S_, DA_WIDTH).astype(q.dtype)


def dilated_pattern(q, k, v, table_b, dilation):
    B_, S_, H, dh = q.shape
    L = S_ // dilation
    nb = -(-L // BLOCK)
    Lp = nb * BLOCK
    scale = DL_HEAD_DIM ** -0.5

    def to_sub(t):
        t = t.reshape(B_, L, dilation, H, dh).transpose(0, 2, 3, 1, 4)
        t = jnp.pad(t, ((0, 0), (0, 0), (0, 0), (0, Lp - L), (0, 0)))
        return t.reshape(B_, dilation, H, nb, BLOCK, dh)

    def with_prev(t):
        prev = jnp.pad(t, ((0, 0), (0, 0), (0, 0), (1, 0), (0, 0), (0, 0)))[:, :, :, :-1]
        return jnp.concatenate([prev, t], axis=4)

    qs = to_sub(q)
    kw = with_prev(to_sub(k))
    vw = with_prev(to_sub(v)).astype(jnp.float32)
    i = jnp.arange(BLOCK)[:, None]
    j = jnp.arange(2 * BLOCK)[None, :]
    delta = i - j + BLOCK
    band = (delta >= 0) & (delta <= DL_OFFSETS)
    first = (jnp.arange(nb)[:, None, None] > 0) | (j[None] >= BLOCK)
    mask = band[None] & first
    bias = table_b[rel_bucket(delta * dilation)].transpose(2, 0, 1)
    s = jnp.einsum('brhnqd,brhnkd->brhnqk', qs, kw).astype(jnp.float32) * scale + bias[:, None]
    s = jnp.where(mask, s, -jnp.inf)
    m = jnp.max(s, axis=-1)
    e = jnp.exp(s - m[..., None])
    den = jnp.sum(e, axis=-1)
    o = jnp.einsum('brhnqk,brhnkd->brhnqd', e, vw) / den[..., None]

    def from_sub(t):
        t = t.reshape((B_, dilation, H, Lp) + t.shape[5:])[:, :, :, :L]
        t = jnp.moveaxis(t, 3, 1)
        return t.reshape((B_, S_, H) + t.shape[4:])

    return from_sub(o), from_sub(m), from_sub(den)


def dilated_attention(q, k, v, table_b):
    outs = [dilated_pattern(q, k, v, table_b, d) for d in DL_DILATIONS]
    ms = jnp.stack([r[1] for r in outs])
    dens = jnp.stack([r[2] for r in outs])
    os_ = jnp.stack([r[0] for r in outs])
    w = dens * jnp.exp(ms - jnp.max(ms, axis=0, keepdims=True))
    o = jnp.sum(w[..., None] * os_, axis=0) / jnp.sum(w, axis=0)[..., None]
    B_, S_ = q.shape[:2]
    return o.reshape(B_, S_, DL_WIDTH).astype(q.dtype)


def setup_inputs(seed: int = 0) -> dict:
    key = jax.random.key(seed)
    ks = jax.random.split(key, 16)
    f32 = jnp.float32
    x = jax.random.normal(ks[0], (BATCH, SEQ, D_MODEL), f32)
    col_scale = jnp.concatenate([jnp.full((n,), s, f32) for n, s in zip(IN_SPLIT_SIZES, IN_COL_SCALES)])
    w_in = jax.random.normal(ks[1], (DEPTH, D_MODEL, N_IN), f32) * (D_MODEL ** -0.5) * col_scale
    b_gate = 0.1 * jax.random.normal(ks[2], (DEPTH, 2, D_MODEL), f32)
    w_proj_a = jax.random.normal(ks[3], (DEPTH, DA_WIDTH, D_MODEL), f32) * (DA_WIDTH ** -0.5) * DEEPNORM_BETA
    w_proj_b = jax.random.normal(ks[4], (DEPTH, DL_WIDTH, D_MODEL), f32) * (DL_WIDTH ** -0.5) * DEEPNORM_BETA
    w_out = jax.random.normal(ks[5], (DEPTH, D_MODEL, D_MODEL), f32) * (D_MODEL ** -0.5) * DEEPNORM_BETA
    da_lambda_q1 = 0.1 * jax.random.normal(ks[6], (DEPTH, DA_HEAD_DIM), f32)
    da_lambda_k1 = 0.1 * jax.random.normal(ks[7], (DEPTH, DA_HEAD_DIM), f32)
    da_lambda_q2 = 0.1 * jax.random.normal(ks[8], (DEPTH, DA_HEAD_DIM), f32)
    da_lambda_k2 = 0.1 * jax.random.normal(ks[9], (DEPTH, DA_HEAD_DIM), f32)
    da_subln_w = 1.0 + 0.02 * jax.random.normal(ks[10], (DEPTH, DA_V_DIM), f32)
    rel_bias = 0.2 * jax.random.normal(ks[11], (REL_BUCKETS, N_REL_HEADS), f32)
    ln_g = 1.0 + 0.02 * jax.random.normal(ks[12], (DEPTH, D_MODEL), f32)
    ln_b = 0.02 * jax.random.normal(ks[13], (DEPTH, D_MODEL), f32)
    return {"x": x, "w_in": w_in, "b_gate": b_gate, "w_proj_a": w_proj_a, "w_proj_b": w_proj_b,
            "w_out": w_out, "da_lambda_q1": da_lambda_q1, "da_lambda_k1": da_lambda_k1,
            "da_lambda_q2": da_lambda_q2, "da_lambda_k2": da_lambda_k2, "da_subln_w": da_subln_w,
            "rel_bias": rel_bias, "ln_g": ln_g, "ln_b": ln_b}


def reference(x, w_in, b_gate, w_proj_a, w_proj_b, w_out, da_lambda_q1, da_lambda_k1,
              da_lambda_q2, da_lambda_k2, da_subln_w, rel_bias, ln_g, ln_b):
    B_, S_, _ = x.shape
    split_idx = tuple(int(c) for c in np.cumsum(IN_SPLIT_SIZES)[:-1])
    table_a = rel_bias[:, :DA_HEADS]
    table_b = rel_bias[:, DA_HEADS:]
    for layer in range(DEPTH):
        h = jnp.einsum('bsd,dn->bsn', x, w_in[layer])
        aq, ak, av, az, bq, bk, bv, bz, ga, gb = jnp.split(h, split_idx, axis=-1)
        lam_init = 0.8 - 0.6 * math.exp(-0.3 * layer)
        lam = (jnp.exp(jnp.sum(da_lambda_q1[layer].astype(jnp.float32) * da_lambda_k1[layer]))
               - jnp.exp(jnp.sum(da_lambda_q2[layer].astype(jnp.float32) * da_lambda_k2[layer])) + lam_init)
        ya = diff_attention(aq.reshape(B_, S_, DA_HEADS, 2, DA_HEAD_DIM),
                            ak.reshape(B_, S_, DA_HEADS, 2, DA_HEAD_DIM),
                            av.reshape(B_, S_, DA_HEADS, DA_V_DIM),
                            table_a, lam, da_subln_w[layer].astype(jnp.float32), lam_init)
        yb = dilated_attention(bq.reshape(B_, S_, DL_HEADS, DL_HEAD_DIM),
                               bk.reshape(B_, S_, DL_HEADS, DL_HEAD_DIM),
                               bv.reshape(B_, S_, DL_HEADS, DL_HEAD_DIM), table_b)
        ua = jnp.einsum('bse,ed->bsd', ya * jax.nn.silu(az), w_proj_a[layer])
        ub = jnp.einsum('bse,ed->bsd', yb * jax.nn.silu(bz), w_proj_b[layer])
        merged = jax.nn.sigmoid(ga + b_gate[layer, 0]) * ua + jax.nn.sigmoid(gb + b_gate[layer, 1]) * ub
        sub = jnp.einsum('bsd,de->bse', merged, w_out[layer])
        x = layer_norm(DEEPNORM_ALPHA * x + sub, ln_g[layer], ln_b[layer])
    return x
```

```python
import math
from contextlib import ExitStack

import numpy as np
import ml_dtypes

import concourse.bass as bass
import concourse.mybir as mybir
from concourse.bass_utils import run_bass_kernel_spmd

F32 = mybir.dt.float32
BF16 = mybir.dt.bfloat16
AF = mybir.ActivationFunctionType
ALU = mybir.AluOpType
AX = mybir.AxisListType

S_TOK = 2048
D = 1024
NCHUNK = 80
LAM_INIT = 0.2
ALPHA = 2.0 ** 0.25
SUBLN_EPS = 1e-5
LN_EPS = 1e-5
RA = 2560
TW = 2432
RB = 384
DILS = (1, 4, 16)
NEG = -30000.0

DEBUG = False


class _Buf:
    __slots__ = ("w", "r")

    def __init__(self):
        self.w = None
        self.r = {}


class Sched:
    def __init__(self, nc, n_dma=32):
        self.nc = nc
        self.engs = {"pe": nc.tensor, "act": nc.scalar, "dve": nc.vector, "pool": nc.gpsimd, "sp": nc.sync}
        self.sems = {}
        for e in ("pe", "act", "dve", "pool"):
            self.sems[e] = nc.alloc_semaphore(name=f"sem_{e}")
        self.n_dma = n_dma
        for i in range(n_dma):
            self.sems[("d", i)] = nc.alloc_semaphore(name=f"sem_dma{i}")
        self.dma_cnt = [0] * n_dma
        self.dma_rr = {"pool": 0, "sp": n_dma // 2}
        self.cnt = {e: 0 for e in ("pe", "act", "dve", "pool")}
        self.known = {e: {} for e in self.engs}
        self.bufs = {}
        self.n_waits = 0
        self.swdge_out = []
        self.swdge_limit = 6

    def _b(self, name):
        b = self.bufs.get(name)
        if b is None:
            b = self.bufs[name] = _Buf()
        return b

    def _deps(self, reads, writes):
        ev = {}

        def add(k, v):
            if ev.get(k, 0) < v:
                ev[k] = v

        for b in reads:
            if b.w is not None:
                add(*b.w)
        for b in writes:
            if b.w is not None:
                add(*b.w)
            for k, v in b.r.items():
                add(k, v)
        return ev

    def _wait(self, eng, ev):
        kn = self.known[eng]
        for k, v in ev.items():
            if eng == "pe" and k == "pe":
                continue
            if kn.get(k, 0) >= v:
                continue
            self.engs[eng].wait_ge(self.sems[k], v)
            kn[k] = v
            self.n_waits += 1

    def _commit(self, event, reads, writes):
        k, v = event
        for b in writes:
            b.w = event
            b.r = {}
        for b in reads:
            if b in writes:
                continue
            if b.r.get(k, 0) < v:
                b.r[k] = v

    def op(self, eng, fn, reads=(), writes=()):
        rb = [self._b(x) for x in reads]
        wb = [self._b(x) for x in writes]
        self._wait(eng, self._deps(rb, wb))
        ins = fn()
        self.cnt[eng] += 1
        ins.then_inc(self.sems[eng], 1)
        self._commit((eng, self.cnt[eng]), rb, wb)

    def dma(self, q, out, in_, reads=(), writes=()):
        rb = [self._b(x) for x in reads]
        wb = [self._b(x) for x in writes]
        self._wait(q, self._deps(rb, wb))
        if q == "pool" and len(self.swdge_out) >= self.swdge_limit:
            k0, v0 = self.swdge_out.pop(0)
            self._wait("pool", {k0: v0})
        half = self.n_dma // 2
        i = self.dma_rr[q]
        base = 0 if q == "pool" else half
        self.dma_rr[q] = base + (i - base + 1) % half
        ins = self.engs[q].dma_start(out=out, in_=in_)
        self.dma_cnt[i] += 16
        ins.then_inc(self.sems[("d", i)], 16)
        if q == "pool":
            self.swdge_out.append((("d", i), self.dma_cnt[i]))
        self._commit((("d", i), self.dma_cnt[i]), rb, wb)

    def barrier(self):
        ev = {e: self.cnt[e] for e in ("pe", "act", "dve", "pool") if self.cnt[e] > 0}
        for i in range(self.n_dma):
            if self.dma_cnt[i] > 0:
                ev[("d", i)] = self.dma_cnt[i]
        for e in self.engs:
            self._wait(e, ev)

    def wait_all_dma(self, eng):
        ev = {("d", i): self.dma_cnt[i] for i in range(self.n_dma) if self.dma_cnt[i] > 0}
        self._wait(eng, ev)


def _bucket(d):
    d = np.maximum(np.asarray(d), 0).astype(np.int32)
    distf = np.maximum(d, 1).astype(np.float32)
    large = 16 + (np.log(distf / np.float32(16)) / np.float32(math.log(2048 / 16)) * np.float32(16)).astype(np.int32)
    large = np.minimum(large, 31)
    return np.where(d < 16, d, large)


def _onehots():
    oha = np.zeros((33, RA), np.float32)
    r = np.arange(RA) - 511
    bk = _bucket(r)
    for i in range(RA):
        if r[i] < 0:
            oha[32, i] = 1.0
        else:
            oha[bk[i], i] = 1.0
    ohb = np.zeros((33, 3 * RB), np.float32)
    for di, d in enumerate(DILS):
        dl = np.arange(RB) - 127
        bk = _bucket(dl * d)
        for i in range(RB):
            if 0 <= dl[i] <= 128:
                ohb[bk[i], di * RB + i] = 1.0
            else:
                ohb[32, di * RB + i] = 1.0
    return oha, ohb


def build_nc():
    nc = bass.Bass("TRN2", target_bir_lowering=False)
    dt = nc.dram_tensor
    xT_d = dt("xT", [D, S_TOK], F32, kind="ExternalInput").ap()
    xtok_d = dt("xtok", [S_TOK, D], F32, kind="ExternalInput").ap()
    win_d = dt("win", [NCHUNK, 128, 1024], F32, kind="ExternalInput").ap()
    pa_d = dt("pa", [8, 128, 1024], F32, kind="ExternalInput").ap()
    pb_d = dt("pb", [8, 128, 1024], F32, kind="ExternalInput").ap()
    wout_d = dt("wout", [D, D], F32, kind="ExternalInput").ap()
    bg_d = dt("bg", [128, 16], F32, kind="ExternalInput").ap()
    lam_d = dt("lam", [4, 64], F32, kind="ExternalInput").ap()
    subw_d = dt("subw", [128, 1], F32, kind="ExternalInput").ap()
    relb_d = dt("relb", [32, 24], F32, kind="ExternalInput").ap()
    lng_d = dt("lng", [1, D], F32, kind="ExternalInput").ap()
    lnb_d = dt("lnb", [1, D], F32, kind="ExternalInput").ap()
    oha_d = dt("oha", [33, RA], F32, kind="ExternalInput").ap()
    ohb_d = dt("ohb", [33, 3 * RB], F32, kind="ExternalInput").ap()
    ident_d = dt("ident", [128, 128], F32, kind="ExternalInput").ap()
    jmat_d = dt("jmat", [128, 128], F32, kind="ExternalInput").ap()
    out_d = dt("out", [S_TOK, D], F32, kind="ExternalOutput").ap()
    sa_d = dt("sa_scr", [8, RA], BF16, kind="Internal").ap()
    sb_d = dt("sb_scr", [16, 3 * RB], BF16, kind="Internal").ap()
    if DEBUG:
        dbg_a = dt("dbg_a", [128, 8 * S_TOK], F32, kind="ExternalOutput").ap()
        dbg_b = dt("dbg_b", [128, 8 * S_TOK], F32, kind="ExternalOutput").ap()
        dbg_m = dt("dbg_m", [128, 8 * S_TOK], BF16, kind="ExternalOutput").ap()

    S = Sched(nc)
    pe, act, dve, pool = nc.tensor, nc.scalar, nc.vector, nc.gpsimd

    es = ExitStack()

    def sb(stack, name, shape, dtype):
        return stack.enter_context(nc.sbuf_tensor(name, shape, dtype))

    xT = sb(es, "xT_sb", [128, 8, S_TOK], BF16)
    yag = sb(es, "yag", [128, 8, S_TOK], BF16)
    ybg = sb(es, "ybg", [128, 8, S_TOK], BF16)
    ident = sb(es, "ident_sb", [128, 128], BF16)
    jmat = sb(es, "jmat_sb", [128, 128], BF16)
    ones_bf = sb(es, "ones_bf", [128, 128], BF16)
    ones_f = sb(es, "ones_f", [128, 128], F32)
    neglam = sb(es, "neglam", [128, 1], F32)
    subw = sb(es, "subw_sb", [128, 1], F32)
    bg = sb(es, "bg_sb", [128, 16], F32)
    smallf = sb(es, "smallf", [128, 16], F32)
    ps = es.enter_context(nc.psum_tensor("ps", [128, 4096], F32))
    psb = ps[:].bitcast(BF16)

    def bank(i, rows=slice(0, 128), c0=0, c1=512):
        return ps[rows, i * 512 + c0:i * 512 + c1]

    def bankbf(i):
        return psb[:, i * 1024:(i + 1) * 1024]

    def pn(i):
        return f"ps{i}"

    act_state = {"f": None}
    warm = sb(es, "act_warm", [128, 4], F32)
    S.op("dve", lambda: dve.memset(warm[:], 1.0), writes=["warm_in"])

    def act_fn(func):
        if act_state["f"] is not func:
            act_state["f"] = func
            S.op("act", lambda: act.activation(out=warm[:, 2:3], in_=warm[:, 0:1], func=func), reads=["warm_in"],
                 writes=["warm_out"])
        return func

    for kc in range(8):
        S.dma("pool", xT[:, kc, :], xT_d[kc * 128:(kc + 1) * 128, :], writes=[f"xT{kc}"])
    XT_NAMES = [f"xT{kc}" for kc in range(8)]
    S.dma("pool", ident[:], ident_d[:], writes=["ident"])
    S.dma("pool", jmat[:], jmat_d[:], writes=["jmat"])
    S.dma("sp", subw[:], subw_d[:], writes=["subw"])
    S.dma("sp", bg[:], bg_d[:], writes=["bg"])
    S.op("dve", lambda: dve.memset(ones_bf[:], 1.0), writes=["ones_bf"])
    S.op("dve", lambda: dve.memset(ones_f[:], 1.0), writes=["ones_f"])

    st_a = ExitStack()
    NW = 8
    wch = [sb(st_a, f"wch{i}", [128, 1024], BF16) for i in range(NW)]
    wstate = {"rr": 0}

    def load_chunk(src_ap):
        i = wstate["rr"]
        wstate["rr"] = (i + 1) % NW
        S.dma("pool", wch[i][:], src_ap, writes=[f"wch{i}"])
        return i

    qT = sb(st_a, "qT", [128, S_TOK], BF16)
    kT = sb(st_a, "kT", [128, S_TOK], BF16)
    aT = sb(st_a, "aT", [128, S_TOK], BF16)
    zT = sb(st_a, "zT", [128, S_TOK], BF16)
    NP = 10
    Pb = [sb(st_a, f"P{i}", [128, 512], BF16) for i in range(NP)]
    pstate = {"rr": 0}

    def next_p():
        i = pstate["rr"]
        pstate["rr"] = (i + 1) % NP
        return i

    tf = [sb(st_a, f"tf{i}", [128, 512], F32) for i in range(5)]
    sqb = sb(st_a, "sqb", [128, 512], BF16)

    st_s = ExitStack()
    tabx = sb(st_s, "tabx", [33, 24], F32)
    oha = sb(st_s, "oha_sb", [33, RA], F32)
    ohb = sb(st_s, "ohb_sb", [33, 3 * RB], F32)
    ega = sb(st_s, "ega", [24, RA], BF16)
    egb = sb(st_s, "egb", [24, 3 * RB], BF16)
    lamv = sb(st_s, "lamv", [128, 4, 64], F32)
    lamt = sb(st_s, "lamt", [128, 2, 64], F32)
    S.op("dve", lambda: dve.memset(tabx[32:33, :], NEG), writes=["tabx_m"])
    S.dma("sp", tabx[0:32, :], relb_d[:], writes=["tabx"])
    S.dma("sp", oha[:], oha_d[:], writes=["oha"])
    S.dma("sp", ohb[:], ohb_d[:], writes=["ohb"])
    S.dma("sp", lamv[:].rearrange("p a b -> p (a b)"),
          bass.AP(tensor=lam_d.tensor, offset=0, ap=[[0, 128], [1, 256]]), writes=["lamv"])
    ring = {"rr": 0}

    def rbank():
        i = ring["rr"]
        ring["rr"] = (i + 1) % 4
        return 4 + i

    def setup_g(oh, ohname, eg, egname, width):
        c = 0
        while c < width:
            n = min(512, width - c)
            bi = rbank()
            S.op("pe", lambda bi=bi, c=c, n=n: pe.matmul(bank(bi, slice(0, 24), 0, n), lhsT=tabx[0:33, :],
                                                         rhs=oh[0:33, c:c + n], start=True, stop=True),
                 reads=["tabx", "tabx_m", ohname], writes=[pn(bi)])
            S.op("act", lambda bi=bi, c=c, n=n: act.activation(out=eg[0:24, c:c + n], in_=bank(bi, slice(0, 24), 0, n),
                                                               func=AF.Exp),
                 reads=[pn(bi)], writes=[egname])
            c += n

    setup_g(oha, "oha", ega, "ega", RA)
    setup_g(ohb, "ohb", egb, "egb", 3 * RB)
    S.dma("sp", sa_d[:], ega[0:8, :], reads=["ega"], writes=["SA"])
    S.dma("sp", sb_d[:], egb[8:24, :], reads=["egb"], writes=["SB"])

    S.op("dve", lambda: dve.tensor_tensor(out=lamt[:, 0, :], in0=lamv[:, 0, :], in1=lamv[:, 1, :], op=ALU.mult),
         reads=["lamv"], writes=["lamt0"])
    S.op("dve", lambda: dve.tensor_tensor(out=lamt[:, 1, :], in0=lamv[:, 2, :], in1=lamv[:, 3, :], op=ALU.mult),
         reads=["lamv"], writes=["lamt1"])
    S.op("dve", lambda: dve.reduce_sum(out=smallf[:, 0:1], in_=lamt[:, 0, :], axis=AX.X), reads=["lamt0"], writes=["sm0"])
    S.op("dve", lambda: dve.reduce_sum(out=smallf[:, 1:2], in_=lamt[:, 1, :], axis=AX.X), reads=["lamt1"], writes=["sm1"])
    S.op("act", lambda: act.activation(out=smallf[:, 2:4], in_=smallf[:, 0:2], func=AF.Exp), reads=["sm0", "sm1"],
         writes=["sm23"])
    S.op("dve", lambda: dve.tensor_tensor(out=smallf[:, 4:5], in0=smallf[:, 3:4], in1=smallf[:, 2:3], op=ALU.subtract),
         reads=["sm23"], writes=["sm4"])
    S.op("dve", lambda: dve.tensor_scalar(out=neglam[:], in0=smallf[:, 4:5], scalar1=-LAM_INIT, scalar2=None, op0=ALU.add),
         reads=["sm4"], writes=["neglam"])
    S.op("dve", lambda: dve.tensor_scalar(out=smallf[:, 5:6], in0=subw[:], scalar1=1.0 - LAM_INIT, scalar2=None,
                                          op0=ALU.mult), reads=["subw"], writes=["subw_s"])
    S.op("dve", lambda: dve.memset(smallf[:, 6:7], SUBLN_EPS), writes=["eps_a"])
    S.op("dve", lambda: dve.memset(smallf[:, 7:8], LN_EPS), writes=["eps_l"])
    subw_s = smallf[:, 5:6]
    eps_a = smallf[:, 6:7]
    eps_l = smallf[:, 7:8]

    def proj(wi, dst, dname, kind):
        w = wch[wi]
        for tb in range(4):
            bi = rbank()

            def mm(bi=bi, tb=tb):
                ins = None
                for kc in range(8):
                    ins = pe.matmul(bank(bi), lhsT=w[:, kc * 128:(kc + 1) * 128], rhs=xT[:, kc, tb * 512:(tb + 1) * 512],
                                    start=(kc == 0), stop=(kc == 7))
                return ins

            S.op("pe", mm, reads=[f"wch{wi}"] + XT_NAMES, writes=[pn(bi)])
            o = dst[:, tb * 512:(tb + 1) * 512]
            if kind == "silu":
                S.op("act", lambda bi=bi, o=o: act.activation(out=o, in_=bank(bi), func=AF.Silu), reads=[pn(bi)],
                     writes=[f"{dname}{tb}"])
            elif kind == "act":
                S.op("act", lambda bi=bi, o=o: act.copy(out=o, in_=bank(bi)), reads=[pn(bi)], writes=[f"{dname}{tb}"])
            else:
                S.op("dve", lambda bi=bi, o=o: dve.tensor_copy(out=o, in_=bank(bi)), reads=[pn(bi)],
                     writes=[f"{dname}{tb}"])

    def names4(n):
        return [f"{n}{tb}" for tb in range(4)]

    st_pa = ExitStack()
    vtok = sb(st_pa, "vtok", [128, 16, 128], BF16)
    XT = sb(st_pa, "XTh", [128, TW], BF16)
    Ttab = sb(st_pa, "Ttab", [128, TW], BF16)

    def phaseA_prep(h):
        wq = load_chunk(win_d[h])
        wk = load_chunk(win_d[8 + h])
        wv = load_chunk(win_d[16 + h])
        wz = load_chunk(win_d[24 + h])
        S.dma("sp", XT[:], bass.AP(tensor=sa_d.tensor, offset=h * RA, ap=[[1, 128], [1, TW]]), reads=["SA"],
              writes=["XT"])
        return wq, wk, wv, wz

    def phaseA_proj(ws, hook=None):
        wq, wk, wv, wz = ws
        proj(wq, qT, "qT", "dve")
        proj(wk, kT, "kT", "dve")
        if hook is not None:
            hook()
        proj(wv, aT, "aT", "act")
        proj(wz, zT, "zT", "silu")
        for g in range(2):
            bi = rbank()

            def tr(bi=bi, g=g):
                ins = None
                for t8 in range(8):
                    ti = 8 * g + t8
                    ins = pe.transpose(bankbf(bi)[:, t8 * 128:(t8 + 1) * 128], aT[:, ti * 128:(ti + 1) * 128], ident[:])
                return ins

            S.op("pe", tr, reads=names4("aT") + ["ident"], writes=[pn(bi)])
            S.op("act", lambda bi=bi, g=g: act.copy(out=vtok[:, 8 * g:8 * g + 8, :].rearrange("p a b -> p (a b)"),
                                                   in_=bankbf(bi)), reads=[pn(bi)], writes=[f"vtok{g}"])
        c = 0
        while c < TW:
            n = min(512, TW - c)
            bi = rbank()
            S.op("pe", lambda bi=bi, c=c, n=n: pe.matmul(bank(bi, slice(0, 128), 0, n), lhsT=jmat[:], rhs=XT[:, c:c + n],
                                                         start=True, stop=True), reads=["jmat", "XT"], writes=[pn(bi)])
            S.op("dve", lambda bi=bi, c=c, n=n: dve.tensor_copy(out=Ttab[:, c:c + n], in_=bank(bi, slice(0, 128), 0, n)),
                 reads=[pn(bi)], writes=["Ttab"])
            c += n

    OB_, DB_ = (0, 1), (2, 3)
    mmc = {"n": 0}

    def phaseA_attn(h):
        for qb in range(4):
            nkt = 4 * qb + 4
            pending = []

            def flush(keep):
                while len(pending) > keep:
                    pending.pop(0)()

            for kt in range(nkt):
                j = kt - 4 * qb
                c0 = 128 * j if j > 0 else 0
                off = 128 * (4 * qb - kt) + 384
                for m in range(2):
                    rows = slice(64 * m, 64 * m + 64)
                    sbk = rbank()
                    S.op("pe", lambda sbk=sbk, rows=rows, kt=kt, c0=c0, qb=qb: pe.matmul(
                        bank(sbk, slice(0, 128), c0, 512), lhsT=kT[rows, kt * 128:(kt + 1) * 128],
                        rhs=qT[rows, qb * 512 + c0:(qb + 1) * 512], start=True, stop=True),
                        reads=names4("kT") + names4("qT"), writes=[pn(sbk)])
                    pi = next_p()
                    S.op("act", lambda sbk=sbk, pi=pi, c0=c0: act.activation(out=Pb[pi][:, c0:512],
                                                                             in_=bank(sbk, slice(0, 128), c0, 512),
                                                                             func=AF.Exp, scale=0.125),
                         reads=[pn(sbk)], writes=[f"P{pi}"])
                    mmc["n"] += 1
                    if mmc["n"] % 3 != 0:
                        S.op("dve", lambda pi=pi, c0=c0, off=off: dve.tensor_tensor(
                            out=Pb[pi][:, c0:512], in0=Pb[pi][:, c0:512], in1=Ttab[:, off + c0:off + 512], op=ALU.mult),
                            reads=[f"P{pi}", "Ttab"], writes=[f"P{pi}"])
                    else:
                        S.op("pool", lambda pi=pi, c0=c0, off=off: pool.tensor_tensor(
                            out=Pb[pi][:, c0:512], in0=Pb[pi][:, c0:512], in1=Ttab[:, off + c0:off + 512], op=ALU.mult),
                            reads=[f"P{pi}", "Ttab"], writes=[f"P{pi}"])

                    def av(m=m, pi=pi, c0=c0, kt=kt, nkt=nkt):
                        def f():
                            pe.matmul(bank(OB_[m], slice(0, 128), c0, 512), lhsT=vtok[:, kt, :], rhs=Pb[pi][:, c0:512],
                                      start=(kt == 0), stop=(kt == nkt - 1))
                            return pe.matmul(bank(DB_[m], slice(0, 128), c0, 512), lhsT=ones_bf[:], rhs=Pb[pi][:, c0:512],
                                             start=(kt == 0), stop=(kt == nkt - 1))

                        S.op("pe", f, reads=[f"P{pi}", "vtok0", "vtok1", "ones_bf"], writes=[pn(OB_[m]), pn(DB_[m])])

                    pending.append(av)
                flush(4)
                if kt == 3:
                    run_pending_finA()
            flush(0)
            fin_part1()
            finA["p2"] = (lambda h=h, qb=qb: fin_part2(h, qb))

    finA = {"p2": None}

    def run_pending_finA():
        f = finA["p2"]
        if f is not None:
            finA["p2"] = None
            f()

    def fin_part1():
        r0, r1, o0, o1, tt = tf
        S.op("act", lambda: act.activation(out=r0[:], in_=bank(DB_[0]), func=AF.Ln), reads=[pn(DB_[0])], writes=["tf0"])
        S.op("dve", lambda: dve.tensor_copy(out=o0[:], in_=bank(OB_[0])), reads=[pn(OB_[0])], writes=["tf2"])
        S.op("act", lambda: act.activation(out=r1[:], in_=bank(DB_[1]), func=AF.Ln), reads=[pn(DB_[1])], writes=["tf1"])
        S.op("dve", lambda: dve.tensor_copy(out=o1[:], in_=bank(OB_[1])), reads=[pn(OB_[1])], writes=["tf3"])
        S.op("act", lambda: act.activation(out=r0[:], in_=r0[:], func=AF.Exp, scale=-1.0), reads=["tf0"], writes=["tf0"])
        S.op("act", lambda: act.activation(out=r1[:], in_=r1[:], func=AF.Exp, scale=-1.0), reads=["tf1"], writes=["tf1"])
        S.op("dve", lambda: dve.tensor_tensor(out=o0[:], in0=o0[:], in1=r0[:], op=ALU.mult),
             reads=["tf2", "tf0"], writes=["tf2"])
        S.op("dve", lambda: dve.tensor_tensor(out=o1[:], in0=o1[:], in1=r1[:], op=ALU.mult),
             reads=["tf3", "tf1"], writes=["tf3"])
        S.op("dve", lambda: dve.scalar_tensor_tensor(out=o0[:], in0=o1[:], scalar=neglam[:, 0:1], in1=o0[:],
                                                     op0=ALU.mult, op1=ALU.add),
             reads=["tf2", "tf3", "neglam"], writes=["tf2"])
        S.op("dve", lambda: dve.tensor_tensor(out=sqb[:], in0=o0[:], in1=o0[:], op=ALU.mult), reads=["tf2"],
             writes=["sqb"])

    def fin_part2(h, qb):
        r0, r1, o0, o1, tt = tf
        bi = rbank()
        S.op("pe", lambda bi=bi: pe.matmul(bank(bi), lhsT=ones_bf[:], rhs=sqb[:], start=True, stop=True),
             reads=["sqb", "ones_bf"], writes=[pn(bi)])
        S.op("act", lambda bi=bi: act.activation(out=r0[:], in_=bank(bi), func=AF.Ln, bias=eps_a, scale=1.0 / 128.0),
             reads=[pn(bi), "eps_a", "tf0"], writes=["tf0"])
        S.op("act", lambda: act.activation(out=r1[:], in_=r0[:], func=AF.Exp, scale=-0.5), reads=["tf0", "tf1"],
             writes=["tf1"])
        S.op("dve", lambda: dve.tensor_tensor(out=tt[:], in0=o0[:], in1=r1[:], op=ALU.mult), reads=["tf2", "tf1"],
             writes=["tf4"])
        S.op("dve", lambda h=h, qb=qb: dve.scalar_tensor_tensor(
            out=yag[:, h, qb * 512:(qb + 1) * 512], in0=tt[:], scalar=subw_s, in1=zT[:, qb * 512:(qb + 1) * 512],
            op0=ALU.mult, op1=ALU.mult), reads=["tf4", "subw_s", f"zT{qb}"], writes=[f"yag{h}_{qb}"])

    ws = phaseA_prep(0)
    for h in range(8):
        phaseA_proj(ws, run_pending_finA)
        if h + 1 < 8:
            ws = phaseA_prep(h + 1)
        phaseA_attn(h)
    run_pending_finA()

    S.barrier()
    st_pa.close()
    st_s.close()

    st_pb = ExitStack()
    q4 = sb(st_pb, "q4", [128, S_TOK], BF16)
    k4 = sb(st_pb, "k4", [128, S_TOK], BF16)
    q16 = sb(st_pb, "q16", [128, S_TOK], BF16)
    k16 = sb(st_pb, "k16", [128, S_TOK], BF16)
    Vd = sb(st_pb, "Vd", [128, 3, 16, 193], BF16)
    XB = sb(st_pb, "XB", [128, 6 * 256], BF16)
    Btab = sb(st_pb, "Btab", [128, 6, 256], BF16)
    accs = sb(st_pb, "accs", [128, S_TOK], F32)
    zr = [sb(st_pb, f"zr{i}", [128, 512], F32) for i in range(2)]

    S.op("pool", lambda: pool.memset(Vd[:].rearrange("p a b c -> p (a b c)"), 0.0), writes=["Vd"])
    S.op("pool", lambda: pool.memset(Vd[:, :, :, 64:66], 1.0), reads=[], writes=["Vd"])

    def phaseB_prep(j):
        wq = load_chunk(win_d[32 + j])
        wk = load_chunk(win_d[40 + j])
        wv = load_chunk(win_d[48 + j])
        wz = load_chunk(win_d[56 + j])
        S.dma("sp", XB[:].rearrange("p (a b c) -> p a b c", a=2, b=3),
              bass.AP(tensor=sb_d.tensor, offset=2 * j * 3 * RB, ap=[[1, 128], [3 * RB, 2], [RB, 3], [1, 256]]),
              reads=["SB"], writes=["XB"])
        return wq, wk, wv, wz

    qperm = {1: qT, 4: q4, 16: q16}
    kperm = {1: kT, 4: k4, 16: k16}
    sring = {"rr": 0}

    def phaseB_proj(ws, hook=None):
        wq, wk, wv, wz = ws
        proj(wq, qT, "qT", "dve")
        proj(wk, kT, "kT", "dve")
        if hook is not None:
            hook()
        proj(wv, aT, "aT", "act")
        proj(wz, zT, "zT", "silu")
        for d, qd, kd in ((4, q4, k4), (16, q16, k16)):
            if d == 4:
                S.op("dve", lambda d=d, qd=qd: dve.tensor_copy(out=qd[:].rearrange("p (r l) -> p r l", r=d),
                                                              in_=qT[:].rearrange("p (l r) -> p r l", r=d)),
                     reads=names4("qT"), writes=[f"q{d}"])
                S.op("dve", lambda d=d, kd=kd: dve.tensor_copy(out=kd[:].rearrange("p (r l) -> p r l", r=d),
                                                              in_=kT[:].rearrange("p (l r) -> p r l", r=d)),
                     reads=names4("kT"), writes=[f"k{d}"])
            else:
                S.op("act", lambda d=d, qd=qd: act.copy(out=qd[:].rearrange("p (r l) -> p r l", r=d),
                                                       in_=qT[:].rearrange("p (l r) -> p r l", r=d)),
                     reads=names4("qT"), writes=[f"q{d}"])
                S.op("act", lambda d=d, kd=kd: act.copy(out=kd[:].rearrange("p (r l) -> p r l", r=d),
                                                       in_=kT[:].rearrange("p (l r) -> p r l", r=d)),
                     reads=names4("kT"), writes=[f"k{d}"])
        for di, d in enumerate(DILS):
            av = aT[:].rearrange("p (l r) -> p r l", r=d)
            nb = (S_TOK // d) // 128
            for g in range(2):
                bi = rbank()

                def tr(bi=bi, g=g, av=av, nb=nb):
                    ins = None
                    for t8 in range(8):
                        ti = 8 * g + t8
                        r, n = ti // nb, ti % nb
                        ins = pe.transpose(bankbf(bi)[:, t8 * 128:(t8 + 1) * 128], av[:, r, n * 128:(n + 1) * 128],
                                           ident[:])
                    return ins

                S.op("pe", tr, reads=names4("aT") + ["ident"], writes=[pn(bi)])
                src = bankbf(bi).rearrange("p (a b) -> p a b", a=8)
                S.op("act", lambda bi=bi, g=g, di=di, src=src: act.copy(out=Vd[:, di, 8 * g:8 * g + 8, 0:64],
                                                                       in_=src[:, :, 0:64]),
                     reads=[pn(bi)], writes=["Vd"])
                S.op("dve", lambda bi=bi, g=g, di=di, src=src: dve.tensor_copy(out=Vd[:, di, 8 * g:8 * g + 8, 129:193],
                                                                              in_=src[:, :, 64:128]),
                     reads=[pn(bi)], writes=["Vd"])
        for c in range(3):
            bi = rbank()
            S.op("pe", lambda bi=bi, c=c: pe.matmul(bank(bi), lhsT=jmat[:], rhs=XB[:, c * 512:(c + 1) * 512], start=True,
                                                    stop=True), reads=["jmat", "XB"], writes=[pn(bi)])
            S.op("dve", lambda bi=bi, c=c: dve.tensor_copy(out=Btab[:, 2 * c:2 * c + 2, :].rearrange("p a b -> p (a b)"),
                                                          in_=bank(bi)), reads=[pn(bi)], writes=["Btab"])

    def phaseB_attn(j):
        for hh in range(2):
            rows = slice(64 * hh, 64 * hh + 64)
            started = set()
            tiles = []
            for di, d in enumerate(DILS):
                nb = (S_TOK // d) // 128
                for r in range(d):
                    for n in range(nb):
                        tiles.append((di, d, nb, r, n))
            pending = []

            def flush(keep):
                while len(pending) > keep:
                    pending.pop(0)()

            for gi in range(0, len(tiles), 2):
                grp = tiles[gi:gi + 2]
                di, d = grp[0][0], grp[0][1]
                sbk = rbank()

                def qk(grp=grp, sbk=sbk, rows=rows):
                    ins = None
                    for s, (di_, d_, nb, r, n) in enumerate(grp):
                        nq = 256 if n + 1 < nb else 128
                        col = (r * nb + n) * 128
                        ins = pe.matmul(bank(sbk, slice(0, 128), s * 256, s * 256 + nq),
                                        lhsT=kperm[d_][rows, col:col + 128], rhs=qperm[d_][rows, col:col + nq],
                                        start=True, stop=True)
                    return ins

                qn = names4("qT") if d == 1 else [f"q{d}"]
                kn = names4("kT") if d == 1 else [f"k{d}"]
                S.op("pe", qk, reads=qn + kn, writes=[pn(sbk)])
                pi = next_p()
                S.op("act", lambda pi=pi, sbk=sbk: act.activation(out=Pb[pi][:], in_=bank(sbk), func=AF.Exp, scale=0.125),
                     reads=[pn(sbk)], writes=[f"P{pi}"])
                bsl = Btab[:, hh * 3 + di, :]
                bap = bass.AP(tensor=bsl.tensor, offset=bsl.offset, ap=[list(bsl.ap[0]), [0, 2], [1, 256]])
                pv = Pb[pi][:].rearrange("p (a b) -> p a b", a=2)
                if (gi // 2) % 3 != 2:
                    S.op("dve", lambda pv=pv, bap=bap: dve.tensor_tensor(out=pv, in0=pv, in1=bap, op=ALU.mult),
                         reads=[f"P{pi}", "Btab"], writes=[f"P{pi}"])
                else:
                    S.op("pool", lambda pv=pv, bap=bap: pool.tensor_tensor(out=pv, in0=pv, in1=bap, op=ALU.mult),
                         reads=[f"P{pi}", "Btab"], writes=[f"P{pi}"])

                def av(grp=grp, pi=pi, hh=hh):
                    wr = set()
                    plan = []
                    mrows = slice(0, 65) if hh == 0 else slice(0, 128)
                    for s, (di_, d_, nb, r, n) in enumerate(grp):
                        ti = r * nb + n
                        lhs = Vd[:, di_, ti, 0:65] if hh == 0 else Vd[:, di_, ti, 65:193]
                        for half in range(2):
                            nn = n + half
                            if nn >= nb:
                                continue
                            pc0 = s * 256 + half * 128
                            if d_ == 1:
                                b_ = nn // 4
                                plan.append((b_, bank(b_, mrows, (nn % 4) * 128, (nn % 4) * 128 + 128), lhs,
                                             Pb[pi][:, pc0:pc0 + 128]))
                            elif d_ == 4:
                                b_ = nn
                                o = bank(b_, mrows).rearrange("p (j d) -> p d j", d=4)[:, r, :]
                                plan.append((b_, o, lhs, Pb[pi][:, pc0:pc0 + 128]))
                            else:
                                for b_ in range(4):
                                    o = bank(b_, mrows).rearrange("p (j d) -> p d j", d=16)[:, r, :]
                                    plan.append((b_, o, lhs, Pb[pi][:, pc0 + 32 * b_:pc0 + 32 * b_ + 32]))
                    for b_, *_ in plan:
                        wr.add(b_)

                    def f():
                        ins = None
                        for b_, o, lhs, rhs in plan:
                            st = b_ not in started
                            started.add(b_)
                            ins = pe.matmul(o, lhsT=lhs, rhs=rhs, start=st, stop=False, skip_group_check=True)
                        return ins

                    S.op("pe", f, reads=[f"P{pi}", "Vd"], writes=[pn(b_) for b_ in sorted(wr)])

                pending.append(av)
                flush(2)
                if gi == 4:
                    run_pending_finB()
            flush(0)
            finB_part1(hh)
            finB["p2"] = (lambda j=j, hh=hh: finB_part2(j, hh))

    finB = {"p2": None}

    def run_pending_finB():
        f = finB["p2"]
        if f is not None:
            finB["p2"] = None
            f()

    def finB_part1(hh):
        mrows = slice(0, 65) if hh == 0 else slice(0, 128)
        for pc in range(4):
            csl = slice(pc * 512, (pc + 1) * 512)
            if pc % 2 == 0:
                S.op("act", lambda pc=pc, csl=csl, mrows=mrows: act.copy(out=accs[mrows, csl], in_=bank(pc, mrows)),
                     reads=[pn(pc)], writes=[f"accs{pc}"])
            else:
                S.op("dve", lambda pc=pc, csl=csl, mrows=mrows: dve.tensor_copy(out=accs[mrows, csl], in_=bank(pc, mrows)),
                     reads=[pn(pc)], writes=[f"accs{pc}"])
        row = 64 if hh == 0 else 0
        rsl = slice(row, row + 1)
        an = [f"accs{pc}" for pc in range(4)]
        S.op("act", lambda rsl=rsl: act.activation(out=accs[rsl, :], in_=accs[rsl, :], func=AF.Ln), reads=an, writes=an)
        S.op("act", lambda rsl=rsl: act.activation(out=accs[rsl, :], in_=accs[rsl, :], func=AF.Exp, scale=-1.0), reads=an,
             writes=an)

    def finB_part2(j, hh):
        rows = slice(64 * hh, 64 * hh + 64)
        row = 64 if hh == 0 else 0
        rsl = slice(row, row + 1)
        mm_rows = slice(0, 64) if hh == 0 else slice(0, 128)
        mcols = 64 if hh == 0 else 128
        for pc in range(4):
            csl = slice(pc * 512, (pc + 1) * 512)
            bi = rbank()
            S.op("pe", lambda bi=bi, csl=csl: pe.matmul(bank(bi, mm_rows), lhsT=ones_f[rsl, 0:mcols], rhs=accs[rsl, csl],
                                                        start=True, stop=True),
                 reads=[f"accs{pc}", "ones_f"], writes=[pn(bi)])
            k2 = pc % 2
            S.op("dve", lambda bi=bi, k2=k2, csl=csl: dve.tensor_tensor(out=zr[k2][rows, :], in0=bank(bi, rows),
                                                                       in1=accs[rows, csl], op=ALU.mult),
                 reads=[pn(bi), f"accs{pc}"], writes=[f"zr{k2}"])
            S.op("pool", lambda k2=k2, csl=csl: pool.tensor_tensor(out=ybg[rows, j, csl], in0=zr[k2][rows, :],
                                                                  in1=zT[rows, csl], op=ALU.mult),
                 reads=[f"zr{k2}", f"zT{pc}"], writes=[f"ybg{j}_{pc}_{hh}"])

    ws = phaseB_prep(0)
    for j in range(8):
        phaseB_proj(ws, run_pending_finB)
        if j + 1 < 8:
            ws = phaseB_prep(j + 1)
        phaseB_attn(j)
    run_pending_finB()

    S.barrier()
    st_pb.close()
    st_a.close()

    if DEBUG:
        st_d = ExitStack()
        dtmp = sb(st_d, "dtmp", [128, 8 * S_TOK], F32)
        S.op("dve", lambda: dve.tensor_copy(out=dtmp[:], in_=yag[:].rearrange("p a b -> p (a b)")), writes=["dtmp"])
        S.dma("sp", dbg_a[:], dtmp[:], reads=["dtmp"], writes=["dbg_a"])
        S.op("dve", lambda: dve.tensor_copy(out=dtmp[:], in_=ybg[:].rearrange("p a b -> p (a b)")), reads=["dbg_a"],
             writes=["dtmp"])
        S.dma("sp", dbg_b[:], dtmp[:], reads=["dtmp"], writes=["dbg_b"])
        S.barrier()
        st_d.close()

    st_c = ExitStack()
    merged = sb(st_c, "merged", [128, 8, S_TOK], BF16)
    wout = sb(st_c, "wout_sb", [128, 8, D], BF16)
    NWC = 8
    wc = [sb(st_c, f"wc{i}", [128, 1024], BF16) for i in range(NWC)]
    sg = [sb(st_c, f"sg{i}", [128, 512], F32) for i in range(4)]
    t12 = [sb(st_c, f"t12_{i}", [128, 512], F32) for i in range(4)]
    lng = sb(st_c, "lng_sb", [128, D], F32)
    lnb = sb(st_c, "lnb_sb", [128, D], F32)
    xtk = [sb(st_c, f"xtk{i}", [128, D], F32) for i in range(2)]
    yb_ = [sb(st_c, f"ybuf{i}", [128, D], F32) for i in range(2)]
    stats = sb(st_c, "stats", [128, 2, 6], F32)
    mv = sb(st_c, "mv", [128, 8], F32)

    S.dma("sp", lng[:], bass.AP(tensor=lng_d.tensor, offset=0, ap=[[0, 128], [1, D]]), writes=["lng"])
    S.dma("sp", lnb[:], bass.AP(tensor=lnb_d.tensor, offset=0, ap=[[0, 128], [1, D]]), writes=["lnb"])
    wcs = {"rr": 0}

    def load_c(src_ap):
        i = wcs["rr"]
        wcs["rr"] = (i + 1) % NWC
        S.dma("pool", wc[i][:], src_ap, writes=[f"wc{i}"])
        return i

    def prepC(dc):
        return (load_c(pa_d[dc]), load_c(pb_d[dc]), load_c(win_d[64 + dc]), load_c(win_d[72 + dc]))

    pring = {"rr": 0}

    def nbank():
        i = pring["rr"]
        pring["rr"] = (i + 1) % 8
        return i

    cw = prepC(0)
    it = 0
    for dc in range(8):
        wpa, wpb, wga, wgb = cw
        if dc + 1 < 8:
            cw = prepC(dc + 1)
        S.dma("pool", wout[:, dc, :], wout_d[dc * 128:(dc + 1) * 128, :], writes=[f"wout{dc}"])
        for tb in range(4):
            tsl = slice(tb * 512, (tb + 1) * 512)
            k2 = it % 2
            it += 1
            banks = {}
            for nm, wi, src, snames in (("ua", wpa, yag, [f"yag{kc}_{tb}" for kc in range(8)]),
                                        ("ga", wga, xT, XT_NAMES),
                                        ("ub", wpb, ybg, [f"ybg{kc}_{tb}_{hh}" for kc in range(8) for hh in range(2)]),
                                        ("gb", wgb, xT, XT_NAMES)):
                bi = nbank()
                banks[nm] = bi

                def mm(bi=bi, wi=wi, src=src, tsl=tsl):
                    ins = None
                    for kc in range(8):
                        ins = pe.matmul(bank(bi), lhsT=wc[wi][:, kc * 128:(kc + 1) * 128], rhs=src[:, kc, tsl],
                                        start=(kc == 0), stop=(kc == 7))
                    return ins

                S.op("pe", mm, reads=[f"wc{wi}"] + snames, writes=[pn(bi)])
            for gi, nm in enumerate(("ga", "gb")):
                bi = banks[nm]
                sgi = 2 * k2 + gi
                S.op("act", lambda bi=bi, sgi=sgi, gi=gi, dc=dc: act.activation(
                    out=sg[sgi][:], in_=bank(bi), func=AF.Sigmoid, bias=bg[:, gi * 8 + dc:gi * 8 + dc + 1]),
                    reads=[pn(bi), "bg"], writes=[f"sg{sgi}"])
            for gi, nm in enumerate(("ua", "ub")):
                bi = banks[nm]
                sgi = 2 * k2 + gi
                S.op("dve", lambda bi=bi, sgi=sgi: dve.tensor_tensor(out=t12[sgi][:], in0=bank(bi), in1=sg[sgi][:],
                                                                    op=ALU.mult),
                     reads=[pn(bi), f"sg{sgi}"], writes=[f"t12_{sgi}"])
            S.op("pool", lambda k2=k2, dc=dc, tsl=tsl: pool.tensor_tensor(out=merged[:, dc, tsl], in0=t12[2 * k2][:],
                                                                         in1=t12[2 * k2 + 1][:], op=ALU.add),
                 reads=[f"t12_{2 * k2}", f"t12_{2 * k2 + 1}"], writes=[f"mg{dc}_{tb}"])

    def out_stage1(tt):
        k2 = tt % 2
        tb = tt // 4
        S.dma("sp", xtk[k2][:], xtok_d[tt * 128:(tt + 1) * 128, :], writes=[f"xtk{k2}"])
        b0 = nbank()
        b1 = nbank()
        for half, bi in enumerate((b0, b1)):
            def mm(bi=bi, half=half, tt=tt):
                ins = None
                for kc in range(8):
                    ins = pe.matmul(bank(bi), lhsT=merged[:, kc, tt * 128:(tt + 1) * 128],
                                    rhs=wout[:, kc, half * 512:(half + 1) * 512], start=(kc == 0), stop=(kc == 7))
                return ins

            S.op("pe", mm, reads=[f"mg{kc}_{tb}" for kc in range(8)] + [f"wout{kc}" for kc in range(8)], writes=[pn(bi)])
        for half, bi in enumerate((b0, b1)):
            hs = slice(half * 512, (half + 1) * 512)
            S.op("dve", lambda bi=bi, hs=hs, k2=k2: dve.scalar_tensor_tensor(
                out=yb_[k2][:, hs], in0=xtk[k2][:, hs], scalar=ALPHA, in1=bank(bi), op0=ALU.mult, op1=ALU.add),
                reads=[pn(bi), f"xtk{k2}"], writes=[f"yb{k2}_{half}"])
            S.op("dve", lambda hs=hs, k2=k2, half=half: dve.bn_stats(out=stats[:, half, :], in_=yb_[k2][:, hs]),
                 reads=[f"yb{k2}_{half}"], writes=[f"stats{half}"])
        S.op("dve", lambda: dve.bn_aggr(out=mv[:, 0:2], in_=stats[:].rearrange("p a b -> p (a b)")),
             reads=["stats0", "stats1"], writes=["mv01"])
        S.op("act", lambda: act.activation(out=mv[:, 2:3], in_=mv[:, 1:2], func=AF.Ln, bias=eps_l), reads=["mv01", "eps_l"],
             writes=["mv2"])
        S.op("act", lambda: act.activation(out=mv[:, 3:4], in_=mv[:, 2:3], func=AF.Exp, scale=-0.5), reads=["mv2"],
             writes=["mv3"])
        ynames = [f"yb{k2}_0", f"yb{k2}_1"]
        S.op("dve", lambda: dve.tensor_scalar(out=mv[:, 4:5], in0=mv[:, 0:1], scalar1=mv[:, 3:4], scalar2=-1.0,
                                              op0=ALU.mult, op1=ALU.mult), reads=["mv01", "mv3"], writes=["mv4"])
        S.op("act", lambda k2=k2: act.activation(out=yb_[k2][:], in_=yb_[k2][:], func=AF.Identity, bias=mv[:, 4:5],
                                                 scale=mv[:, 3:4]), reads=ynames + ["mv3", "mv4"], writes=ynames)

    def out_stage2(tt):
        k2 = tt % 2
        ynames = [f"yb{k2}_0", f"yb{k2}_1"]
        S.op("dve", lambda k2=k2: dve.tensor_tensor(out=yb_[k2][:], in0=yb_[k2][:], in1=lng[:], op=ALU.mult),
             reads=ynames + ["lng"], writes=ynames)
        S.op("dve", lambda k2=k2: dve.tensor_tensor(out=yb_[k2][:], in0=yb_[k2][:], in1=lnb[:], op=ALU.add),
             reads=ynames + ["lnb"], writes=ynames)
        S.dma("sp", out_d[tt * 128:(tt + 1) * 128, :], yb_[k2][:], reads=ynames, writes=[f"out{tt}"])

    for tt in range(16):
        out_stage1(tt)
        if tt > 0:
            out_stage2(tt - 1)
    out_stage2(15)

    if DEBUG:
        S.barrier()
        S.dma("sp", dbg_m[:], merged[:].rearrange("p a b -> p (a b)"), writes=["dbg_m"])
    S.wait_all_dma("sp")
    S.barrier()
    st_c.close()
    es.close()
    return nc


_CACHE = {}


def _chunks(w, n):
    return np.ascontiguousarray(w.reshape(8, 128, n, 128).transpose(2, 1, 0, 3).reshape(n, 128, 1024))


def kernel(x, w_in, b_gate, w_proj_a, w_proj_b, w_out, da_lambda_q1, da_lambda_k1, da_lambda_q2, da_lambda_k2,
           da_subln_w, rel_bias, ln_g, ln_b):
    f = np.float32
    x = np.asarray(x, f)
    B = x.shape[0]
    if "nc" not in _CACHE:
        _CACHE["nc"] = build_nc()
        _CACHE["oh"] = _onehots()
    nc = _CACHE["nc"]
    oha, ohb = _CACHE["oh"]
    win_r = _chunks(np.asarray(w_in, f)[0], NCHUNK)
    pa_r = _chunks(np.asarray(w_proj_a, f)[0], 8)
    pb_r = _chunks(np.asarray(w_proj_b, f)[0], 8)
    wout = np.ascontiguousarray(np.asarray(w_out, f)[0])
    bg = np.ascontiguousarray(np.asarray(b_gate, f)[0].reshape(2, 8, 128).transpose(2, 0, 1).reshape(128, 16))
    lam = np.ascontiguousarray(np.concatenate([np.asarray(a, f).reshape(1, 64) for a in
                                               (da_lambda_q1, da_lambda_k1, da_lambda_q2, da_lambda_k2)], axis=0))
    subw = np.ascontiguousarray(np.asarray(da_subln_w, f).reshape(128, 1))
    relb = np.ascontiguousarray(np.asarray(rel_bias, f))
    lng = np.ascontiguousarray(np.asarray(ln_g, f).reshape(1, D))
    lnb = np.ascontiguousarray(np.asarray(ln_b, f).reshape(1, D))
    ident = np.eye(128, dtype=f)
    jmat = np.ascontiguousarray(np.eye(128, dtype=f)[::-1])
    shared = {"win": win_r, "pa": pa_r, "pb": pb_r, "wout": wout, "bg": bg, "lam": lam, "subw": subw, "relb": relb,
              "lng": lng, "lnb": lnb, "oha": oha, "ohb": ohb, "ident": ident, "jmat": jmat}
    in_maps = []
    for b in range(B):
        m = dict(shared)
        m["xT"] = np.ascontiguousarray(x[b].T)
        m["xtok"] = np.ascontiguousarray(x[b])
        in_maps.append(m)
    res = run_bass_kernel_spmd(nc, in_maps, core_ids=list(range(B)))
    _CACHE["last"] = res
    return np.stack([np.asarray(r["out"], f) for r in res.results], axis=0)
```

```python
import math
from contextlib import ExitStack

import numpy as np
import ml_dtypes

import concourse.bass as bass
import concourse.mybir as mybir
from concourse.bass_utils import run_bass_kernel_spmd

F32 = mybir.dt.float32
BF16 = mybir.dt.bfloat16
AF = mybir.ActivationFunctionType
ALU = mybir.AluOpType
AX = mybir.AxisListType

S_TOK = 2048
D = 1024
NCHUNK = 80
LAM_INIT = 0.2
ALPHA = 2.0 ** 0.25
SUBLN_EPS = 1e-5
LN_EPS = 1e-5
RA = 2560
TW = 2432
RB = 384
DILS = (1, 4, 16)
NEG = -30000.0

DEBUG = False


class _Buf:
    __slots__ = ("w", "r")

    def __init__(self):
        self.w = None
        self.r = {}


class Sched:
    def __init__(self, nc, n_dma=32):
        self.nc = nc
        self.engs = {"pe": nc.tensor, "act": nc.scalar, "dve": nc.vector, "pool": nc.gpsimd, "sp": nc.sync}
        self.sems = {}
        for e in ("pe", "act", "dve", "pool"):
            self.sems[e] = nc.alloc_semaphore(name=f"sem_{e}")
        self.n_dma = n_dma
        for i in range(n_dma):
            self.sems[("d", i)] = nc.alloc_semaphore(name=f"sem_dma{i}")
        self.dma_cnt = [0] * n_dma
        self.dma_rr = {"pool": 0, "sp": n_dma // 2}
        self.cnt = {e: 0 for e in ("pe", "act", "dve", "pool")}
        self.known = {e: {} for e in self.engs}
        self.bufs = {}
        self.n_waits = 0
        self.swdge_out = []
        self.swdge_limit = 6

    def _b(self, name):
        b = self.bufs.get(name)
        if b is None:
            b = self.bufs[name] = _Buf()
        return b

    def _deps(self, reads, writes):
        ev = {}

        def add(k, v):
            if ev.get(k, 0) < v:
                ev[k] = v

        for b in reads:
            if b.w is not None:
                add(*b.w)
        for b in writes:
            if b.w is not None:
                add(*b.w)
            for k, v in b.r.items():
                add(k, v)
        return ev

    def _wait(self, eng, ev):
        kn = self.known[eng]
        for k, v in ev.items():
            if eng == "pe" and k == "pe":
                continue
            if kn.get(k, 0) >= v:
                continue
            self.engs[eng].wait_ge(self.sems[k], v)
            kn[k] = v
            self.n_waits += 1

    def _commit(self, event, reads, writes):
        k, v = event
        for b in writes:
            b.w = event
            b.r = {}
        for b in reads:
            if b in writes:
                continue
            if b.r.get(k, 0) < v:
                b.r[k] = v

    def op(self, eng, fn, reads=(), writes=()):
        rb = [self._b(x) for x in reads]
        wb = [self._b(x) for x in writes]
        self._wait(eng, self._deps(rb, wb))
        ins = fn()
        self.cnt[eng] += 1
        ins.then_inc(self.sems[eng], 1)
        self._commit((eng, self.cnt[eng]), rb, wb)

    def dma(self, q, out, in_, reads=(), writes=()):
        rb = [self._b(x) for x in reads]
        wb = [self._b(x) for x in writes]
        self._wait(q, self._deps(rb, wb))
        if q == "pool" and len(self.swdge_out) >= self.swdge_limit:
            k0, v0 = self.swdge_out.pop(0)
            self._wait("pool", {k0: v0})
        half = self.n_dma // 2
        i = self.dma_rr[q]
        base = 0 if q == "pool" else half
        self.dma_rr[q] = base + (i - base + 1) % half
        ins = self.engs[q].dma_start(out=out, in_=in_)
        self.dma_cnt[i] += 16
        ins.then_inc(self.sems[("d", i)], 16)
        if q == "pool":
            self.swdge_out.append((("d", i), self.dma_cnt[i]))
        self._commit((("d", i), self.dma_cnt[i]), rb, wb)

    def barrier(self):
        ev = {e: self.cnt[e] for e in ("pe", "act", "dve", "pool") if self.cnt[e] > 0}
        for i in range(self.n_dma):
            if self.dma_cnt[i] > 0:
                ev[("d", i)] = self.dma_cnt[i]
        for e in self.engs:
            self._wait(e, ev)

    def wait_all_dma(self, eng):
        ev = {("d", i): self.dma_cnt[i] for i in range(self.n_dma) if self.dma_cnt[i] > 0}
        self._wait(eng, ev)


def _bucket(d):
    d = np.maximum(np.asarray(d), 0).astype(np.int32)
    distf = np.maximum(d, 1).astype(np.float32)
    large = 16 + (np.log(distf / np.float32(16)) / np.float32(math.log(2048 / 16)) * np.float32(16)).astype(np.int32)
    large = np.minimum(large, 31)
    return np.where(d < 16, d, large)


def _onehots():
    oha = np.zeros((33, RA), np.float32)
    r = np.arange(RA) - 511
    bk = _bucket(r)
    for i in range(RA):
        if r[i] < 0:
            oha[32, i] = 1.0
        else:
            oha[bk[i], i] = 1.0
    ohb = np.zeros((33, 3 * RB), np.float32)
    for di, d in enumerate(DILS):
        dl = np.arange(RB) - 127
        bk = _bucket(dl * d)
        for i in range(RB):
            if 0 <= dl[i] <= 128:
                ohb[bk[i], di * RB + i] = 1.0
            else:
                ohb[32, di * RB + i] = 1.0
    return oha, ohb


def build_nc():
    nc = bass.Bass("TRN2", target_bir_lowering=False)
    dt = nc.dram_tensor
    xT_d = dt("xT", [D, S_TOK], F32, kind="ExternalInput").ap()
    xtok_d = dt("xtok", [S_TOK, D], F32, kind="ExternalInput").ap()
    win_d = dt("win", [NCHUNK, 128, 1024], F32, kind="ExternalInput").ap()
    pa_d = dt("pa", [8, 128, 1024], F32, kind="ExternalInput").ap()
    pb_d = dt("pb", [8, 128, 1024], F32, kind="ExternalInput").ap()
    wout_d = dt("wout", [D, D], F32, kind="ExternalInput").ap()
    bg_d = dt("bg", [128, 16], F32, kind="ExternalInput").ap()
    lam_d = dt("lam", [4, 64], F32, kind="ExternalInput").ap()
    subw_d = dt("subw", [128, 1], F32, kind="ExternalInput").ap()
    relb_d = dt("relb", [32, 24], F32, kind="ExternalInput").ap()
    lng_d = dt("lng", [1, D], F32, kind="ExternalInput").ap()
    lnb_d = dt("lnb", [1, D], F32, kind="ExternalInput").ap()
    oha_d = dt("oha", [33, RA], F32, kind="ExternalInput").ap()
    ohb_d = dt("ohb", [33, 3 * RB], F32, kind="ExternalInput").ap()
    ident_d = dt("ident", [128, 128], F32, kind="ExternalInput").ap()
    jmat_d = dt("jmat", [128, 128], F32, kind="ExternalInput").ap()
    out_d = dt("out", [S_TOK, D], F32, kind="ExternalOutput").ap()
    sa_d = dt("sa_scr", [8, RA], BF16, kind="Internal").ap()
    sb_d = dt("sb_scr", [16, 3 * RB], BF16, kind="Internal").ap()
    if DEBUG:
        dbg_a = dt("dbg_a", [128, 8 * S_TOK], F32, kind="ExternalOutput").ap()
        dbg_b = dt("dbg_b", [128, 8 * S_TOK], F32, kind="ExternalOutput").ap()
        dbg_m = dt("dbg_m", [128, 8 * S_TOK], BF16, kind="ExternalOutput").ap()

    S = Sched(nc)
    pe, act, dve, pool = nc.tensor, nc.scalar, nc.vector, nc.gpsimd

    es = ExitStack()

    def sb(stack, name, shape, dtype):
        return stack.enter_context(nc.sbuf_tensor(name, shape, dtype))

    xT = sb(es, "xT_sb", [128, 8, S_TOK], BF16)
    yag = sb(es, "yag", [128, 8, S_TOK], BF16)
    ybg = sb(es, "ybg", [128, 8, S_TOK], BF16)
    ident = sb(es, "ident_sb", [128, 128], BF16)
    jmat = sb(es, "jmat_sb", [128, 128], BF16)
    ones_bf = sb(es, "ones_bf", [128, 128], BF16)
    ones_f = sb(es, "ones_f", [128, 128], F32)
    neglam = sb(es, "neglam", [128, 1], F32)
    subw = sb(es, "subw_sb", [128, 1], F32)
    bg = sb(es, "bg_sb", [128, 16], F32)
    smallf = sb(es, "smallf", [128, 16], F32)
    ps = es.enter_context(nc.psum_tensor("ps", [128, 4096], F32))
    psb = ps[:].bitcast(BF16)

    def bank(i, rows=slice(0, 128), c0=0, c1=512):
        return ps[rows, i * 512 + c0:i * 512 + c1]

    def bankbf(i):
        return psb[:, i * 1024:(i + 1) * 1024]

    def pn(i):
        return f"ps{i}"

    act_state = {"f": None}
    warm = sb(es, "act_warm", [128, 4], F32)
    S.op("dve", lambda: dve.memset(warm[:], 1.0), writes=["warm_in"])

    def act_fn(func):
        if act_state["f"] is not func:
            act_state["f"] = func
            S.op("act", lambda: act.activation(out=warm[:, 2:3], in_=warm[:, 0:1], func=func), reads=["warm_in"],
                 writes=["warm_out"])
        return func

    for kc in range(8):
        S.dma("pool", xT[:, kc, :], xT_d[kc * 128:(kc + 1) * 128, :], writes=[f"xT{kc}"])
    XT_NAMES = [f"xT{kc}" for kc in range(8)]
    S.dma("pool", ident[:], ident_d[:], writes=["ident"])
    S.dma("pool", jmat[:], jmat_d[:], writes=["jmat"])
    S.dma("sp", subw[:], subw_d[:], writes=["subw"])
    S.dma("sp", bg[:], bg_d[:], writes=["bg"])
    S.op("dve", lambda: dve.memset(ones_bf[:], 1.0), writes=["ones_bf"])
    S.op("dve", lambda: dve.memset(ones_f[:], 1.0), writes=["ones_f"])

    st_a = ExitStack()
    NW = 8
    wch = [sb(st_a, f"wch{i}", [128, 1024], BF16) for i in range(NW)]
    wstate = {"rr": 0}

    def load_chunk(src_ap):
        i = wstate["rr"]
        wstate["rr"] = (i + 1) % NW
        S.dma("pool", wch[i][:], src_ap, writes=[f"wch{i}"])
        return i

    qT = sb(st_a, "qT", [128, S_TOK], BF16)
    kT = sb(st_a, "kT", [128, S_TOK], BF16)
    aT = sb(st_a, "aT", [128, S_TOK], BF16)
    zT = sb(st_a, "zT", [128, S_TOK], BF16)
    NP = 12
    Pb = [sb(st_a, f"P{i}", [128, 512], BF16) for i in range(NP)]
    pstate = {"rr": 0}

    def next_p():
        i = pstate["rr"]
        pstate["rr"] = (i + 1) % NP
        return i

    tf = [sb(st_a, f"tf{i}", [128, 512], F32) for i in range(5)]
    sqb = sb(st_a, "sqb", [128, 512], BF16)

    st_s = ExitStack()
    tabx = sb(st_s, "tabx", [33, 24], F32)
    oha = sb(st_s, "oha_sb", [33, RA], F32)
    ohb = sb(st_s, "ohb_sb", [33, 3 * RB], F32)
    ega = sb(st_s, "ega", [24, RA], BF16)
    egb = sb(st_s, "egb", [24, 3 * RB], BF16)
    lamv = sb(st_s, "lamv", [128, 4, 64], F32)
    lamt = sb(st_s, "lamt", [128, 2, 64], F32)
    S.op("dve", lambda: dve.memset(tabx[32:33, :], NEG), writes=["tabx_m"])
    S.dma("sp", tabx[0:32, :], relb_d[:], writes=["tabx"])
    S.dma("sp", oha[:], oha_d[:], writes=["oha"])
    S.dma("sp", ohb[:], ohb_d[:], writes=["ohb"])
    S.dma("sp", lamv[:].rearrange("p a b -> p (a b)"),
          bass.AP(tensor=lam_d.tensor, offset=0, ap=[[0, 128], [1, 256]]), writes=["lamv"])
    ring = {"rr": 0}

    def rbank():
        i = ring["rr"]
        ring["rr"] = (i + 1) % 4
        return 4 + i

    def setup_g(oh, ohname, eg, egname, width):
        c = 0
        while c < width:
            n = min(512, width - c)
            bi = rbank()
            S.op("pe", lambda bi=bi, c=c, n=n: pe.matmul(bank(bi, slice(0, 24), 0, n), lhsT=tabx[0:33, :],
                                                         rhs=oh[0:33, c:c + n], start=True, stop=True),
                 reads=["tabx", "tabx_m", ohname], writes=[pn(bi)])
            S.op("act", lambda bi=bi, c=c, n=n: act.activation(out=eg[0:24, c:c + n], in_=bank(bi, slice(0, 24), 0, n),
                                                               func=AF.Exp),
                 reads=[pn(bi)], writes=[egname])
            c += n

    setup_g(oha, "oha", ega, "ega", RA)
    setup_g(ohb, "ohb", egb, "egb", 3 * RB)
    S.dma("sp", sa_d[:], ega[0:8, :], reads=["ega"], writes=["SA"])
    S.dma("sp", sb_d[:], egb[8:24, :], reads=["egb"], writes=["SB"])

    S.op("dve", lambda: dve.tensor_tensor(out=lamt[:, 0, :], in0=lamv[:, 0, :], in1=lamv[:, 1, :], op=ALU.mult),
         reads=["lamv"], writes=["lamt0"])
    S.op("dve", lambda: dve.tensor_tensor(out=lamt[:, 1, :], in0=lamv[:, 2, :], in1=lamv[:, 3, :], op=ALU.mult),
         reads=["lamv"], writes=["lamt1"])
    S.op("dve", lambda: dve.reduce_sum(out=smallf[:, 0:1], in_=lamt[:, 0, :], axis=AX.X), reads=["lamt0"], writes=["sm0"])
    S.op("dve", lambda: dve.reduce_sum(out=smallf[:, 1:2], in_=lamt[:, 1, :], axis=AX.X), reads=["lamt1"], writes=["sm1"])
    S.op("act", lambda: act.activation(out=smallf[:, 2:4], in_=smallf[:, 0:2], func=AF.Exp), reads=["sm0", "sm1"],
         writes=["sm23"])
    S.op("dve", lambda: dve.tensor_tensor(out=smallf[:, 4:5], in0=smallf[:, 3:4], in1=smallf[:, 2:3], op=ALU.subtract),
         reads=["sm23"], writes=["sm4"])
    S.op("dve", lambda: dve.tensor_scalar(out=neglam[:], in0=smallf[:, 4:5], scalar1=-LAM_INIT, scalar2=None, op0=ALU.add),
         reads=["sm4"], writes=["neglam"])
    S.op("dve", lambda: dve.tensor_scalar(out=smallf[:, 5:6], in0=subw[:], scalar1=1.0 - LAM_INIT, scalar2=None,
                                          op0=ALU.mult), reads=["subw"], writes=["subw_s"])
    S.op("dve", lambda: dve.memset(smallf[:, 6:7], SUBLN_EPS), writes=["eps_a"])
    S.op("dve", lambda: dve.memset(smallf[:, 7:8], LN_EPS), writes=["eps_l"])
    subw_s = smallf[:, 5:6]
    eps_a = smallf[:, 6:7]
    eps_l = smallf[:, 7:8]

    def proj(wi, dst, dname, kind):
        w = wch[wi]
        for tb in range(4):
            bi = rbank()

            def mm(bi=bi, tb=tb):
                ins = None
                for kc in range(8):
                    ins = pe.matmul(bank(bi), lhsT=w[:, kc * 128:(kc + 1) * 128], rhs=xT[:, kc, tb * 512:(tb + 1) * 512],
                                    start=(kc == 0), stop=(kc == 7))
                return ins

            S.op("pe", mm, reads=[f"wch{wi}"] + XT_NAMES, writes=[pn(bi)])
            o = dst[:, tb * 512:(tb + 1) * 512]
            if kind == "silu":
                S.op("act", lambda bi=bi, o=o: act.activation(out=o, in_=bank(bi), func=AF.Silu), reads=[pn(bi)],
                     writes=[f"{dname}{tb}"])
            elif kind == "act":
                S.op("act", lambda bi=bi, o=o: act.copy(out=o, in_=bank(bi)), reads=[pn(bi)], writes=[f"{dname}{tb}"])
            else:
                S.op("dve", lambda bi=bi, o=o: dve.tensor_copy(out=o, in_=bank(bi)), reads=[pn(bi)],
                     writes=[f"{dname}{tb}"])

    def names4(n):
        return [f"{n}{tb}" for tb in range(4)]

    st_pa = ExitStack()
    vtok = sb(st_pa, "vtok", [128, 16, 128], BF16)
    XT = sb(st_pa, "XTh", [128, TW], BF16)
    Ttab = sb(st_pa, "Ttab", [128, TW], BF16)

    def phaseA_prep(h):
        wq = load_chunk(win_d[h])
        wk = load_chunk(win_d[8 + h])
        wv = load_chunk(win_d[16 + h])
        wz = load_chunk(win_d[24 + h])
        S.dma("sp", XT[:], bass.AP(tensor=sa_d.tensor, offset=h * RA, ap=[[1, 128], [1, TW]]), reads=["SA"],
              writes=["XT"])
        return wq, wk, wv, wz

    def phaseA_proj(ws, hook=None):
        wq, wk, wv, wz = ws
        proj(wq, qT, "qT", "dve")
        proj(wk, kT, "kT", "dve")
        if hook is not None:
            hook()
        proj(wv, aT, "aT", "act")
        proj(wz, zT, "zT", "silu")
        for g in range(2):
            bi = rbank()

            def tr(bi=bi, g=g):
                ins = None
                for t8 in range(8):
                    ti = 8 * g + t8
                    ins = pe.transpose(bankbf(bi)[:, t8 * 128:(t8 + 1) * 128], aT[:, ti * 128:(ti + 1) * 128], ident[:])
                return ins

            S.op("pe", tr, reads=names4("aT") + ["ident"], writes=[pn(bi)])
            S.op("act", lambda bi=bi, g=g: act.copy(out=vtok[:, 8 * g:8 * g + 8, :].rearrange("p a b -> p (a b)"),
                                                   in_=bankbf(bi)), reads=[pn(bi)], writes=[f"vtok{g}"])
        c = 0
        while c < TW:
            n = min(512, TW - c)
            bi = rbank()
            S.op("pe", lambda bi=bi, c=c, n=n: pe.matmul(bank(bi, slice(0, 128), 0, n), lhsT=jmat[:], rhs=XT[:, c:c + n],
                                                         start=True, stop=True), reads=["jmat", "XT"], writes=[pn(bi)])
            S.op("dve", lambda bi=bi, c=c, n=n: dve.tensor_copy(out=Ttab[:, c:c + n], in_=bank(bi, slice(0, 128), 0, n)),
                 reads=[pn(bi)], writes=["Ttab"])
            c += n

    OB_, DB_ = (0, 1), (2, 3)
    mmc = {"n": 0}

    def phaseA_attn(h):
        for qb in range(4):
            nkt = 4 * qb + 4
            pending = []

            def flush(keep):
                while len(pending) > keep:
                    pending.pop(0)()

            for kt in range(nkt):
                j = kt - 4 * qb
                c0 = 128 * j if j > 0 else 0
                off = 128 * (4 * qb - kt) + 384
                for m in range(2):
                    rows = slice(64 * m, 64 * m + 64)
                    sbk = rbank()
                    S.op("pe", lambda sbk=sbk, rows=rows, kt=kt, c0=c0, qb=qb: pe.matmul(
                        bank(sbk, slice(0, 128), c0, 512), lhsT=kT[rows, kt * 128:(kt + 1) * 128],
                        rhs=qT[rows, qb * 512 + c0:(qb + 1) * 512], start=True, stop=True),
                        reads=names4("kT") + names4("qT"), writes=[pn(sbk)])
                    pi = next_p()
                    S.op("act", lambda sbk=sbk, pi=pi, c0=c0: act.activation(out=Pb[pi][:, c0:512],
                                                                             in_=bank(sbk, slice(0, 128), c0, 512),
                                                                             func=AF.Exp, scale=0.125),
                         reads=[pn(sbk)], writes=[f"P{pi}"])
                    mmc["n"] += 1
                    if mmc["n"] % 3 != 0:
                        S.op("dve", lambda pi=pi, c0=c0, off=off: dve.tensor_tensor(
                            out=Pb[pi][:, c0:512], in0=Pb[pi][:, c0:512], in1=Ttab[:, off + c0:off + 512], op=ALU.mult),
                            reads=[f"P{pi}", "Ttab"], writes=[f"P{pi}"])
                    else:
                        S.op("pool", lambda pi=pi, c0=c0, off=off: pool.tensor_tensor(
                            out=Pb[pi][:, c0:512], in0=Pb[pi][:, c0:512], in1=Ttab[:, off + c0:off + 512], op=ALU.mult),
                            reads=[f"P{pi}", "Ttab"], writes=[f"P{pi}"])

                    def av(m=m, pi=pi, c0=c0, kt=kt, nkt=nkt):
                        def f():
                            pe.matmul(bank(OB_[m], slice(0, 128), c0, 512), lhsT=vtok[:, kt, :], rhs=Pb[pi][:, c0:512],
                                      start=(kt == 0), stop=(kt == nkt - 1))
                            return pe.matmul(bank(DB_[m], slice(0, 128), c0, 512), lhsT=ones_bf[:], rhs=Pb[pi][:, c0:512],
                                             start=(kt == 0), stop=(kt == nkt - 1))

                        S.op("pe", f, reads=[f"P{pi}", "vtok0", "vtok1", "ones_bf"], writes=[pn(OB_[m]), pn(DB_[m])])

                    pending.append(av)
                flush(6)
                if kt == 1:
                    run_pending_finA()
            flush(0)
            fin_part1()
            finA["p2"] = (lambda h=h, qb=qb: fin_part2(h, qb))

    finA = {"p2": None}

    def run_pending_finA():
        f = finA["p2"]
        if f is not None:
            finA["p2"] = None
            f()

    def fin_part1():
        r0, r1, o0, o1, tt = tf
        S.op("act", lambda: act.activation(out=r0[:], in_=bank(DB_[0]), func=AF.Ln), reads=[pn(DB_[0])], writes=["tf0"])
        S.op("act", lambda: act.activation(out=r1[:], in_=bank(DB_[1]), func=AF.Ln), reads=[pn(DB_[1])], writes=["tf1"])
        S.op("act", lambda: act.activation(out=r0[:], in_=r0[:], func=AF.Exp, scale=-1.0), reads=["tf0"], writes=["tf0"])
        S.op("act", lambda: act.activation(out=r1[:], in_=r1[:], func=AF.Exp, scale=-1.0), reads=["tf1"], writes=["tf1"])
        S.op("dve", lambda: dve.tensor_tensor(out=o0[:], in0=bank(OB_[0]), in1=r0[:], op=ALU.mult),
             reads=[pn(OB_[0]), "tf0"], writes=["tf2"])
        S.op("dve", lambda: dve.tensor_tensor(out=o1[:], in0=bank(OB_[1]), in1=r1[:], op=ALU.mult),
             reads=[pn(OB_[1]), "tf1"], writes=["tf3"])
        S.op("dve", lambda: dve.scalar_tensor_tensor(out=o0[:], in0=o1[:], scalar=neglam[:, 0:1], in1=o0[:],
                                                     op0=ALU.mult, op1=ALU.add),
             reads=["tf2", "tf3", "neglam"], writes=["tf2"])
        S.op("pool", lambda: pool.tensor_tensor(out=sqb[:], in0=o0[:], in1=o0[:], op=ALU.mult), reads=["tf2"],
             writes=["sqb"])

    def fin_part2(h, qb):
        r0, r1, o0, o1, tt = tf
        bi = rbank()
        S.op("pe", lambda bi=bi: pe.matmul(bank(bi), lhsT=ones_bf[:], rhs=sqb[:], start=True, stop=True),
             reads=["sqb", "ones_bf"], writes=[pn(bi)])
        S.op("act", lambda bi=bi: act.activation(out=r0[:], in_=bank(bi), func=AF.Ln, bias=eps_a, scale=1.0 / 128.0),
             reads=[pn(bi), "eps_a", "tf0"], writes=["tf0"])
        S.op("act", lambda: act.activation(out=r1[:], in_=r0[:], func=AF.Exp, scale=-0.5), reads=["tf0", "tf1"],
             writes=["tf1"])
        S.op("dve", lambda: dve.tensor_tensor(out=tt[:], in0=o0[:], in1=r1[:], op=ALU.mult), reads=["tf2", "tf1"],
             writes=["tf4"])
        S.op("dve", lambda h=h, qb=qb: dve.scalar_tensor_tensor(
            out=yag[:, h, qb * 512:(qb + 1) * 512], in0=tt[:], scalar=subw_s, in1=zT[:, qb * 512:(qb + 1) * 512],
            op0=ALU.mult, op1=ALU.mult), reads=["tf4", "subw_s", f"zT{qb}"], writes=[f"yag{h}_{qb}"])

    ws = phaseA_prep(0)
    for h in range(8):
        phaseA_proj(ws, run_pending_finA)
        if h + 1 < 8:
            ws = phaseA_prep(h + 1)
        phaseA_attn(h)
    run_pending_finA()

    S.barrier()
    st_pa.close()
    st_s.close()

    st_pb = ExitStack()
    q4 = sb(st_pb, "q4", [128, S_TOK], BF16)
    k4 = sb(st_pb, "k4", [128, S_TOK], BF16)
    q16 = sb(st_pb, "q16", [128, S_TOK], BF16)
    k16 = sb(st_pb, "k16", [128, S_TOK], BF16)
    Vd = sb(st_pb, "Vd", [128, 3, 16, 193], BF16)
    XB = sb(st_pb, "XB", [128, 6 * 256], BF16)
    Btab = sb(st_pb, "Btab", [128, 6, 256], BF16)
    accs = sb(st_pb, "accs", [128, S_TOK], F32)
    zr = [sb(st_pb, f"zr{i}", [128, 512], F32) for i in range(2)]

    S.op("pool", lambda: pool.memset(Vd[:].rearrange("p a b c -> p (a b c)"), 0.0), writes=["Vd"])
    S.op("pool", lambda: pool.memset(Vd[:, :, :, 64:66], 1.0), reads=[], writes=["Vd"])

    def phaseB_prep(j):
        wq = load_chunk(win_d[32 + j])
        wk = load_chunk(win_d[40 + j])
        wv = load_chunk(win_d[48 + j])
        wz = load_chunk(win_d[56 + j])
        S.dma("sp", XB[:].rearrange("p (a b c) -> p a b c", a=2, b=3),
              bass.AP(tensor=sb_d.tensor, offset=2 * j * 3 * RB, ap=[[1, 128], [3 * RB, 2], [RB, 3], [1, 256]]),
              reads=["SB"], writes=["XB"])
        return wq, wk, wv, wz

    qperm = {1: qT, 4: q4, 16: q16}
    kperm = {1: kT, 4: k4, 16: k16}
    sring = {"rr": 0}

    def phaseB_proj(ws, hook=None):
        wq, wk, wv, wz = ws
        proj(wq, qT, "qT", "dve")
        proj(wk, kT, "kT", "dve")
        if hook is not None:
            hook()
        proj(wv, aT, "aT", "act")
        proj(wz, zT, "zT", "silu")
        for d, qd, kd in ((4, q4, k4), (16, q16, k16)):
            if d == 4:
                S.op("dve", lambda d=d, qd=qd: dve.tensor_copy(out=qd[:].rearrange("p (r l) -> p r l", r=d),
                                                              in_=qT[:].rearrange("p (l r) -> p r l", r=d)),
                     reads=names4("qT"), writes=[f"q{d}"])
                S.op("dve", lambda d=d, kd=kd: dve.tensor_copy(out=kd[:].rearrange("p (r l) -> p r l", r=d),
                                                              in_=kT[:].rearrange("p (l r) -> p r l", r=d)),
                     reads=names4("kT"), writes=[f"k{d}"])
            else:
                S.op("act", lambda d=d, qd=qd: act.copy(out=qd[:].rearrange("p (r l) -> p r l", r=d),
                                                       in_=qT[:].rearrange("p (l r) -> p r l", r=d)),
                     reads=names4("qT"), writes=[f"q{d}"])
                S.op("act", lambda d=d, kd=kd: act.copy(out=kd[:].rearrange("p (r l) -> p r l", r=d),
                                                       in_=kT[:].rearrange("p (l r) -> p r l", r=d)),
                     reads=names4("kT"), writes=[f"k{d}"])
        for di, d in enumerate(DILS):
            av = aT[:].rearrange("p (l r) -> p r l", r=d)
            nb = (S_TOK // d) // 128
            for g in range(2):
                bi = rbank()

                def tr(bi=bi, g=g, av=av, nb=nb):
                    ins = None
                    for t8 in range(8):
                        ti = 8 * g + t8
                        r, n = ti // nb, ti % nb
                        ins = pe.transpose(bankbf(bi)[:, t8 * 128:(t8 + 1) * 128], av[:, r, n * 128:(n + 1) * 128],
                                           ident[:])
                    return ins

                S.op("pe", tr, reads=names4("aT") + ["ident"], writes=[pn(bi)])
                src = bankbf(bi).rearrange("p (a b) -> p a b", a=8)
                S.op("act", lambda bi=bi, g=g, di=di, src=src: act.copy(out=Vd[:, di, 8 * g:8 * g + 8, 0:64],
                                                                       in_=src[:, :, 0:64]),
                     reads=[pn(bi)], writes=["Vd"])
                S.op("dve", lambda bi=bi, g=g, di=di, src=src: dve.tensor_copy(out=Vd[:, di, 8 * g:8 * g + 8, 129:193],
                                                                              in_=src[:, :, 64:128]),
                     reads=[pn(bi)], writes=["Vd"])
        for c in range(3):
            bi = rbank()
            S.op("pe", lambda bi=bi, c=c: pe.matmul(bank(bi), lhsT=jmat[:], rhs=XB[:, c * 512:(c + 1) * 512], start=True,
                                                    stop=True), reads=["jmat", "XB"], writes=[pn(bi)])
            S.op("dve", lambda bi=bi, c=c: dve.tensor_copy(out=Btab[:, 2 * c:2 * c + 2, :].rearrange("p a b -> p (a b)"),
                                                          in_=bank(bi)), reads=[pn(bi)], writes=["Btab"])

    def phaseB_attn(j):
        for hh in range(2):
            rows = slice(64 * hh, 64 * hh + 64)
            started = set()
            tiles = []
            for di, d in enumerate(DILS):
                nb = (S_TOK // d) // 128
                for r in range(d):
                    for n in range(nb):
                        tiles.append((di, d, nb, r, n))
            pending = []

            def flush(keep):
                while len(pending) > keep:
                    pending.pop(0)()

            for gi in range(0, len(tiles), 2):
                grp = tiles[gi:gi + 2]
                di, d = grp[0][0], grp[0][1]
                sbk = rbank()

                def qk(grp=grp, sbk=sbk, rows=rows):
                    ins = None
                    for s, (di_, d_, nb, r, n) in enumerate(grp):
                        nq = 256 if n + 1 < nb else 128
                        col = (r * nb + n) * 128
                        ins = pe.matmul(bank(sbk, slice(0, 128), s * 256, s * 256 + nq),
                                        lhsT=kperm[d_][rows, col:col + 128], rhs=qperm[d_][rows, col:col + nq],
                                        start=True, stop=True)
                    return ins

                qn = names4("qT") if d == 1 else [f"q{d}"]
                kn = names4("kT") if d == 1 else [f"k{d}"]
                S.op("pe", qk, reads=qn + kn, writes=[pn(sbk)])
                pi = next_p()
                S.op("act", lambda pi=pi, sbk=sbk: act.activation(out=Pb[pi][:], in_=bank(sbk), func=AF.Exp, scale=0.125),
                     reads=[pn(sbk)], writes=[f"P{pi}"])
                bsl = Btab[:, hh * 3 + di, :]
                bap = bass.AP(tensor=bsl.tensor, offset=bsl.offset, ap=[list(bsl.ap[0]), [0, 2], [1, 256]])
                pv = Pb[pi][:].rearrange("p (a b) -> p a b", a=2)
                if (gi // 2) % 3 != 2:
                    S.op("dve", lambda pv=pv, bap=bap: dve.tensor_tensor(out=pv, in0=pv, in1=bap, op=ALU.mult),
                         reads=[f"P{pi}", "Btab"], writes=[f"P{pi}"])
                else:
                    S.op("pool", lambda pv=pv, bap=bap: pool.tensor_tensor(out=pv, in0=pv, in1=bap, op=ALU.mult),
                         reads=[f"P{pi}", "Btab"], writes=[f"P{pi}"])

                def av(grp=grp, pi=pi, hh=hh):
                    wr = set()
                    plan = []
                    mrows = slice(0, 65) if hh == 0 else slice(0, 128)
                    for s, (di_, d_, nb, r, n) in enumerate(grp):
                        ti = r * nb + n
                        lhs = Vd[:, di_, ti, 0:65] if hh == 0 else Vd[:, di_, ti, 65:193]
                        for half in range(2):
                            nn = n + half
                            if nn >= nb:
                                continue
                            pc0 = s * 256 + half * 128
                            if d_ == 1:
                                b_ = nn // 4
                                plan.append((b_, bank(b_, mrows, (nn % 4) * 128, (nn % 4) * 128 + 128), lhs,
                                             Pb[pi][:, pc0:pc0 + 128]))
                            elif d_ == 4:
                                b_ = nn
                                o = bank(b_, mrows).rearrange("p (j d) -> p d j", d=4)[:, r, :]
                                plan.append((b_, o, lhs, Pb[pi][:, pc0:pc0 + 128]))
                            else:
                                for b_ in range(4):
                                    o = bank(b_, mrows).rearrange("p (j d) -> p d j", d=16)[:, r, :]
                                    plan.append((b_, o, lhs, Pb[pi][:, pc0 + 32 * b_:pc0 + 32 * b_ + 32]))
                    for b_, *_ in plan:
                        wr.add(b_)

                    def f():
                        ins = None
                        for b_, o, lhs, rhs in plan:
                            st = b_ not in started
                            started.add(b_)
                            ins = pe.matmul(o, lhsT=lhs, rhs=rhs, start=st, stop=False, skip_group_check=True)
                        return ins

                    S.op("pe", f, reads=[f"P{pi}", "Vd"], writes=[pn(b_) for b_ in sorted(wr)])

                pending.append(av)
                flush(4)
                if gi == 4:
                    run_pending_finB()
            flush(0)
            finB_part1(hh)
            finB["p2"] = (lambda j=j, hh=hh: finB_part2(j, hh))

    finB = {"p2": None}

    def run_pending_finB():
        f = finB["p2"]
        if f is not None:
            finB["p2"] = None
            f()

    def finB_part1(hh):
        mrows = slice(0, 65) if hh == 0 else slice(0, 128)
        for pc in range(4):
            csl = slice(pc * 512, (pc + 1) * 512)
            if pc % 2 == 0:
                S.op("act", lambda pc=pc, csl=csl, mrows=mrows: act.copy(out=accs[mrows, csl], in_=bank(pc, mrows)),
                     reads=[pn(pc)], writes=[f"accs{pc}"])
            else:
                S.op("dve", lambda pc=pc, csl=csl, mrows=mrows: dve.tensor_copy(out=accs[mrows, csl], in_=bank(pc, mrows)),
                     reads=[pn(pc)], writes=[f"accs{pc}"])
        row = 64 if hh == 0 else 0
        rsl = slice(row, row + 1)
        an = [f"accs{pc}" for pc in range(4)]
        S.op("act", lambda rsl=rsl: act.activation(out=accs[rsl, :], in_=accs[rsl, :], func=AF.Ln), reads=an, writes=an)
        S.op("act", lambda rsl=rsl: act.activation(out=accs[rsl, :], in_=accs[rsl, :], func=AF.Exp, scale=-1.0), reads=an,
             writes=an)

    def finB_part2(j, hh):
        rows = slice(64 * hh, 64 * hh + 64)
        row = 64 if hh == 0 else 0
        rsl = slice(row, row + 1)
        mm_rows = slice(0, 64) if hh == 0 else slice(0, 128)
        mcols = 64 if hh == 0 else 128
        for pc in range(4):
            csl = slice(pc * 512, (pc + 1) * 512)
            bi = rbank()
            S.op("pe", lambda bi=bi, csl=csl: pe.matmul(bank(bi, mm_rows), lhsT=ones_f[rsl, 0:mcols], rhs=accs[rsl, csl],
                                                        start=True, stop=True),
                 reads=[f"accs{pc}", "ones_f"], writes=[pn(bi)])
            k2 = pc % 2
            S.op("dve", lambda bi=bi, k2=k2, csl=csl: dve.tensor_tensor(out=zr[k2][rows, :], in0=bank(bi, rows),
                                                                       in1=accs[rows, csl], op=ALU.mult),
                 reads=[pn(bi), f"accs{pc}"], writes=[f"zr{k2}"])
            S.op("pool", lambda k2=k2, csl=csl: pool.tensor_tensor(out=ybg[rows, j, csl], in0=zr[k2][rows, :],
                                                                  in1=zT[rows, csl], op=ALU.mult),
                 reads=[f"zr{k2}", f"zT{pc}"], writes=[f"ybg{j}_{pc}_{hh}"])

    ws = phaseB_prep(0)
    for j in range(8):
        phaseB_proj(ws, run_pending_finB)
        if j + 1 < 8:
            ws = phaseB_prep(j + 1)
        phaseB_attn(j)
    run_pending_finB()

    S.barrier()
    st_pb.close()
    st_a.close()

    if DEBUG:
        st_d = ExitStack()
        dtmp = sb(st_d, "dtmp", [128, 8 * S_TOK], F32)
        S.op("dve", lambda: dve.tensor_copy(out=dtmp[:], in_=yag[:].rearrange("p a b -> p (a b)")), writes=["dtmp"])
        S.dma("sp", dbg_a[:], dtmp[:], reads=["dtmp"], writes=["dbg_a"])
        S.op("dve", lambda: dve.tensor_copy(out=dtmp[:], in_=ybg[:].rearrange("p a b -> p (a b)")), reads=["dbg_a"],
             writes=["dtmp"])
        S.dma("sp", dbg_b[:], dtmp[:], reads=["dtmp"], writes=["dbg_b"])
        S.barrier()
        st_d.close()

    st_c = ExitStack()
    merged = sb(st_c, "merged", [128, 8, S_TOK], BF16)
    wout = sb(st_c, "wout_sb", [128, 8, D], BF16)
    NWC = 8
    wc = [sb(st_c, f"wc{i}", [128, 1024], BF16) for i in range(NWC)]
    sg = [sb(st_c, f"sg{i}", [128, 512], F32) for i in range(4)]
    t12 = [sb(st_c, f"t12_{i}", [128, 512], F32) for i in range(4)]
    lng = sb(st_c, "lng_sb", [128, D], F32)
    lnb = sb(st_c, "lnb_sb", [128, D], F32)
    xtk = [sb(st_c, f"xtk{i}", [128, D], F32) for i in range(2)]
    yb_ = [sb(st_c, f"ybuf{i}", [128, D], F32) for i in range(2)]
    stats = sb(st_c, "stats", [128, 2, 6], F32)
    mv = sb(st_c, "mv", [128, 8], F32)

    S.dma("sp", lng[:], bass.AP(tensor=lng_d.tensor, offset=0, ap=[[0, 128], [1, D]]), writes=["lng"])
    S.dma("sp", lnb[:], bass.AP(tensor=lnb_d.tensor, offset=0, ap=[[0, 128], [1, D]]), writes=["lnb"])
    wcs = {"rr": 0}

    def load_c(src_ap):
        i = wcs["rr"]
        wcs["rr"] = (i + 1) % NWC
        S.dma("pool", wc[i][:], src_ap, writes=[f"wc{i}"])
        return i

    def prepC(dc):
        return (load_c(pa_d[dc]), load_c(pb_d[dc]), load_c(win_d[64 + dc]), load_c(win_d[72 + dc]))

    pring = {"rr": 0}

    def nbank():
        i = pring["rr"]
        pring["rr"] = (i + 1) % 8
        return i

    cw = prepC(0)
    it = 0
    for dc in range(8):
        wpa, wpb, wga, wgb = cw
        if dc + 1 < 8:
            cw = prepC(dc + 1)
        S.dma("pool", wout[:, dc, :], wout_d[dc * 128:(dc + 1) * 128, :], writes=[f"wout{dc}"])
        for tb in range(4):
            tsl = slice(tb * 512, (tb + 1) * 512)
            k2 = it % 2
            it += 1
            banks = {}
            for nm, wi, src, snames in (("ua", wpa, yag, [f"yag{kc}_{tb}" for kc in range(8)]),
                                        ("ga", wga, xT, XT_NAMES),
                                        ("ub", wpb, ybg, [f"ybg{kc}_{tb}_{hh}" for kc in range(8) for hh in range(2)]),
                                        ("gb", wgb, xT, XT_NAMES)):
                bi = nbank()
                banks[nm] = bi

                def mm(bi=bi, wi=wi, src=src, tsl=tsl):
                    ins = None
                    for kc in range(8):
                        ins = pe.matmul(bank(bi), lhsT=wc[wi][:, kc * 128:(kc + 1) * 128], rhs=src[:, kc, tsl],
                                        start=(kc == 0), stop=(kc == 7))
                    return ins

                S.op("pe", mm, reads=[f"wc{wi}"] + snames, writes=[pn(bi)])
            for gi, nm in enumerate(("ga", "gb")):
                bi = banks[nm]
                sgi = 2 * k2 + gi
                S.op("act", lambda bi=bi, sgi=sgi, gi=gi, dc=dc: act.activation(
                    out=sg[sgi][:], in_=bank(bi), func=AF.Sigmoid, bias=bg[:, gi * 8 + dc:gi * 8 + dc + 1]),
                    reads=[pn(bi), "bg"], writes=[f"sg{sgi}"])
            for gi, nm in enumerate(("ua", "ub")):
                bi = banks[nm]
                sgi = 2 * k2 + gi
                S.op("dve", lambda bi=bi, sgi=sgi: dve.tensor_tensor(out=t12[sgi][:], in0=bank(bi), in1=sg[sgi][:],
                                                                    op=ALU.mult),
                     reads=[pn(bi), f"sg{sgi}"], writes=[f"t12_{sgi}"])
            S.op("pool", lambda k2=k2, dc=dc, tsl=tsl: pool.tensor_tensor(out=merged[:, dc, tsl], in0=t12[2 * k2][:],
                                                                         in1=t12[2 * k2 + 1][:], op=ALU.add),
                 reads=[f"t12_{2 * k2}", f"t12_{2 * k2 + 1}"], writes=[f"mg{dc}_{tb}"])

    def out_stage1(tt):
        k2 = tt % 2
        tb = tt // 4
        S.dma("sp", xtk[k2][:], xtok_d[tt * 128:(tt + 1) * 128, :], writes=[f"xtk{k2}"])
        b0 = nbank()
        b1 = nbank()
        for half, bi in enumerate((b0, b1)):
            def mm(bi=bi, half=half, tt=tt):
                ins = None
                for kc in range(8):
                    ins = pe.matmul(bank(bi), lhsT=merged[:, kc, tt * 128:(tt + 1) * 128],
                                    rhs=wout[:, kc, half * 512:(half + 1) * 512], start=(kc == 0), stop=(kc == 7))
                return ins

            S.op("pe", mm, reads=[f"mg{kc}_{tb}" for kc in range(8)] + [f"wout{kc}" for kc in range(8)], writes=[pn(bi)])
        for half, bi in enumerate((b0, b1)):
            hs = slice(half * 512, (half + 1) * 512)
            S.op("dve", lambda bi=bi, hs=hs, k2=k2: dve.scalar_tensor_tensor(
                out=yb_[k2][:, hs], in0=xtk[k2][:, hs], scalar=ALPHA, in1=bank(bi), op0=ALU.mult, op1=ALU.add),
                reads=[pn(bi), f"xtk{k2}"], writes=[f"yb{k2}_{half}"])
            S.op("dve", lambda hs=hs, k2=k2, half=half: dve.bn_stats(out=stats[:, half, :], in_=yb_[k2][:, hs]),
                 reads=[f"yb{k2}_{half}"], writes=[f"stats{half}"])
        S.op("dve", lambda: dve.bn_aggr(out=mv[:, 0:2], in_=stats[:].rearrange("p a b -> p (a b)")),
             reads=["stats0", "stats1"], writes=["mv01"])
        S.op("act", lambda: act.activation(out=mv[:, 2:3], in_=mv[:, 1:2], func=AF.Ln, bias=eps_l), reads=["mv01", "eps_l"],
             writes=["mv2"])
        S.op("act", lambda: act.activation(out=mv[:, 3:4], in_=mv[:, 2:3], func=AF.Exp, scale=-0.5), reads=["mv2"],
             writes=["mv3"])
        ynames = [f"yb{k2}_0", f"yb{k2}_1"]
        S.op("dve", lambda: dve.tensor_scalar(out=mv[:, 4:5], in0=mv[:, 0:1], scalar1=mv[:, 3:4], scalar2=-1.0,
                                              op0=ALU.mult, op1=ALU.mult), reads=["mv01", "mv3"], writes=["mv4"])
        S.op("act", lambda k2=k2: act.activation(out=yb_[k2][:], in_=yb_[k2][:], func=AF.Identity, bias=mv[:, 4:5],
                                                 scale=mv[:, 3:4]), reads=ynames + ["mv3", "mv4"], writes=ynames)

    def out_stage2(tt):
        k2 = tt % 2
        ynames = [f"yb{k2}_0", f"yb{k2}_1"]
        S.op("dve", lambda k2=k2: dve.tensor_tensor(out=yb_[k2][:], in0=yb_[k2][:], in1=lng[:], op=ALU.mult),
             reads=ynames + ["lng"], writes=ynames)
        S.op("dve", lambda k2=k2: dve.tensor_tensor(out=yb_[k2][:], in0=yb_[k2][:], in1=lnb[:], op=ALU.add),
             reads=ynames + ["lnb"], writes=ynames)
        S.dma("sp", out_d[tt * 128:(tt + 1) * 128, :], yb_[k2][:], reads=ynames, writes=[f"out{tt}"])

    for tt in range(16):
        out_stage1(tt)
        if tt > 0:
            out_stage2(tt - 1)
    out_stage2(15)

    if DEBUG:
        S.barrier()
        S.dma("sp", dbg_m[:], merged[:].rearrange("p a b -> p (a b)"), writes=["dbg_m"])
    S.wait_all_dma("sp")
    S.barrier()
    st_c.close()
    es.close()
    return nc


_CACHE = {}


def _chunks(w, n):
    return np.ascontiguousarray(w.reshape(8, 128, n, 128).transpose(2, 1, 0, 3).reshape(n, 128, 1024))


def kernel(x, w_in, b_gate, w_proj_a, w_proj_b, w_out, da_lambda_q1, da_lambda_k1, da_lambda_q2, da_lambda_k2,
           da_subln_w, rel_bias, ln_g, ln_b):
    f = np.float32
    x = np.asarray(x, f)
    B = x.shape[0]
    if "nc" not in _CACHE:
        _CACHE["nc"] = build_nc()
        _CACHE["oh"] = _onehots()
    nc = _CACHE["nc"]
    oha, ohb = _CACHE["oh"]
    win_r = _chunks(np.asarray(w_in, f)[0], NCHUNK)
    pa_r = _chunks(np.asarray(w_proj_a, f)[0], 8)
    pb_r = _chunks(np.asarray(w_proj_b, f)[0], 8)
    wout = np.ascontiguousarray(np.asarray(w_out, f)[0])
    bg = np.ascontiguousarray(np.asarray(b_gate, f)[0].reshape(2, 8, 128).transpose(2, 0, 1).reshape(128, 16))
    lam = np.ascontiguousarray(np.concatenate([np.asarray(a, f).reshape(1, 64) for a in
                                               (da_lambda_q1, da_lambda_k1, da_lambda_q2, da_lambda_k2)], axis=0))
    subw = np.ascontiguousarray(np.asarray(da_subln_w, f).reshape(128, 1))
    relb = np.ascontiguousarray(np.asarray(rel_bias, f))
    lng = np.ascontiguousarray(np.asarray(ln_g, f).reshape(1, D))
    lnb = np.ascontiguousarray(np.asarray(ln_b, f).reshape(1, D))
    ident = np.eye(128, dtype=f)
    jmat = np.ascontiguousarray(np.eye(128, dtype=f)[::-1])
    shared = {"win": win_r, "pa": pa_r, "pb": pb_r, "wout": wout, "bg": bg, "lam": lam, "subw": subw, "relb": relb,
              "lng": lng, "lnb": lnb, "oha": oha, "ohb": ohb, "ident": ident, "jmat": jmat}
    in_maps = []
    for b in range(B):
        m = dict(shared)
        m["xT"] = np.ascontiguousarray(x[b].T)
        m["xtok"] = np.ascontiguousarray(x[b])
        in_maps.append(m)
    res = run_bass_kernel_spmd(nc, in_maps, core_ids=list(range(B)))
    _CACHE["last"] = res
    return np.stack([np.asarray(r["out"], f) for r in res.results], axis=0)
```

```python
import math
from contextlib import ExitStack

import numpy as np
import ml_dtypes

import concourse.bass as bass
import concourse.mybir as mybir
from concourse.bass_utils import run_bass_kernel_spmd

F32 = mybir.dt.float32
BF16 = mybir.dt.bfloat16
AF = mybir.ActivationFunctionType
ALU = mybir.AluOpType
AX = mybir.AxisListType

S_TOK = 2048
D = 1024
NCHUNK = 80
LAM_INIT = 0.2
ALPHA = 2.0 ** 0.25
SUBLN_EPS = 1e-5
LN_EPS = 1e-5
RA = 2560
TW = 2432
RB = 384
DILS = (1, 4, 16)
NEG = -30000.0

DEBUG = False


class _Buf:
    __slots__ = ("w", "r")

    def __init__(self):
        self.w = None
        self.r = {}


class Sched:
    def __init__(self, nc, n_dma=32):
        self.nc = nc
        self.engs = {"pe": nc.tensor, "act": nc.scalar, "dve": nc.vector, "pool": nc.gpsimd, "sp": nc.sync}
        self.sems = {}
        for e in ("pe", "act", "dve", "pool"):
            self.sems[e] = nc.alloc_semaphore(name=f"sem_{e}")
        self.n_dma = n_dma
        for i in range(n_dma):
            self.sems[("d", i)] = nc.alloc_semaphore(name=f"sem_dma{i}")
        self.dma_cnt = [0] * n_dma
        self.dma_rr = {"pool": 0, "sp": n_dma // 2}
        self.cnt = {e: 0 for e in ("pe", "act", "dve", "pool")}
        self.known = {e: {} for e in self.engs}
        self.bufs = {}
        self.n_waits = 0
        self.swdge_out = []
        self.swdge_limit = 6

    def _b(self, name):
        b = self.bufs.get(name)
        if b is None:
            b = self.bufs[name] = _Buf()
        return b

    def _deps(self, reads, writes):
        ev = {}

        def add(k, v):
            if ev.get(k, 0) < v:
                ev[k] = v

        for b in reads:
            if b.w is not None:
                add(*b.w)
        for b in writes:
            if b.w is not None:
                add(*b.w)
            for k, v in b.r.items():
                add(k, v)
        return ev

    def _wait(self, eng, ev):
        kn = self.known[eng]
        for k, v in ev.items():
            if eng == "pe" and k == "pe":
                continue
            if kn.get(k, 0) >= v:
                continue
            self.engs[eng].wait_ge(self.sems[k], v)
            kn[k] = v
            self.n_waits += 1

    def _commit(self, event, reads, writes):
        k, v = event
        for b in writes:
            b.w = event
            b.r = {}
        for b in reads:
            if b in writes:
                continue
            if b.r.get(k, 0) < v:
                b.r[k] = v

    def op(self, eng, fn, reads=(), writes=()):
        rb = [self._b(x) for x in reads]
        wb = [self._b(x) for x in writes]
        self._wait(eng, self._deps(rb, wb))
        ins = fn()
        self.cnt[eng] += 1
        ins.then_inc(self.sems[eng], 1)
        self._commit((eng, self.cnt[eng]), rb, wb)

    def dma(self, q, out, in_, reads=(), writes=()):
        rb = [self._b(x) for x in reads]
        wb = [self._b(x) for x in writes]
        self._wait(q, self._deps(rb, wb))
        if q == "pool" and len(self.swdge_out) >= self.swdge_limit:
            k0, v0 = self.swdge_out.pop(0)
            self._wait("pool", {k0: v0})
        half = self.n_dma // 2
        i = self.dma_rr[q]
        base = 0 if q == "pool" else half
        self.dma_rr[q] = base + (i - base + 1) % half
        ins = self.engs[q].dma_start(out=out, in_=in_)
        self.dma_cnt[i] += 16
        ins.then_inc(self.sems[("d", i)], 16)
        if q == "pool":
            self.swdge_out.append((("d", i), self.dma_cnt[i]))
        self._commit((("d", i), self.dma_cnt[i]), rb, wb)

    def barrier(self):
        ev = {e: self.cnt[e] for e in ("pe", "act", "dve", "pool") if self.cnt[e] > 0}
        for i in range(self.n_dma):
            if self.dma_cnt[i] > 0:
                ev[("d", i)] = self.dma_cnt[i]
        for e in self.engs:
            self._wait(e, ev)

    def wait_all_dma(self, eng):
        ev = {("d", i): self.dma_cnt[i] for i in range(self.n_dma) if self.dma_cnt[i] > 0}
        self._wait(eng, ev)


def _bucket(d):
    d = np.maximum(np.asarray(d), 0).astype(np.int32)
    distf = np.maximum(d, 1).astype(np.float32)
    large = 16 + (np.log(distf / np.float32(16)) / np.float32(math.log(2048 / 16)) * np.float32(16)).astype(np.int32)
    large = np.minimum(large, 31)
    return np.where(d < 16, d, large)


def _onehots():
    oha = np.zeros((33, RA), np.float32)
    r = np.arange(RA) - 511
    bk = _bucket(r)
    for i in range(RA):
        if r[i] < 0:
            oha[32, i] = 1.0
        else:
            oha[bk[i], i] = 1.0
    ohb = np.zeros((33, 3 * RB), np.float32)
    for di, d in enumerate(DILS):
        dl = np.arange(RB) - 127
        bk = _bucket(dl * d)
        for i in range(RB):
            if 0 <= dl[i] <= 128:
                ohb[bk[i], di * RB + i] = 1.0
            else:
                ohb[32, di * RB + i] = 1.0
    return oha, ohb


def build_nc():
    nc = bass.Bass("TRN2", target_bir_lowering=False)
    dt = nc.dram_tensor
    xT_d = dt("xT", [D, S_TOK], F32, kind="ExternalInput").ap()
    xtok_d = dt("xtok", [S_TOK, D], F32, kind="ExternalInput").ap()
    win_d = dt("win", [NCHUNK, 128, 1024], F32, kind="ExternalInput").ap()
    pa_d = dt("pa", [8, 128, 1024], F32, kind="ExternalInput").ap()
    pb_d = dt("pb", [8, 128, 1024], F32, kind="ExternalInput").ap()
    wout_d = dt("wout", [D, D], F32, kind="ExternalInput").ap()
    bg_d = dt("bg", [128, 16], F32, kind="ExternalInput").ap()
    lam_d = dt("lam", [4, 64], F32, kind="ExternalInput").ap()
    subw_d = dt("subw", [128, 1], F32, kind="ExternalInput").ap()
    relb_d = dt("relb", [32, 24], F32, kind="ExternalInput").ap()
    lng_d = dt("lng", [1, D], F32, kind="ExternalInput").ap()
    lnb_d = dt("lnb", [1, D], F32, kind="ExternalInput").ap()
    oha_d = dt("oha", [33, RA], F32, kind="ExternalInput").ap()
    ohb_d = dt("ohb", [33, 3 * RB], F32, kind="ExternalInput").ap()
    ident_d = dt("ident", [128, 128], F32, kind="ExternalInput").ap()
    jmat_d = dt("jmat", [128, 128], F32, kind="ExternalInput").ap()
    out_d = dt("out", [S_TOK, D], F32, kind="ExternalOutput").ap()
    sa_d = dt("sa_scr", [8, RA], BF16, kind="Internal").ap()
    sb_d = dt("sb_scr", [16, 3 * RB], BF16, kind="Internal").ap()
    if DEBUG:
        dbg_a = dt("dbg_a", [128, 8 * S_TOK], F32, kind="ExternalOutput").ap()
        dbg_b = dt("dbg_b", [128, 8 * S_TOK], F32, kind="ExternalOutput").ap()
        dbg_m = dt("dbg_m", [128, 8 * S_TOK], BF16, kind="ExternalOutput").ap()

    S = Sched(nc)
    pe, act, dve, pool = nc.tensor, nc.scalar, nc.vector, nc.gpsimd

    es = ExitStack()

    def sb(stack, name, shape, dtype):
        return stack.enter_context(nc.sbuf_tensor(name, shape, dtype))

    xT = sb(es, "xT_sb", [128, 8, S_TOK], BF16)
    yag = sb(es, "yag", [128, 8, S_TOK], BF16)
    ybg = sb(es, "ybg", [128, 8, S_TOK], BF16)
    ident = sb(es, "ident_sb", [128, 128], BF16)
    jmat = sb(es, "jmat_sb", [128, 128], BF16)
    ones_bf = sb(es, "ones_bf", [128, 128], BF16)
    ones_f = sb(es, "ones_f", [128, 128], F32)
    neglam = sb(es, "neglam", [128, 1], F32)
    subw = sb(es, "subw_sb", [128, 1], F32)
    bg = sb(es, "bg_sb", [128, 16], F32)
    smallf = sb(es, "smallf", [128, 16], F32)
    ps = es.enter_context(nc.psum_tensor("ps", [128, 4096], F32))
    psb = ps[:].bitcast(BF16)

    def bank(i, rows=slice(0, 128), c0=0, c1=512):
        return ps[rows, i * 512 + c0:i * 512 + c1]

    def bankbf(i):
        return psb[:, i * 1024:(i + 1) * 1024]

    def pn(i):
        return f"ps{i}"

    act_state = {"f": None}
    warm = sb(es, "act_warm", [128, 4], F32)
    S.op("dve", lambda: dve.memset(warm[:], 1.0), writes=["warm_in"])

    def act_fn(func):
        if act_state["f"] is not func:
            act_state["f"] = func
            S.op("act", lambda: act.activation(out=warm[:, 2:3], in_=warm[:, 0:1], func=func), reads=["warm_in"],
                 writes=["warm_out"])
        return func

    for kc in range(8):
        S.dma("pool", xT[:, kc, :], xT_d[kc * 128:(kc + 1) * 128, :], writes=[f"xT{kc}"])
    XT_NAMES = [f"xT{kc}" for kc in range(8)]
    S.dma("pool", ident[:], ident_d[:], writes=["ident"])
    S.dma("pool", jmat[:], jmat_d[:], writes=["jmat"])
    S.dma("sp", subw[:], subw_d[:], writes=["subw"])
    S.dma("sp", bg[:], bg_d[:], writes=["bg"])
    S.op("dve", lambda: dve.memset(ones_bf[:], 1.0), writes=["ones_bf"])
    S.op("dve", lambda: dve.memset(ones_f[:], 1.0), writes=["ones_f"])

    st_a = ExitStack()
    NW = 8
    wch = [sb(st_a, f"wch{i}", [128, 1024], BF16) for i in range(NW)]
    wstate = {"rr": 0}

    def load_chunk(src_ap):
        i = wstate["rr"]
        wstate["rr"] = (i + 1) % NW
        S.dma("pool", wch[i][:], src_ap, writes=[f"wch{i}"])
        return i

    qT = sb(st_a, "qT", [128, S_TOK], BF16)
    kT = sb(st_a, "kT", [128, S_TOK], BF16)
    aT = sb(st_a, "aT", [128, S_TOK], BF16)
    zT = sb(st_a, "zT", [128, S_TOK], BF16)
    NP = 12
    Pb = [sb(st_a, f"P{i}", [128, 512], BF16) for i in range(NP)]
    pstate = {"rr": 0}

    def next_p():
        i = pstate["rr"]
        pstate["rr"] = (i + 1) % NP
        return i

    tf = [sb(st_a, f"tf{i}", [128, 512], F32) for i in range(5)]
    sqb = sb(st_a, "sqb", [128, 512], BF16)

    st_s = ExitStack()
    tabx = sb(st_s, "tabx", [33, 24], F32)
    oha = sb(st_s, "oha_sb", [33, RA], F32)
    ohb = sb(st_s, "ohb_sb", [33, 3 * RB], F32)
    ega = sb(st_s, "ega", [24, RA], BF16)
    egb = sb(st_s, "egb", [24, 3 * RB], BF16)
    lamv = sb(st_s, "lamv", [128, 4, 64], F32)
    lamt = sb(st_s, "lamt", [128, 2, 64], F32)
    S.op("dve", lambda: dve.memset(tabx[32:33, :], NEG), writes=["tabx_m"])
    S.dma("sp", tabx[0:32, :], relb_d[:], writes=["tabx"])
    S.dma("sp", oha[:], oha_d[:], writes=["oha"])
    S.dma("sp", ohb[:], ohb_d[:], writes=["ohb"])
    S.dma("sp", lamv[:].rearrange("p a b -> p (a b)"),
          bass.AP(tensor=lam_d.tensor, offset=0, ap=[[0, 128], [1, 256]]), writes=["lamv"])
    ring = {"rr": 0}

    def rbank():
        i = ring["rr"]
        ring["rr"] = (i + 1) % 4
        return 4 + i

    def setup_g(oh, ohname, eg, egname, width):
        c = 0
        while c < width:
            n = min(512, width - c)
            bi = rbank()
            S.op("pe", lambda bi=bi, c=c, n=n: pe.matmul(bank(bi, slice(0, 24), 0, n), lhsT=tabx[0:33, :],
                                                         rhs=oh[0:33, c:c + n], start=True, stop=True),
                 reads=["tabx", "tabx_m", ohname], writes=[pn(bi)])
            S.op("act", lambda bi=bi, c=c, n=n: act.activation(out=eg[0:24, c:c + n], in_=bank(bi, slice(0, 24), 0, n),
                                                               func=AF.Exp),
                 reads=[pn(bi)], writes=[egname])
            c += n

    setup_g(oha, "oha", ega, "ega", RA)
    setup_g(ohb, "ohb", egb, "egb", 3 * RB)
    S.dma("sp", sa_d[:], ega[0:8, :], reads=["ega"], writes=["SA"])
    S.dma("sp", sb_d[:], egb[8:24, :], reads=["egb"], writes=["SB"])

    S.op("dve", lambda: dve.tensor_tensor(out=lamt[:, 0, :], in0=lamv[:, 0, :], in1=lamv[:, 1, :], op=ALU.mult),
         reads=["lamv"], writes=["lamt0"])
    S.op("dve", lambda: dve.tensor_tensor(out=lamt[:, 1, :], in0=lamv[:, 2, :], in1=lamv[:, 3, :], op=ALU.mult),
         reads=["lamv"], writes=["lamt1"])
    S.op("dve", lambda: dve.reduce_sum(out=smallf[:, 0:1], in_=lamt[:, 0, :], axis=AX.X), reads=["lamt0"], writes=["sm0"])
    S.op("dve", lambda: dve.reduce_sum(out=smallf[:, 1:2], in_=lamt[:, 1, :], axis=AX.X), reads=["lamt1"], writes=["sm1"])
    S.op("act", lambda: act.activation(out=smallf[:, 2:4], in_=smallf[:, 0:2], func=AF.Exp), reads=["sm0", "sm1"],
         writes=["sm23"])
    S.op("dve", lambda: dve.tensor_tensor(out=smallf[:, 4:5], in0=smallf[:, 3:4], in1=smallf[:, 2:3], op=ALU.subtract),
         reads=["sm23"], writes=["sm4"])
    S.op("dve", lambda: dve.tensor_scalar(out=neglam[:], in0=smallf[:, 4:5], scalar1=-LAM_INIT, scalar2=None, op0=ALU.add),
         reads=["sm4"], writes=["neglam"])
    S.op("dve", lambda: dve.tensor_scalar(out=smallf[:, 5:6], in0=subw[:], scalar1=1.0 - LAM_INIT, scalar2=None,
                                          op0=ALU.mult), reads=["subw"], writes=["subw_s"])
    S.op("dve", lambda: dve.memset(smallf[:, 6:7], SUBLN_EPS), writes=["eps_a"])
    S.op("dve", lambda: dve.memset(smallf[:, 7:8], LN_EPS), writes=["eps_l"])
    subw_s = smallf[:, 5:6]
    eps_a = smallf[:, 6:7]
    eps_l = smallf[:, 7:8]

    def proj(wi, dst, dname, kind):
        w = wch[wi]
        for tb in range(4):
            bi = rbank()

            def mm(bi=bi, tb=tb):
                ins = None
                for kc in range(8):
                    ins = pe.matmul(bank(bi), lhsT=w[:, kc * 128:(kc + 1) * 128], rhs=xT[:, kc, tb * 512:(tb + 1) * 512],
                                    start=(kc == 0), stop=(kc == 7))
                return ins

            S.op("pe", mm, reads=[f"wch{wi}"] + XT_NAMES, writes=[pn(bi)])
            o = dst[:, tb * 512:(tb + 1) * 512]
            if kind == "silu":
                S.op("act", lambda bi=bi, o=o: act.activation(out=o, in_=bank(bi), func=AF.Silu), reads=[pn(bi)],
                     writes=[f"{dname}{tb}"])
            elif kind == "act":
                S.op("act", lambda bi=bi, o=o: act.copy(out=o, in_=bank(bi)), reads=[pn(bi)], writes=[f"{dname}{tb}"])
            else:
                S.op("dve", lambda bi=bi, o=o: dve.tensor_copy(out=o, in_=bank(bi)), reads=[pn(bi)],
                     writes=[f"{dname}{tb}"])

    def names4(n):
        return [f"{n}{tb}" for tb in range(4)]

    st_pa = ExitStack()
    vtok = sb(st_pa, "vtok", [128, 16, 128], BF16)
    XT = sb(st_pa, "XTh", [128, TW], BF16)
    Ttab = sb(st_pa, "Ttab", [128, TW], BF16)

    def phaseA_prep(h):
        wq = load_chunk(win_d[h])
        wk = load_chunk(win_d[8 + h])
        wv = load_chunk(win_d[16 + h])
        wz = load_chunk(win_d[24 + h])
        S.dma("sp", XT[:], bass.AP(tensor=sa_d.tensor, offset=h * RA, ap=[[1, 128], [1, TW]]), reads=["SA"],
              writes=["XT"])
        return wq, wk, wv, wz

    def phaseA_proj(ws, hook=None):
        wq, wk, wv, wz = ws
        proj(wq, qT, "qT", "dve")
        proj(wk, kT, "kT", "dve")
        if hook is not None:
            hook()
        proj(wv, aT, "aT", "act")
        proj(wz, zT, "zT", "silu")
        for g in range(2):
            bi = rbank()

            def tr(bi=bi, g=g):
                ins = None
                for t8 in range(8):
                    ti = 8 * g + t8
                    ins = pe.transpose(bankbf(bi)[:, t8 * 128:(t8 + 1) * 128], aT[:, ti * 128:(ti + 1) * 128], ident[:])
                return ins

            S.op("pe", tr, reads=names4("aT") + ["ident"], writes=[pn(bi)])
            S.op("act", lambda bi=bi, g=g: act.copy(out=vtok[:, 8 * g:8 * g + 8, :].rearrange("p a b -> p (a b)"),
                                                   in_=bankbf(bi)), reads=[pn(bi)], writes=[f"vtok{g}"])
        c = 0
        while c < TW:
            n = min(512, TW - c)
            bi = rbank()
            S.op("pe", lambda bi=bi, c=c, n=n: pe.matmul(bank(bi, slice(0, 128), 0, n), lhsT=jmat[:], rhs=XT[:, c:c + n],
                                                         start=True, stop=True), reads=["jmat", "XT"], writes=[pn(bi)])
            S.op("dve", lambda bi=bi, c=c, n=n: dve.tensor_copy(out=Ttab[:, c:c + n], in_=bank(bi, slice(0, 128), 0, n)),
                 reads=[pn(bi)], writes=["Ttab"])
            c += n

    OB_, DB_ = (0, 1), (2, 3)
    mmc = {"n": 0}

    def phaseA_attn(h):
        for qb in range(4):
            nkt = 4 * qb + 4
            pending = []

            def flush(keep):
                while len(pending) > keep:
                    pending.pop(0)()

            for kt in range(nkt):
                j = kt - 4 * qb
                c0 = 128 * j if j > 0 else 0
                off = 128 * (4 * qb - kt) + 384
                for m in range(2):
                    rows = slice(64 * m, 64 * m + 64)
                    sbk = rbank()
                    S.op("pe", lambda sbk=sbk, rows=rows, kt=kt, c0=c0, qb=qb: pe.matmul(
                        bank(sbk, slice(0, 128), c0, 512), lhsT=kT[rows, kt * 128:(kt + 1) * 128],
                        rhs=qT[rows, qb * 512 + c0:(qb + 1) * 512], start=True, stop=True),
                        reads=names4("kT") + names4("qT"), writes=[pn(sbk)])
                    pi = next_p()
                    S.op("act", lambda sbk=sbk, pi=pi, c0=c0: act.activation(out=Pb[pi][:, c0:512],
                                                                             in_=bank(sbk, slice(0, 128), c0, 512),
                                                                             func=AF.Exp, scale=0.125),
                         reads=[pn(sbk)], writes=[f"P{pi}"])
                    mmc["n"] += 1
                    if True:
                        S.op("dve", lambda pi=pi, c0=c0, off=off: dve.tensor_tensor(
                            out=Pb[pi][:, c0:512], in0=Pb[pi][:, c0:512], in1=Ttab[:, off + c0:off + 512], op=ALU.mult),
                            reads=[f"P{pi}", "Ttab"], writes=[f"P{pi}"])
                    else:
                        S.op("pool", lambda pi=pi, c0=c0, off=off: pool.tensor_tensor(
                            out=Pb[pi][:, c0:512], in0=Pb[pi][:, c0:512], in1=Ttab[:, off + c0:off + 512], op=ALU.mult),
                            reads=[f"P{pi}", "Ttab"], writes=[f"P{pi}"])

                    def av(m=m, pi=pi, c0=c0, kt=kt, nkt=nkt):
                        def f():
                            pe.matmul(bank(OB_[m], slice(0, 128), c0, 512), lhsT=vtok[:, kt, :], rhs=Pb[pi][:, c0:512],
                                      start=(kt == 0), stop=(kt == nkt - 1))
                            return pe.matmul(bank(DB_[m], slice(0, 128), c0, 512), lhsT=ones_bf[:], rhs=Pb[pi][:, c0:512],
                                             start=(kt == 0), stop=(kt == nkt - 1))

                        S.op("pe", f, reads=[f"P{pi}", "vtok0", "vtok1", "ones_bf"], writes=[pn(OB_[m]), pn(DB_[m])])

                    pending.append(av)
                flush(6)
                if kt == 1:
                    run_pending_finA()
            flush(0)
            fin_part1()
            finA["p2"] = (lambda h=h, qb=qb: fin_part2(h, qb))

    finA = {"p2": None}

    def run_pending_finA():
        f = finA["p2"]
        if f is not None:
            finA["p2"] = None
            f()

    def fin_part1():
        r0, r1, o0, o1, tt = tf
        S.op("act", lambda: act.activation(out=r0[:], in_=bank(DB_[0]), func=AF.Ln), reads=[pn(DB_[0])], writes=["tf0"])
        S.op("act", lambda: act.activation(out=r1[:], in_=bank(DB_[1]), func=AF.Ln), reads=[pn(DB_[1])], writes=["tf1"])
        S.op("act", lambda: act.activation(out=r0[:], in_=r0[:], func=AF.Exp, scale=-1.0), reads=["tf0"], writes=["tf0"])
        S.op("act", lambda: act.activation(out=r1[:], in_=r1[:], func=AF.Exp, scale=-1.0), reads=["tf1"], writes=["tf1"])
        S.op("dve", lambda: dve.tensor_tensor(out=o0[:], in0=bank(OB_[0]), in1=r0[:], op=ALU.mult),
             reads=[pn(OB_[0]), "tf0"], writes=["tf2"])
        S.op("dve", lambda: dve.tensor_tensor(out=o1[:], in0=bank(OB_[1]), in1=r1[:], op=ALU.mult),
             reads=[pn(OB_[1]), "tf1"], writes=["tf3"])
        S.op("dve", lambda: dve.scalar_tensor_tensor(out=o0[:], in0=o1[:], scalar=neglam[:, 0:1], in1=o0[:],
                                                     op0=ALU.mult, op1=ALU.add),
             reads=["tf2", "tf3", "neglam"], writes=["tf2"])
        S.op("pool", lambda: pool.tensor_tensor(out=sqb[:], in0=o0[:], in1=o0[:], op=ALU.mult), reads=["tf2"],
             writes=["sqb"])

    def fin_part2(h, qb):
        r0, r1, o0, o1, tt = tf
        bi = rbank()
        S.op("pe", lambda bi=bi: pe.matmul(bank(bi), lhsT=ones_bf[:], rhs=sqb[:], start=True, stop=True),
             reads=["sqb", "ones_bf"], writes=[pn(bi)])
        S.op("act", lambda bi=bi: act.activation(out=r0[:], in_=bank(bi), func=AF.Ln, bias=eps_a, scale=1.0 / 128.0),
             reads=[pn(bi), "eps_a", "tf0"], writes=["tf0"])
        S.op("act", lambda: act.activation(out=r1[:], in_=r0[:], func=AF.Exp, scale=-0.5), reads=["tf0", "tf1"],
             writes=["tf1"])
        S.op("dve", lambda: dve.tensor_tensor(out=tt[:], in0=o0[:], in1=r1[:], op=ALU.mult), reads=["tf2", "tf1"],
             writes=["tf4"])
        S.op("dve", lambda h=h, qb=qb: dve.scalar_tensor_tensor(
            out=yag[:, h, qb * 512:(qb + 1) * 512], in0=tt[:], scalar=subw_s, in1=zT[:, qb * 512:(qb + 1) * 512],
            op0=ALU.mult, op1=ALU.mult), reads=["tf4", "subw_s", f"zT{qb}"], writes=[f"yag{h}_{qb}"])

    ws = phaseA_prep(0)
    for h in range(8):
        phaseA_proj(ws, run_pending_finA)
        if h + 1 < 8:
            ws = phaseA_prep(h + 1)
        phaseA_attn(h)
    run_pending_finA()

    S.barrier()
    st_pa.close()
    st_s.close()

    st_pb = ExitStack()
    q4 = sb(st_pb, "q4", [128, S_TOK], BF16)
    k4 = sb(st_pb, "k4", [128, S_TOK], BF16)
    q16 = sb(st_pb, "q16", [128, S_TOK], BF16)
    k16 = sb(st_pb, "k16", [128, S_TOK], BF16)
    Vd = sb(st_pb, "Vd", [128, 3, 16, 193], BF16)
    XB = sb(st_pb, "XB", [128, 6 * 256], BF16)
    Btab = sb(st_pb, "Btab", [128, 6, 256], BF16)
    accs = sb(st_pb, "accs", [128, S_TOK], F32)
    zr = [sb(st_pb, f"zr{i}", [128, 512], F32) for i in range(2)]

    S.op("pool", lambda: pool.memset(Vd[:].rearrange("p a b c -> p (a b c)"), 0.0), writes=["Vd"])
    S.op("pool", lambda: pool.memset(Vd[:, :, :, 64:66], 1.0), reads=[], writes=["Vd"])

    def phaseB_prep(j):
        wq = load_chunk(win_d[32 + j])
        wk = load_chunk(win_d[40 + j])
        wv = load_chunk(win_d[48 + j])
        wz = load_chunk(win_d[56 + j])
        S.dma("sp", XB[:].rearrange("p (a b c) -> p a b c", a=2, b=3),
              bass.AP(tensor=sb_d.tensor, offset=2 * j * 3 * RB, ap=[[1, 128], [3 * RB, 2], [RB, 3], [1, 256]]),
              reads=["SB"], writes=["XB"])
        return wq, wk, wv, wz

    qperm = {1: qT, 4: q4, 16: q16}
    kperm = {1: kT, 4: k4, 16: k16}
    sring = {"rr": 0}

    def phaseB_proj(ws, hook=None):
        wq, wk, wv, wz = ws
        proj(wq, qT, "qT", "dve")
        proj(wk, kT, "kT", "dve")
        if hook is not None:
            hook()
        proj(wv, aT, "aT", "act")
        proj(wz, zT, "zT", "silu")
        for d, qd, kd in ((4, q4, k4), (16, q16, k16)):
            if d == 4:
                S.op("dve", lambda d=d, qd=qd: dve.tensor_copy(out=qd[:].rearrange("p (r l) -> p r l", r=d),
                                                              in_=qT[:].rearrange("p (l r) -> p r l", r=d)),
                     reads=names4("qT"), writes=[f"q{d}"])
                S.op("dve", lambda d=d, kd=kd: dve.tensor_copy(out=kd[:].rearrange("p (r l) -> p r l", r=d),
                                                              in_=kT[:].rearrange("p (l r) -> p r l", r=d)),
                     reads=names4("kT"), writes=[f"k{d}"])
            else:
                S.op("act", lambda d=d, qd=qd: act.copy(out=qd[:].rearrange("p (r l) -> p r l", r=d),
                                                       in_=qT[:].rearrange("p (l r) -> p r l", r=d)),
                     reads=names4("qT"), writes=[f"q{d}"])
                S.op("act", lambda d=d, kd=kd: act.copy(out=kd[:].rearrange("p (r l) -> p r l", r=d),
                                                       in_=kT[:].rearrange("p (l r) -> p r l", r=d)),
                     reads=names4("kT"), writes=[f"k{d}"])
        for di, d in enumerate(DILS):
            av = aT[:].rearrange("p (l r) -> p r l", r=d)
            nb = (S_TOK // d) // 128
            for g in range(2):
                bi = rbank()

                def tr(bi=bi, g=g, av=av, nb=nb):
                    ins = None
                    for t8 in range(8):
                        ti = 8 * g + t8
                        r, n = ti // nb, ti % nb
                        ins = pe.transpose(bankbf(bi)[:, t8 * 128:(t8 + 1) * 128], av[:, r, n * 128:(n + 1) * 128],
                                           ident[:])
                    return ins

                S.op("pe", tr, reads=names4("aT") + ["ident"], writes=[pn(bi)])
                src = bankbf(bi).rearrange("p (a b) -> p a b", a=8)
                S.op("act", lambda bi=bi, g=g, di=di, src=src: act.copy(out=Vd[:, di, 8 * g:8 * g + 8, 0:64],
                                                                       in_=src[:, :, 0:64]),
                     reads=[pn(bi)], writes=["Vd"])
                S.op("dve", lambda bi=bi, g=g, di=di, src=src: dve.tensor_copy(out=Vd[:, di, 8 * g:8 * g + 8, 129:193],
                                                                              in_=src[:, :, 64:128]),
                     reads=[pn(bi)], writes=["Vd"])
        for c in range(3):
            bi = rbank()
            S.op("pe", lambda bi=bi, c=c: pe.matmul(bank(bi), lhsT=jmat[:], rhs=XB[:, c * 512:(c + 1) * 512], start=True,
                                                    stop=True), reads=["jmat", "XB"], writes=[pn(bi)])
            S.op("dve", lambda bi=bi, c=c: dve.tensor_copy(out=Btab[:, 2 * c:2 * c + 2, :].rearrange("p a b -> p (a b)"),
                                                          in_=bank(bi)), reads=[pn(bi)], writes=["Btab"])

    def phaseB_attn(j):
        for hh in range(2):
            rows = slice(64 * hh, 64 * hh + 64)
            started = set()
            tiles = []
            for di, d in enumerate(DILS):
                nb = (S_TOK // d) // 128
                for r in range(d):
                    for n in range(nb):
                        tiles.append((di, d, nb, r, n))
            pending = []

            def flush(keep):
                while len(pending) > keep:
                    pending.pop(0)()

            for gi in range(0, len(tiles), 2):
                grp = tiles[gi:gi + 2]
                di, d = grp[0][0], grp[0][1]
                sbk = rbank()

                def qk(grp=grp, sbk=sbk, rows=rows):
                    ins = None
                    for s, (di_, d_, nb, r, n) in enumerate(grp):
                        nq = 256 if n + 1 < nb else 128
                        col = (r * nb + n) * 128
                        ins = pe.matmul(bank(sbk, slice(0, 128), s * 256, s * 256 + nq),
                                        lhsT=kperm[d_][rows, col:col + 128], rhs=qperm[d_][rows, col:col + nq],
                                        start=True, stop=True)
                    return ins

                qn = names4("qT") if d == 1 else [f"q{d}"]
                kn = names4("kT") if d == 1 else [f"k{d}"]
                S.op("pe", qk, reads=qn + kn, writes=[pn(sbk)])
                pi = next_p()
                S.op("act", lambda pi=pi, sbk=sbk: act.activation(out=Pb[pi][:], in_=bank(sbk), func=AF.Exp, scale=0.125),
                     reads=[pn(sbk)], writes=[f"P{pi}"])
                bsl = Btab[:, hh * 3 + di, :]
                bap = bass.AP(tensor=bsl.tensor, offset=bsl.offset, ap=[list(bsl.ap[0]), [0, 2], [1, 256]])
                pv = Pb[pi][:].rearrange("p (a b) -> p a b", a=2)
                if True:
                    S.op("dve", lambda pv=pv, bap=bap: dve.tensor_tensor(out=pv, in0=pv, in1=bap, op=ALU.mult),
                         reads=[f"P{pi}", "Btab"], writes=[f"P{pi}"])
                else:
                    S.op("pool", lambda pv=pv, bap=bap: pool.tensor_tensor(out=pv, in0=pv, in1=bap, op=ALU.mult),
                         reads=[f"P{pi}", "Btab"], writes=[f"P{pi}"])

                def av(grp=grp, pi=pi, hh=hh):
                    wr = set()
                    plan = []
                    mrows = slice(0, 65) if hh == 0 else slice(0, 128)
                    for s, (di_, d_, nb, r, n) in enumerate(grp):
                        ti = r * nb + n
                        lhs = Vd[:, di_, ti, 0:65] if hh == 0 else Vd[:, di_, ti, 65:193]
                        for half in range(2):
                            nn = n + half
                            if nn >= nb:
                                continue
                            pc0 = s * 256 + half * 128
                            if d_ == 1:
                                b_ = nn // 4
                                plan.append((b_, bank(b_, mrows, (nn % 4) * 128, (nn % 4) * 128 + 128), lhs,
                                             Pb[pi][:, pc0:pc0 + 128]))
                            elif d_ == 4:
                                b_ = nn
                                o = bank(b_, mrows).rearrange("p (j d) -> p d j", d=4)[:, r, :]
                                plan.append((b_, o, lhs, Pb[pi][:, pc0:pc0 + 128]))
                            else:
                                for b_ in range(4):
                                    o = bank(b_, mrows).rearrange("p (j d) -> p d j", d=16)[:, r, :]
                                    plan.append((b_, o, lhs, Pb[pi][:, pc0 + 32 * b_:pc0 + 32 * b_ + 32]))
                    for b_, *_ in plan:
                        wr.add(b_)

                    def f():
                        ins = None
                        for b_, o, lhs, rhs in plan:
                            st = b_ not in started
                            started.add(b_)
                            ins = pe.matmul(o, lhsT=lhs, rhs=rhs, start=st, stop=False, skip_group_check=True)
                        return ins

                    S.op("pe", f, reads=[f"P{pi}", "Vd"], writes=[pn(b_) for b_ in sorted(wr)])

                pending.append(av)
                flush(4)
                if gi == 4:
                    run_pending_finB()
            flush(0)
            finB_part1(hh)
            finB["p2"] = (lambda j=j, hh=hh: finB_part2(j, hh))

    finB = {"p2": None}

    def run_pending_finB():
        f = finB["p2"]
        if f is not None:
            finB["p2"] = None
            f()

    def finB_part1(hh):
        mrows = slice(0, 65) if hh == 0 else slice(0, 128)
        for pc in range(4):
            csl = slice(pc * 512, (pc + 1) * 512)
            if pc % 2 == 0:
                S.op("act", lambda pc=pc, csl=csl, mrows=mrows: act.copy(out=accs[mrows, csl], in_=bank(pc, mrows)),
                     reads=[pn(pc)], writes=[f"accs{pc}"])
            else:
                S.op("dve", lambda pc=pc, csl=csl, mrows=mrows: dve.tensor_copy(out=accs[mrows, csl], in_=bank(pc, mrows)),
                     reads=[pn(pc)], writes=[f"accs{pc}"])
        row = 64 if hh == 0 else 0
        rsl = slice(row, row + 1)
        an = [f"accs{pc}" for pc in range(4)]
        S.op("act", lambda rsl=rsl: act.activation(out=accs[rsl, :], in_=accs[rsl, :], func=AF.Ln), reads=an, writes=an)
        S.op("act", lambda rsl=rsl: act.activation(out=accs[rsl, :], in_=accs[rsl, :], func=AF.Exp, scale=-1.0), reads=an,
             writes=an)

    def finB_part2(j, hh):
        rows = slice(64 * hh, 64 * hh + 64)
        row = 64 if hh == 0 else 0
        rsl = slice(row, row + 1)
        mm_rows = slice(0, 64) if hh == 0 else slice(0, 128)
        mcols = 64 if hh == 0 else 128
        for pc in range(4):
            csl = slice(pc * 512, (pc + 1) * 512)
            bi = rbank()
            S.op("pe", lambda bi=bi, csl=csl: pe.matmul(bank(bi, mm_rows), lhsT=ones_f[rsl, 0:mcols], rhs=accs[rsl, csl],
                                                        start=True, stop=True),
                 reads=[f"accs{pc}", "ones_f"], writes=[pn(bi)])
            k2 = pc % 2
            S.op("dve", lambda bi=bi, k2=k2, csl=csl: dve.tensor_tensor(out=zr[k2][rows, :], in0=bank(bi, rows),
                                                                       in1=accs[rows, csl], op=ALU.mult),
                 reads=[pn(bi), f"accs{pc}"], writes=[f"zr{k2}"])
            S.op("pool", lambda k2=k2, csl=csl: pool.tensor_tensor(out=ybg[rows, j, csl], in0=zr[k2][rows, :],
                                                                  in1=zT[rows, csl], op=ALU.mult),
                 reads=[f"zr{k2}", f"zT{pc}"], writes=[f"ybg{j}_{pc}_{hh}"])

    ws = phaseB_prep(0)
    for j in range(8):
        phaseB_proj(ws, run_pending_finB)
        if j + 1 < 8:
            ws = phaseB_prep(j + 1)
        phaseB_attn(j)
    run_pending_finB()

    S.barrier()
    st_pb.close()
    st_a.close()

    if DEBUG:
        st_d = ExitStack()
        dtmp = sb(st_d, "dtmp", [128, 8 * S_TOK], F32)
        S.op("dve", lambda: dve.tensor_copy(out=dtmp[:], in_=yag[:].rearrange("p a b -> p (a b)")), writes=["dtmp"])
        S.dma("sp", dbg_a[:], dtmp[:], reads=["dtmp"], writes=["dbg_a"])
        S.op("dve", lambda: dve.tensor_copy(out=dtmp[:], in_=ybg[:].rearrange("p a b -> p (a b)")), reads=["dbg_a"],
             writes=["dtmp"])
        S.dma("sp", dbg_b[:], dtmp[:], reads=["dtmp"], writes=["dbg_b"])
        S.barrier()
        st_d.close()

    st_c = ExitStack()
    merged = sb(st_c, "merged", [128, 8, S_TOK], BF16)
    wout = sb(st_c, "wout_sb", [128, 8, D], BF16)
    NWC = 8
    wc = [sb(st_c, f"wc{i}", [128, 1024], BF16) for i in range(NWC)]
    sg = [sb(st_c, f"sg{i}", [128, 512], F32) for i in range(4)]
    t12 = [sb(st_c, f"t12_{i}", [128, 512], F32) for i in range(4)]
    lng = sb(st_c, "lng_sb", [128, D], F32)
    lnb = sb(st_c, "lnb_sb", [128, D], F32)
    xtk = [sb(st_c, f"xtk{i}", [128, D], F32) for i in range(2)]
    yb_ = [sb(st_c, f"ybuf{i}", [128, D], F32) for i in range(2)]
    stats = sb(st_c, "stats", [128, 2, 6], F32)
    mv = sb(st_c, "mv", [128, 8], F32)

    S.dma("sp", lng[:], bass.AP(tensor=lng_d.tensor, offset=0, ap=[[0, 128], [1, D]]), writes=["lng"])
    S.dma("sp", lnb[:], bass.AP(tensor=lnb_d.tensor, offset=0, ap=[[0, 128], [1, D]]), writes=["lnb"])
    wcs = {"rr": 0}

    def load_c(src_ap):
        i = wcs["rr"]
        wcs["rr"] = (i + 1) % NWC
        S.dma("pool", wc[i][:], src_ap, writes=[f"wc{i}"])
        return i

    def prepC(dc):
        return (load_c(pa_d[dc]), load_c(pb_d[dc]), load_c(win_d[64 + dc]), load_c(win_d[72 + dc]))

    pring = {"rr": 0}

    def nbank():
        i = pring["rr"]
        pring["rr"] = (i + 1) % 8
        return i

    cw = prepC(0)
    it = 0
    for dc in range(8):
        wpa, wpb, wga, wgb = cw
        if dc + 1 < 8:
            cw = prepC(dc + 1)
        S.dma("pool", wout[:, dc, :], wout_d[dc * 128:(dc + 1) * 128, :], writes=[f"wout{dc}"])
        for tb in range(4):
            tsl = slice(tb * 512, (tb + 1) * 512)
            k2 = it % 2
            it += 1
            banks = {}
            for nm, wi, src, snames in (("ua", wpa, yag, [f"yag{kc}_{tb}" for kc in range(8)]),
                                        ("ga", wga, xT, XT_NAMES),
                                        ("ub", wpb, ybg, [f"ybg{kc}_{tb}_{hh}" for kc in range(8) for hh in range(2)]),
                                        ("gb", wgb, xT, XT_NAMES)):
                bi = nbank()
                banks[nm] = bi

                def mm(bi=bi, wi=wi, src=src, tsl=tsl):
                    ins = None
                    for kc in range(8):
                        ins = pe.matmul(bank(bi), lhsT=wc[wi][:, kc * 128:(kc + 1) * 128], rhs=src[:, kc, tsl],
                                        start=(kc == 0), stop=(kc == 7))
                    return ins

                S.op("pe", mm, reads=[f"wc{wi}"] + snames, writes=[pn(bi)])
            for gi, nm in enumerate(("ga", "gb")):
                bi = banks[nm]
                sgi = 2 * k2 + gi
                S.op("act", lambda bi=bi, sgi=sgi, gi=gi, dc=dc: act.activation(
                    out=sg[sgi][:], in_=bank(bi), func=AF.Sigmoid, bias=bg[:, gi * 8 + dc:gi * 8 + dc + 1]),
                    reads=[pn(bi), "bg"], writes=[f"sg{sgi}"])
            for gi, nm in enumerate(("ua", "ub")):
                bi = banks[nm]
                sgi = 2 * k2 + gi
                S.op("dve", lambda bi=bi, sgi=sgi: dve.tensor_tensor(out=t12[sgi][:], in0=bank(bi), in1=sg[sgi][:],
                                                                    op=ALU.mult),
                     reads=[pn(bi), f"sg{sgi}"], writes=[f"t12_{sgi}"])
            S.op("pool", lambda k2=k2, dc=dc, tsl=tsl: pool.tensor_tensor(out=merged[:, dc, tsl], in0=t12[2 * k2][:],
                                                                         in1=t12[2 * k2 + 1][:], op=ALU.add),
                 reads=[f"t12_{2 * k2}", f"t12_{2 * k2 + 1}"], writes=[f"mg{dc}_{tb}"])

    def out_stage1(tt):
        k2 = tt % 2
        tb = tt // 4
        S.dma("sp", xtk[k2][:], xtok_d[tt * 128:(tt + 1) * 128, :], writes=[f"xtk{k2}"])
        b0 = nbank()
        b1 = nbank()
        for half, bi in enumerate((b0, b1)):
            def mm(bi=bi, half=half, tt=tt):
                ins = None
                for kc in range(8):
                    ins = pe.matmul(bank(bi), lhsT=merged[:, kc, tt * 128:(tt + 1) * 128],
                                    rhs=wout[:, kc, half * 512:(half + 1) * 512], start=(kc == 0), stop=(kc == 7))
                return ins

            S.op("pe", mm, reads=[f"mg{kc}_{tb}" for kc in range(8)] + [f"wout{kc}" for kc in range(8)], writes=[pn(bi)])
        for half, bi in enumerate((b0, b1)):
            hs = slice(half * 512, (half + 1) * 512)
            S.op("dve", lambda bi=bi, hs=hs, k2=k2: dve.scalar_tensor_tensor(
                out=yb_[k2][:, hs], in0=xtk[k2][:, hs], scalar=ALPHA, in1=bank(bi), op0=ALU.mult, op1=ALU.add),
                reads=[pn(bi), f"xtk{k2}"], writes=[f"yb{k2}_{half}"])
            S.op("dve", lambda hs=hs, k2=k2, half=half: dve.bn_stats(out=stats[:, half, :], in_=yb_[k2][:, hs]),
                 reads=[f"yb{k2}_{half}"], writes=[f"stats{half}"])
        S.op("dve", lambda: dve.bn_aggr(out=mv[:, 0:2], in_=stats[:].rearrange("p a b -> p (a b)")),
             reads=["stats0", "stats1"], writes=["mv01"])
        S.op("act", lambda: act.activation(out=mv[:, 2:3], in_=mv[:, 1:2], func=AF.Ln, bias=eps_l), reads=["mv01", "eps_l"],
             writes=["mv2"])
        S.op("act", lambda: act.activation(out=mv[:, 3:4], in_=mv[:, 2:3], func=AF.Exp, scale=-0.5), reads=["mv2"],
             writes=["mv3"])
        ynames = [f"yb{k2}_0", f"yb{k2}_1"]
        S.op("dve", lambda: dve.tensor_scalar(out=mv[:, 4:5], in0=mv[:, 0:1], scalar1=mv[:, 3:4], scalar2=-1.0,
                                              op0=ALU.mult, op1=ALU.mult), reads=["mv01", "mv3"], writes=["mv4"])
        S.op("act", lambda k2=k2: act.activation(out=yb_[k2][:], in_=yb_[k2][:], func=AF.Identity, bias=mv[:, 4:5],
                                                 scale=mv[:, 3:4]), reads=ynames + ["mv3", "mv4"], writes=ynames)

    def out_stage2(tt):
        k2 = tt % 2
        ynames = [f"yb{k2}_0", f"yb{k2}_1"]
        S.op("dve", lambda k2=k2: dve.tensor_tensor(out=yb_[k2][:], in0=yb_[k2][:], in1=lng[:], op=ALU.mult),
             reads=ynames + ["lng"], writes=ynames)
        S.op("dve", lambda k2=k2: dve.tensor_tensor(out=yb_[k2][:], in0=yb_[k2][:], in1=lnb[:], op=ALU.add),
             reads=ynames + ["lnb"], writes=ynames)
        S.dma("sp", out_d[tt * 128:(tt + 1) * 128, :], yb_[k2][:], reads=ynames, writes=[f"out{tt}"])

    for tt in range(16):
        out_stage1(tt)
        if tt > 0:
            out_stage2(tt - 1)
    out_stage2(15)

    if DEBUG:
        S.barrier()
        S.dma("sp", dbg_m[:], merged[:].rearrange("p a b -> p (a b)"), writes=["dbg_m"])
    S.wait_all_dma("sp")
    S.barrier()
    st_c.close()
    es.close()
    return nc


_CACHE = {}


def _chunks(w, n):
    return np.ascontiguousarray(w.reshape(8, 128, n, 128).transpose(2, 1, 0, 3).reshape(n, 128, 1024))


def kernel(x, w_in, b_gate, w_proj_a, w_proj_b, w_out, da_lambda_q1, da_lambda_k1, da_lambda_q2, da_lambda_k2,
           da_subln_w, rel_bias, ln_g, ln_b):
    f = np.float32
    x = np.asarray(x, f)
    B = x.shape[0]
    if "nc" not in _CACHE:
        _CACHE["nc"] = build_nc()
        _CACHE["oh"] = _onehots()
    nc = _CACHE["nc"]
    oha, ohb = _CACHE["oh"]
    win_r = _chunks(np.asarray(w_in, f)[0], NCHUNK)
    pa_r = _chunks(np.asarray(w_proj_a, f)[0], 8)
    pb_r = _chunks(np.asarray(w_proj_b, f)[0], 8)
    wout = np.ascontiguousarray(np.asarray(w_out, f)[0])
    bg = np.ascontiguousarray(np.asarray(b_gate, f)[0].reshape(2, 8, 128).transpose(2, 0, 1).reshape(128, 16))
    lam = np.ascontiguousarray(np.concatenate([np.asarray(a, f).reshape(1, 64) for a in
                                               (da_lambda_q1, da_lambda_k1, da_lambda_q2, da_lambda_k2)], axis=0))
    subw = np.ascontiguousarray(np.asarray(da_subln_w, f).reshape(128, 1))
    relb = np.ascontiguousarray(np.asarray(rel_bias, f))
    lng = np.ascontiguousarray(np.asarray(ln_g, f).reshape(1, D))
    lnb = np.ascontiguousarray(np.asarray(ln_b, f).reshape(1, D))
    ident = np.eye(128, dtype=f)
    jmat = np.ascontiguousarray(np.eye(128, dtype=f)[::-1])
    shared = {"win": win_r, "pa": pa_r, "pb": pb_r, "wout": wout, "bg": bg, "lam": lam, "subw": subw, "relb": relb,
              "lng": lng, "lnb": lnb, "oha": oha, "ohb": ohb, "ident": ident, "jmat": jmat}
    in_maps = []
    for b in range(B):
        m = dict(shared)
        m["xT"] = np.ascontiguousarray(x[b].T)
        m["xtok"] = np.ascontiguousarray(x[b])
        in_maps.append(m)
    res = run_bass_kernel_spmd(nc, in_maps, core_ids=list(range(B)))
    _CACHE["last"] = res
    return np.stack([np.asarray(r["out"], f) for r in res.results], axis=0)
```

```python
import math
from contextlib import ExitStack

import numpy as np
import ml_dtypes

import concourse.bass as bass
import concourse.mybir as mybir
from concourse.bass_utils import run_bass_kernel_spmd

F32 = mybir.dt.float32
BF16 = mybir.dt.bfloat16
AF = mybir.ActivationFunctionType
ALU = mybir.AluOpType
AX = mybir.AxisListType

S_TOK = 2048
D = 1024
NCHUNK = 80
LAM_INIT = 0.2
ALPHA = 2.0 ** 0.25
SUBLN_EPS = 1e-5
LN_EPS = 1e-5
RA = 2560
TW = 2432
RB = 384
DILS = (1, 4, 16)
NEG = -30000.0

DEBUG = False


class _Buf:
    __slots__ = ("w", "r")

    def __init__(self):
        self.w = None
        self.r = {}


class Sched:
    def __init__(self, nc, n_dma=32):
        self.nc = nc
        self.engs = {"pe": nc.tensor, "act": nc.scalar, "dve": nc.vector, "pool": nc.gpsimd, "sp": nc.sync}
        self.sems = {}
        for e in ("pe", "act", "dve", "pool"):
            self.sems[e] = nc.alloc_semaphore(name=f"sem_{e}")
        self.n_dma = n_dma
        for i in range(n_dma):
            self.sems[("d", i)] = nc.alloc_semaphore(name=f"sem_dma{i}")
        self.dma_cnt = [0] * n_dma
        self.dma_rr = {"pool": 0, "sp": n_dma // 2}
        self.cnt = {e: 0 for e in ("pe", "act", "dve", "pool")}
        self.known = {e: {} for e in self.engs}
        self.bufs = {}
        self.n_waits = 0
        self.swdge_out = []
        self.swdge_limit = 6

    def _b(self, name):
        b = self.bufs.get(name)
        if b is None:
            b = self.bufs[name] = _Buf()
        return b

    def _deps(self, reads, writes):
        ev = {}

        def add(k, v):
            if ev.get(k, 0) < v:
                ev[k] = v

        for b in reads:
            if b.w is not None:
                add(*b.w)
        for b in writes:
            if b.w is not None:
                add(*b.w)
            for k, v in b.r.items():
                add(k, v)
        return ev

    def _wait(self, eng, ev):
        kn = self.known[eng]
        for k, v in ev.items():
            if eng == "pe" and k == "pe":
                continue
            if kn.get(k, 0) >= v:
                continue
            self.engs[eng].wait_ge(self.sems[k], v)
            kn[k] = v
            self.n_waits += 1

    def _commit(self, event, reads, writes):
        k, v = event
        for b in writes:
            b.w = event
            b.r = {}
        for b in reads:
            if b in writes:
                continue
            if b.r.get(k, 0) < v:
                b.r[k] = v

    def op(self, eng, fn, reads=(), writes=()):
        rb = [self._b(x) for x in reads]
        wb = [self._b(x) for x in writes]
        self._wait(eng, self._deps(rb, wb))
        ins = fn()
        self.cnt[eng] += 1
        ins.then_inc(self.sems[eng], 1)
        self._commit((eng, self.cnt[eng]), rb, wb)

    def dma(self, q, out, in_, reads=(), writes=()):
        rb = [self._b(x) for x in reads]
        wb = [self._b(x) for x in writes]
        self._wait(q, self._deps(rb, wb))
        if q == "pool" and len(self.swdge_out) >= self.swdge_limit:
            k0, v0 = self.swdge_out.pop(0)
            self._wait("pool", {k0: v0})
        half = self.n_dma // 2
        i = self.dma_rr[q]
        base = 0 if q == "pool" else half
        self.dma_rr[q] = base + (i - base + 1) % half
        ins = self.engs[q].dma_start(out=out, in_=in_)
        self.dma_cnt[i] += 16
        ins.then_inc(self.sems[("d", i)], 16)
        if q == "pool":
            self.swdge_out.append((("d", i), self.dma_cnt[i]))
        self._commit((("d", i), self.dma_cnt[i]), rb, wb)

    def barrier(self):
        ev = {e: self.cnt[e] for e in ("pe", "act", "dve", "pool") if self.cnt[e] > 0}
        for i in range(self.n_dma):
            if self.dma_cnt[i] > 0:
                ev[("d", i)] = self.dma_cnt[i]
        for e in self.engs:
            self._wait(e, ev)

    def wait_all_dma(self, eng):
        ev = {("d", i): self.dma_cnt[i] for i in range(self.n_dma) if self.dma_cnt[i] > 0}
        self._wait(eng, ev)


def _bucket(d):
    d = np.maximum(np.asarray(d), 0).astype(np.int32)
    distf = np.maximum(d, 1).astype(np.float32)
    large = 16 + (np.log(distf / np.float32(16)) / np.float32(math.log(2048 / 16)) * np.float32(16)).astype(np.int32)
    large = np.minimum(large, 31)
    return np.where(d < 16, d, large)


def _onehots():
    oha = np.zeros((33, RA), np.float32)
    r = np.arange(RA) - 511
    bk = _bucket(r)
    for i in range(RA):
        if r[i] < 0:
            oha[32, i] = 1.0
        else:
            oha[bk[i], i] = 1.0
    ohb = np.zeros((33, 3 * RB), np.float32)
    for di, d in enumerate(DILS):
        dl = np.arange(RB) - 127
        bk = _bucket(dl * d)
        for i in range(RB):
            if 0 <= dl[i] <= 128:
                ohb[bk[i], di * RB + i] = 1.0
            else:
                ohb[32, di * RB + i] = 1.0
    return oha, ohb


def build_nc():
    nc = bass.Bass("TRN2", target_bir_lowering=False)
    dt = nc.dram_tensor
    xT_d = dt("xT", [D, S_TOK], F32, kind="ExternalInput").ap()
    xtok_d = dt("xtok", [S_TOK, D], F32, kind="ExternalInput").ap()
    win_d = dt("win", [NCHUNK, 128, 1024], F32, kind="ExternalInput").ap()
    pa_d = dt("pa", [8, 128, 1024], F32, kind="ExternalInput").ap()
    pb_d = dt("pb", [8, 128, 1024], F32, kind="ExternalInput").ap()
    wout_d = dt("wout", [D, D], F32, kind="ExternalInput").ap()
    bg_d = dt("bg", [128, 16], F32, kind="ExternalInput").ap()
    lam_d = dt("lam", [4, 64], F32, kind="ExternalInput").ap()
    subw_d = dt("subw", [128, 1], F32, kind="ExternalInput").ap()
    relb_d = dt("relb", [32, 24], F32, kind="ExternalInput").ap()
    lng_d = dt("lng", [1, D], F32, kind="ExternalInput").ap()
    lnb_d = dt("lnb", [1, D], F32, kind="ExternalInput").ap()
    oha_d = dt("oha", [33, RA], F32, kind="ExternalInput").ap()
    ohb_d = dt("ohb", [33, 3 * RB], F32, kind="ExternalInput").ap()
    ident_d = dt("ident", [128, 128], F32, kind="ExternalInput").ap()
    jmat_d = dt("jmat", [128, 128], F32, kind="ExternalInput").ap()
    out_d = dt("out", [S_TOK, D], F32, kind="ExternalOutput").ap()
    sa_d = dt("sa_scr", [8, RA], BF16, kind="Internal").ap()
    sb_d = dt("sb_scr", [16, 3 * RB], BF16, kind="Internal").ap()
    if DEBUG:
        dbg_a = dt("dbg_a", [128, 8 * S_TOK], F32, kind="ExternalOutput").ap()
        dbg_b = dt("dbg_b", [128, 8 * S_TOK], F32, kind="ExternalOutput").ap()
        dbg_m = dt("dbg_m", [128, 8 * S_TOK], BF16, kind="ExternalOutput").ap()

    S = Sched(nc)
    pe, act, dve, pool = nc.tensor, nc.scalar, nc.vector, nc.gpsimd

    es = ExitStack()

    def sb(stack, name, shape, dtype):
        return stack.enter_context(nc.sbuf_tensor(name, shape, dtype))

    xT = sb(es, "xT_sb", [128, 8, S_TOK], BF16)
    yag = sb(es, "yag", [128, 8, S_TOK], BF16)
    ybg = sb(es, "ybg", [128, 8, S_TOK], BF16)
    ident = sb(es, "ident_sb", [128, 128], BF16)
    jmat = sb(es, "jmat_sb", [128, 128], BF16)
    ones_bf = sb(es, "ones_bf", [128, 128], BF16)
    ones_f = sb(es, "ones_f", [128, 128], F32)
    neglam = sb(es, "neglam", [128, 1], F32)
    subw = sb(es, "subw_sb", [128, 1], F32)
    bg = sb(es, "bg_sb", [128, 16], F32)
    smallf = sb(es, "smallf", [128, 16], F32)
    ps = es.enter_context(nc.psum_tensor("ps", [128, 4096], F32))
    psb = ps[:].bitcast(BF16)

    def bank(i, rows=slice(0, 128), c0=0, c1=512):
        return ps[rows, i * 512 + c0:i * 512 + c1]

    def bankbf(i):
        return psb[:, i * 1024:(i + 1) * 1024]

    def pn(i):
        return f"ps{i}"

    act_state = {"f": None}
    warm = sb(es, "act_warm", [128, 4], F32)
    S.op("dve", lambda: dve.memset(warm[:], 1.0), writes=["warm_in"])

    def act_fn(func):
        if act_state["f"] is not func:
            act_state["f"] = func
            S.op("act", lambda: act.activation(out=warm[:, 2:3], in_=warm[:, 0:1], func=func), reads=["warm_in"],
                 writes=["warm_out"])
        return func

    for kc in range(8):
        S.dma("pool", xT[:, kc, :], xT_d[kc * 128:(kc + 1) * 128, :], writes=[f"xT{kc}"])
    XT_NAMES = [f"xT{kc}" for kc in range(8)]
    S.dma("pool", ident[:], ident_d[:], writes=["ident"])
    S.dma("pool", jmat[:], jmat_d[:], writes=["jmat"])
    S.dma("sp", subw[:], subw_d[:], writes=["subw"])
    S.dma("sp", bg[:], bg_d[:], writes=["bg"])
    S.op("dve", lambda: dve.memset(ones_bf[:], 1.0), writes=["ones_bf"])
    S.op("dve", lambda: dve.memset(ones_f[:], 1.0), writes=["ones_f"])

    st_a = ExitStack()
    NW = 8
    wch = [sb(st_a, f"wch{i}", [128, 1024], BF16) for i in range(NW)]
    wstate = {"rr": 0}

    def load_chunk(src_ap):
        i = wstate["rr"]
        wstate["rr"] = (i + 1) % NW
        S.dma("pool", wch[i][:], src_ap, writes=[f"wch{i}"])
        return i

    qT = sb(st_a, "qT", [128, S_TOK], BF16)
    kT = sb(st_a, "kT", [128, S_TOK], BF16)
    aT = sb(st_a, "aT", [128, S_TOK], BF16)
    zT = sb(st_a, "zT", [128, S_TOK], BF16)
    NP = 12
    Pb = [sb(st_a, f"P{i}", [128, 512], BF16) for i in range(NP)]
    pstate = {"rr": 0}

    def next_p():
        i = pstate["rr"]
        pstate["rr"] = (i + 1) % NP
        return i

    tf = [sb(st_a, f"tf{i}", [128, 512], F32) for i in range(5)]
    sqb = sb(st_a, "sqb", [128, 512], BF16)

    st_s = ExitStack()
    tabx = sb(st_s, "tabx", [33, 24], F32)
    oha = sb(st_s, "oha_sb", [33, RA], F32)
    ohb = sb(st_s, "ohb_sb", [33, 3 * RB], F32)
    ega = sb(st_s, "ega", [24, RA], BF16)
    egb = sb(st_s, "egb", [24, 3 * RB], BF16)
    lamv = sb(st_s, "lamv", [128, 4, 64], F32)
    lamt = sb(st_s, "lamt", [128, 2, 64], F32)
    S.op("dve", lambda: dve.memset(tabx[32:33, :], NEG), writes=["tabx_m"])
    S.dma("sp", tabx[0:32, :], relb_d[:], writes=["tabx"])
    S.dma("sp", oha[:], oha_d[:], writes=["oha"])
    S.dma("sp", ohb[:], ohb_d[:], writes=["ohb"])
    S.dma("sp", lamv[:].rearrange("p a b -> p (a b)"),
          bass.AP(tensor=lam_d.tensor, offset=0, ap=[[0, 128], [1, 256]]), writes=["lamv"])
    ring = {"rr": 0}

    def rbank():
        i = ring["rr"]
        ring["rr"] = (i + 1) % 4
        return 4 + i

    def setup_g(oh, ohname, eg, egname, width):
        c = 0
        while c < width:
            n = min(512, width - c)
            bi = rbank()
            S.op("pe", lambda bi=bi, c=c, n=n: pe.matmul(bank(bi, slice(0, 24), 0, n), lhsT=tabx[0:33, :],
                                                         rhs=oh[0:33, c:c + n], start=True, stop=True),
                 reads=["tabx", "tabx_m", ohname], writes=[pn(bi)])
            S.op("act", lambda bi=bi, c=c, n=n: act.activation(out=eg[0:24, c:c + n], in_=bank(bi, slice(0, 24), 0, n),
                                                               func=AF.Exp),
                 reads=[pn(bi)], writes=[egname])
            c += n

    setup_g(oha, "oha", ega, "ega", RA)
    setup_g(ohb, "ohb", egb, "egb", 3 * RB)
    S.dma("sp", sa_d[:], ega[0:8, :], reads=["ega"], writes=["SA"])
    S.dma("sp", sb_d[:], egb[8:24, :], reads=["egb"], writes=["SB"])

    S.op("dve", lambda: dve.tensor_tensor(out=lamt[:, 0, :], in0=lamv[:, 0, :], in1=lamv[:, 1, :], op=ALU.mult),
         reads=["lamv"], writes=["lamt0"])
    S.op("dve", lambda: dve.tensor_tensor(out=lamt[:, 1, :], in0=lamv[:, 2, :], in1=lamv[:, 3, :], op=ALU.mult),
         reads=["lamv"], writes=["lamt1"])
    S.op("dve", lambda: dve.reduce_sum(out=smallf[:, 0:1], in_=lamt[:, 0, :], axis=AX.X), reads=["lamt0"], writes=["sm0"])
    S.op("dve", lambda: dve.reduce_sum(out=smallf[:, 1:2], in_=lamt[:, 1, :], axis=AX.X), reads=["lamt1"], writes=["sm1"])
    S.op("act", lambda: act.activation(out=smallf[:, 2:4], in_=smallf[:, 0:2], func=AF.Exp), reads=["sm0", "sm1"],
         writes=["sm23"])
    S.op("dve", lambda: dve.tensor_tensor(out=smallf[:, 4:5], in0=smallf[:, 3:4], in1=smallf[:, 2:3], op=ALU.subtract),
         reads=["sm23"], writes=["sm4"])
    S.op("dve", lambda: dve.tensor_scalar(out=neglam[:], in0=smallf[:, 4:5], scalar1=-LAM_INIT, scalar2=None, op0=ALU.add),
         reads=["sm4"], writes=["neglam"])
    S.op("dve", lambda: dve.tensor_scalar(out=smallf[:, 5:6], in0=subw[:], scalar1=1.0 - LAM_INIT, scalar2=None,
                                          op0=ALU.mult), reads=["subw"], writes=["subw_s"])
    S.op("dve", lambda: dve.memset(smallf[:, 6:7], SUBLN_EPS), writes=["eps_a"])
    S.op("dve", lambda: dve.memset(smallf[:, 7:8], LN_EPS), writes=["eps_l"])
    subw_s = smallf[:, 5:6]
    eps_a = smallf[:, 6:7]
    eps_l = smallf[:, 7:8]

    def proj(wi, dst, dname, kind):
        w = wch[wi]
        for tb in range(4):
            bi = rbank()

            def mm(bi=bi, tb=tb):
                ins = None
                for kc in range(8):
                    ins = pe.matmul(bank(bi), lhsT=w[:, kc * 128:(kc + 1) * 128], rhs=xT[:, kc, tb * 512:(tb + 1) * 512],
                                    start=(kc == 0), stop=(kc == 7))
                return ins

            S.op("pe", mm, reads=[f"wch{wi}"] + XT_NAMES, writes=[pn(bi)])
            o = dst[:, tb * 512:(tb + 1) * 512]
            if kind == "silu":
                S.op("act", lambda bi=bi, o=o: act.activation(out=o, in_=bank(bi), func=AF.Silu), reads=[pn(bi)],
                     writes=[f"{dname}{tb}"])
            elif kind == "act":
                S.op("act", lambda bi=bi, o=o: act.copy(out=o, in_=bank(bi)), reads=[pn(bi)], writes=[f"{dname}{tb}"])
            else:
                S.op("dve", lambda bi=bi, o=o: dve.tensor_copy(out=o, in_=bank(bi)), reads=[pn(bi)],
                     writes=[f"{dname}{tb}"])

    def names4(n):
        return [f"{n}{tb}" for tb in range(4)]

    st_pa = ExitStack()
    vtok = sb(st_pa, "vtok", [128, 16, 128], BF16)
    XT = sb(st_pa, "XTh", [128, TW], BF16)
    Ttab = sb(st_pa, "Ttab", [128, TW], BF16)

    def phaseA_prep(h):
        wq = load_chunk(win_d[h])
        wk = load_chunk(win_d[8 + h])
        wv = load_chunk(win_d[16 + h])
        wz = load_chunk(win_d[24 + h])
        S.dma("sp", XT[:], bass.AP(tensor=sa_d.tensor, offset=h * RA, ap=[[1, 128], [1, TW]]), reads=["SA"],
              writes=["XT"])
        return wq, wk, wv, wz

    def phaseA_proj(ws, hook=None):
        wq, wk, wv, wz = ws
        proj(wq, qT, "qT", "dve")
        proj(wk, kT, "kT", "dve")
        if hook is not None:
            hook()
        proj(wv, aT, "aT", "act")
        proj(wz, zT, "zT", "silu")
        for g in range(2):
            bi = rbank()

            def tr(bi=bi, g=g):
                ins = None
                for t8 in range(8):
                    ti = 8 * g + t8
                    ins = pe.transpose(bankbf(bi)[:, t8 * 128:(t8 + 1) * 128], aT[:, ti * 128:(ti + 1) * 128], ident[:])
                return ins

            S.op("pe", tr, reads=names4("aT") + ["ident"], writes=[pn(bi)])
            S.op("act", lambda bi=bi, g=g: act.copy(out=vtok[:, 8 * g:8 * g + 8, :].rearrange("p a b -> p (a b)"),
                                                   in_=bankbf(bi)), reads=[pn(bi)], writes=[f"vtok{g}"])
        c = 0
        while c < TW:
            n = min(512, TW - c)
            bi = rbank()
            S.op("pe", lambda bi=bi, c=c, n=n: pe.matmul(bank(bi, slice(0, 128), 0, n), lhsT=jmat[:], rhs=XT[:, c:c + n],
                                                         start=True, stop=True), reads=["jmat", "XT"], writes=[pn(bi)])
            S.op("dve", lambda bi=bi, c=c, n=n: dve.tensor_copy(out=Ttab[:, c:c + n], in_=bank(bi, slice(0, 128), 0, n)),
                 reads=[pn(bi)], writes=["Ttab"])
            c += n

    OB_, DB_ = (0, 1), (2, 3)
    mmc = {"n": 0}

    def phaseA_attn(h):
        for qb in range(4):
            nkt = 4 * qb + 4
            pending = []

            def flush(keep):
                while len(pending) > keep:
                    pending.pop(0)()

            for kt in range(nkt):
                j = kt - 4 * qb
                c0 = 128 * j if j > 0 else 0
                off = 128 * (4 * qb - kt) + 384
                for m in range(2):
                    rows = slice(64 * m, 64 * m + 64)
                    sbk = rbank()
                    S.op("pe", lambda sbk=sbk, rows=rows, kt=kt, c0=c0, qb=qb: pe.matmul(
                        bank(sbk, slice(0, 128), c0, 512), lhsT=kT[rows, kt * 128:(kt + 1) * 128],
                        rhs=qT[rows, qb * 512 + c0:(qb + 1) * 512], start=True, stop=True),
                        reads=names4("kT") + names4("qT"), writes=[pn(sbk)])
                    pi = next_p()
                    S.op("act", lambda sbk=sbk, pi=pi, c0=c0: act.activation(out=Pb[pi][:, c0:512],
                                                                             in_=bank(sbk, slice(0, 128), c0, 512),
                                                                             func=AF.Exp, scale=0.125),
                         reads=[pn(sbk)], writes=[f"P{pi}"])
                    mmc["n"] += 1
                    if True:
                        S.op("dve", lambda pi=pi, c0=c0, off=off: dve.tensor_tensor(
                            out=Pb[pi][:, c0:512], in0=Pb[pi][:, c0:512], in1=Ttab[:, off + c0:off + 512], op=ALU.mult),
                            reads=[f"P{pi}", "Ttab"], writes=[f"P{pi}"])
                    else:
                        S.op("pool", lambda pi=pi, c0=c0, off=off: pool.tensor_tensor(
                            out=Pb[pi][:, c0:512], in0=Pb[pi][:, c0:512], in1=Ttab[:, off + c0:off + 512], op=ALU.mult),
                            reads=[f"P{pi}", "Ttab"], writes=[f"P{pi}"])

                    def av(m=m, pi=pi, c0=c0, kt=kt, nkt=nkt):
                        def f():
                            pe.matmul(bank(OB_[m], slice(0, 128), c0, 512), lhsT=vtok[:, kt, :], rhs=Pb[pi][:, c0:512],
                                      start=(kt == 0), stop=(kt == nkt - 1))
                            return pe.matmul(bank(DB_[m], slice(0, 128), c0, 512), lhsT=ones_bf[:], rhs=Pb[pi][:, c0:512],
                                             start=(kt == 0), stop=(kt == nkt - 1))

                        S.op("pe", f, reads=[f"P{pi}", "vtok0", "vtok1", "ones_bf"], writes=[pn(OB_[m]), pn(DB_[m])])

                    pending.append(av)
                if kt % 2 == 1:
                    flush(4)
                if kt == 3:
                    run_pending_finA()
            flush(0)
            fin_part1()
            finA["p2"] = (lambda h=h, qb=qb: fin_part2(h, qb))

    finA = {"p2": None}

    def run_pending_finA():
        f = finA["p2"]
        if f is not None:
            finA["p2"] = None
            f()

    def fin_part1():
        r0, r1, o0, o1, tt = tf
        S.op("act", lambda: act.activation(out=r0[:], in_=bank(DB_[0]), func=AF.Ln), reads=[pn(DB_[0])], writes=["tf0"])
        S.op("act", lambda: act.activation(out=r1[:], in_=bank(DB_[1]), func=AF.Ln), reads=[pn(DB_[1])], writes=["tf1"])
        S.op("act", lambda: act.activation(out=r0[:], in_=r0[:], func=AF.Exp, scale=-1.0), reads=["tf0"], writes=["tf0"])
        S.op("act", lambda: act.activation(out=r1[:], in_=r1[:], func=AF.Exp, scale=-1.0), reads=["tf1"], writes=["tf1"])
        S.op("dve", lambda: dve.tensor_tensor(out=o0[:], in0=bank(OB_[0]), in1=r0[:], op=ALU.mult),
             reads=[pn(OB_[0]), "tf0"], writes=["tf2"])
        S.op("dve", lambda: dve.tensor_tensor(out=o1[:], in0=bank(OB_[1]), in1=r1[:], op=ALU.mult),
             reads=[pn(OB_[1]), "tf1"], writes=["tf3"])
        S.op("dve", lambda: dve.scalar_tensor_tensor(out=o0[:], in0=o1[:], scalar=neglam[:, 0:1], in1=o0[:],
                                                     op0=ALU.mult, op1=ALU.add),
             reads=["tf2", "tf3", "neglam"], writes=["tf2"])
        S.op("pool", lambda: pool.tensor_tensor(out=sqb[:], in0=o0[:], in1=o0[:], op=ALU.mult), reads=["tf2"],
             writes=["sqb"])

    def fin_part2(h, qb):
        r0, r1, o0, o1, tt = tf
        bi = rbank()
        S.op("pe", lambda bi=bi: pe.matmul(bank(bi), lhsT=ones_bf[:], rhs=sqb[:], start=True, stop=True),
             reads=["sqb", "ones_bf"], writes=[pn(bi)])
        S.op("act", lambda bi=bi: act.activation(out=r0[:], in_=bank(bi), func=AF.Ln, bias=eps_a, scale=1.0 / 128.0),
             reads=[pn(bi), "eps_a", "tf0"], writes=["tf0"])
        S.op("act", lambda: act.activation(out=r1[:], in_=r0[:], func=AF.Exp, scale=-0.5), reads=["tf0", "tf1"],
             writes=["tf1"])
        S.op("dve", lambda: dve.tensor_tensor(out=tt[:], in0=o0[:], in1=r1[:], op=ALU.mult), reads=["tf2", "tf1"],
             writes=["tf4"])
        S.op("dve", lambda h=h, qb=qb: dve.scalar_tensor_tensor(
            out=yag[:, h, qb * 512:(qb + 1) * 512], in0=tt[:], scalar=subw_s, in1=zT[:, qb * 512:(qb + 1) * 512],
            op0=ALU.mult, op1=ALU.mult), reads=["tf4", "subw_s", f"zT{qb}"], writes=[f"yag{h}_{qb}"])

    ws = phaseA_prep(0)
    for h in range(8):
        phaseA_proj(ws, run_pending_finA)
        if h + 1 < 8:
            ws = phaseA_prep(h + 1)
        phaseA_attn(h)
    run_pending_finA()

    S.barrier()
    st_pa.close()
    st_s.close()

    st_pb = ExitStack()
    q4 = sb(st_pb, "q4", [128, S_TOK], BF16)
    k4 = sb(st_pb, "k4", [128, S_TOK], BF16)
    q16 = sb(st_pb, "q16", [128, S_TOK], BF16)
    k16 = sb(st_pb, "k16", [128, S_TOK], BF16)
    Vd = sb(st_pb, "Vd", [128, 3, 16, 193], BF16)
    XB = sb(st_pb, "XB", [128, 6 * 256], BF16)
    Btab = sb(st_pb, "Btab", [128, 6, 256], BF16)
    accs = sb(st_pb, "accs", [128, S_TOK], F32)
    zr = [sb(st_pb, f"zr{i}", [128, 512], F32) for i in range(2)]

    S.op("pool", lambda: pool.memset(Vd[:].rearrange("p a b c -> p (a b c)"), 0.0), writes=["Vd"])
    S.op("pool", lambda: pool.memset(Vd[:, :, :, 64:66], 1.0), reads=[], writes=["Vd"])

    def phaseB_prep(j):
        wq = load_chunk(win_d[32 + j])
        wk = load_chunk(win_d[40 + j])
        wv = load_chunk(win_d[48 + j])
        wz = load_chunk(win_d[56 + j])
        S.dma("sp", XB[:].rearrange("p (a b c) -> p a b c", a=2, b=3),
              bass.AP(tensor=sb_d.tensor, offset=2 * j * 3 * RB, ap=[[1, 128], [3 * RB, 2], [RB, 3], [1, 256]]),
              reads=["SB"], writes=["XB"])
        return wq, wk, wv, wz

    qperm = {1: qT, 4: q4, 16: q16}
    kperm = {1: kT, 4: k4, 16: k16}
    sring = {"rr": 0}

    def phaseB_proj(ws, hook=None):
        wq, wk, wv, wz = ws
        proj(wq, qT, "qT", "dve")
        proj(wk, kT, "kT", "dve")
        if hook is not None:
            hook()
        proj(wv, aT, "aT", "act")
        proj(wz, zT, "zT", "silu")
        for d, qd, kd in ((4, q4, k4), (16, q16, k16)):
            if d == 4:
                S.op("dve", lambda d=d, qd=qd: dve.tensor_copy(out=qd[:].rearrange("p (r l) -> p r l", r=d),
                                                              in_=qT[:].rearrange("p (l r) -> p r l", r=d)),
                     reads=names4("qT"), writes=[f"q{d}"])
                S.op("dve", lambda d=d, kd=kd: dve.tensor_copy(out=kd[:].rearrange("p (r l) -> p r l", r=d),
                                                              in_=kT[:].rearrange("p (l r) -> p r l", r=d)),
                     reads=names4("kT"), writes=[f"k{d}"])
            else:
                S.op("act", lambda d=d, qd=qd: act.copy(out=qd[:].rearrange("p (r l) -> p r l", r=d),
                                                       in_=qT[:].rearrange("p (l r) -> p r l", r=d)),
                     reads=names4("qT"), writes=[f"q{d}"])
                S.op("act", lambda d=d, kd=kd: act.copy(out=kd[:].rearrange("p (r l) -> p r l", r=d),
                                                       in_=kT[:].rearrange("p (l r) -> p r l", r=d)),
                     reads=names4("kT"), writes=[f"k{d}"])
        for di, d in enumerate(DILS):
            av = aT[:].rearrange("p (l r) -> p r l", r=d)
            nb = (S_TOK // d) // 128
            for g in range(2):
                bi = rbank()

                def tr(bi=bi, g=g, av=av, nb=nb):
                    ins = None
                    for t8 in range(8):
                        ti = 8 * g + t8
                        r, n = ti // nb, ti % nb
                        ins = pe.transpose(bankbf(bi)[:, t8 * 128:(t8 + 1) * 128], av[:, r, n * 128:(n + 1) * 128],
                                           ident[:])
                    return ins

                S.op("pe", tr, reads=names4("aT") + ["ident"], writes=[pn(bi)])
                src = bankbf(bi).rearrange("p (a b) -> p a b", a=8)
                S.op("act", lambda bi=bi, g=g, di=di, src=src: act.copy(out=Vd[:, di, 8 * g:8 * g + 8, 0:64],
                                                                       in_=src[:, :, 0:64]),
                     reads=[pn(bi)], writes=["Vd"])
                S.op("dve", lambda bi=bi, g=g, di=di, src=src: dve.tensor_copy(out=Vd[:, di, 8 * g:8 * g + 8, 129:193],
                                                                              in_=src[:, :, 64:128]),
                     reads=[pn(bi)], writes=["Vd"])
        for c in range(3):
            bi = rbank()
            S.op("pe", lambda bi=bi, c=c: pe.matmul(bank(bi), lhsT=jmat[:], rhs=XB[:, c * 512:(c + 1) * 512], start=True,
                                                    stop=True), reads=["jmat", "XB"], writes=[pn(bi)])
            S.op("dve", lambda bi=bi, c=c: dve.tensor_copy(out=Btab[:, 2 * c:2 * c + 2, :].rearrange("p a b -> p (a b)"),
                                                          in_=bank(bi)), reads=[pn(bi)], writes=["Btab"])

    def phaseB_attn(j):
        for hh in range(2):
            rows = slice(64 * hh, 64 * hh + 64)
            started = set()
            tiles = []
            for di, d in enumerate(DILS):
                nb = (S_TOK // d) // 128
                for r in range(d):
                    for n in range(nb):
                        tiles.append((di, d, nb, r, n))
            pending = []

            def flush(keep):
                while len(pending) > keep:
                    pending.pop(0)()

            for gi in range(0, len(tiles), 2):
                grp = tiles[gi:gi + 2]
                di, d = grp[0][0], grp[0][1]
                sbk = rbank()

                def qk(grp=grp, sbk=sbk, rows=rows):
                    ins = None
                    for s, (di_, d_, nb, r, n) in enumerate(grp):
                        nq = 256 if n + 1 < nb else 128
                        col = (r * nb + n) * 128
                        ins = pe.matmul(bank(sbk, slice(0, 128), s * 256, s * 256 + nq),
                                        lhsT=kperm[d_][rows, col:col + 128], rhs=qperm[d_][rows, col:col + nq],
                                        start=True, stop=True)
                    return ins

                qn = names4("qT") if d == 1 else [f"q{d}"]
                kn = names4("kT") if d == 1 else [f"k{d}"]
                S.op("pe", qk, reads=qn + kn, writes=[pn(sbk)])
                pi = next_p()
                S.op("act", lambda pi=pi, sbk=sbk: act.activation(out=Pb[pi][:], in_=bank(sbk), func=AF.Exp, scale=0.125),
                     reads=[pn(sbk)], writes=[f"P{pi}"])
                bsl = Btab[:, hh * 3 + di, :]
                bap = bass.AP(tensor=bsl.tensor, offset=bsl.offset, ap=[list(bsl.ap[0]), [0, 2], [1, 256]])
                pv = Pb[pi][:].rearrange("p (a b) -> p a b", a=2)
                if True:
                    S.op("dve", lambda pv=pv, bap=bap: dve.tensor_tensor(out=pv, in0=pv, in1=bap, op=ALU.mult),
                         reads=[f"P{pi}", "Btab"], writes=[f"P{pi}"])
                else:
                    S.op("pool", lambda pv=pv, bap=bap: pool.tensor_tensor(out=pv, in0=pv, in1=bap, op=ALU.mult),
                         reads=[f"P{pi}", "Btab"], writes=[f"P{pi}"])

                def av(grp=grp, pi=pi, hh=hh):
                    wr = set()
                    plan = []
                    mrows = slice(0, 128)
                    for s, (di_, d_, nb, r, n) in enumerate(grp):
                        ti = r * nb + n
                        lhs = Vd[:, di_, ti, 0:128] if hh == 0 else Vd[:, di_, ti, 65:193]
                        for half in range(2):
                            nn = n + half
                            if nn >= nb:
                                continue
                            pc0 = s * 256 + half * 128
                            if d_ == 1:
                                b_ = nn // 4
                                plan.append((b_, bank(b_, mrows, (nn % 4) * 128, (nn % 4) * 128 + 128), lhs,
                                             Pb[pi][:, pc0:pc0 + 128]))
                            elif d_ == 4:
                                b_ = nn
                                o = bank(b_, mrows).rearrange("p (j d) -> p d j", d=4)[:, r, :]
                                plan.append((b_, o, lhs, Pb[pi][:, pc0:pc0 + 128]))
                            else:
                                for b_ in range(4):
                                    o = bank(b_, mrows).rearrange("p (j d) -> p d j", d=16)[:, r, :]
                                    plan.append((b_, o, lhs, Pb[pi][:, pc0 + 32 * b_:pc0 + 32 * b_ + 32]))
                    for b_, *_ in plan:
                        wr.add(b_)

                    def f():
                        ins = None
                        for b_, o, lhs, rhs in plan:
                            st = b_ not in started
                            started.add(b_)
                            ins = pe.matmul(o, lhsT=lhs, rhs=rhs, start=st, stop=False, skip_group_check=True)
                        return ins

                    S.op("pe", f, reads=[f"P{pi}", "Vd"], writes=[pn(b_) for b_ in sorted(wr)])

                pending.append(av)
                if (gi // 2) % 2 == 1:
                    flush(3)
                if gi == 6:
                    run_pending_finB()
            flush(0)
            finB_part1(hh)
            finB["p2"] = (lambda j=j, hh=hh: finB_part2(j, hh))

    finB = {"p2": None}

    def run_pending_finB():
        f = finB["p2"]
        if f is not None:
            finB["p2"] = None
            f()

    def finB_part1(hh):
        mrows = slice(0, 65) if hh == 0 else slice(0, 128)
        for pc in range(4):
            csl = slice(pc * 512, (pc + 1) * 512)
            if pc % 2 == 0:
                S.op("act", lambda pc=pc, csl=csl, mrows=mrows: act.copy(out=accs[mrows, csl], in_=bank(pc, mrows)),
                     reads=[pn(pc)], writes=[f"accs{pc}"])
            else:
                S.op("dve", lambda pc=pc, csl=csl, mrows=mrows: dve.tensor_copy(out=accs[mrows, csl], in_=bank(pc, mrows)),
                     reads=[pn(pc)], writes=[f"accs{pc}"])
        row = 64 if hh == 0 else 0
        rsl = slice(row, row + 1)
        an = [f"accs{pc}" for pc in range(4)]
        S.op("act", lambda rsl=rsl: act.activation(out=accs[rsl, :], in_=accs[rsl, :], func=AF.Ln), reads=an, writes=an)
        S.op("act", lambda rsl=rsl: act.activation(out=accs[rsl, :], in_=accs[rsl, :], func=AF.Exp, scale=-1.0), reads=an,
             writes=an)

    def finB_part2(j, hh):
        rows = slice(64 * hh, 64 * hh + 64)
        row = 64 if hh == 0 else 0
        rsl = slice(row, row + 1)
        mm_rows = slice(0, 64) if hh == 0 else slice(0, 128)
        mcols = 64 if hh == 0 else 128
        for pc in range(4):
            csl = slice(pc * 512, (pc + 1) * 512)
            bi = rbank()
            S.op("pe", lambda bi=bi, csl=csl: pe.matmul(bank(bi, mm_rows), lhsT=ones_f[rsl, 0:mcols], rhs=accs[rsl, csl],
                                                        start=True, stop=True),
                 reads=[f"accs{pc}", "ones_f"], writes=[pn(bi)])
            k2 = pc % 2
            S.op("dve", lambda bi=bi, k2=k2, csl=csl: dve.tensor_tensor(out=zr[k2][rows, :], in0=bank(bi, rows),
                                                                       in1=accs[rows, csl], op=ALU.mult),
                 reads=[pn(bi), f"accs{pc}"], writes=[f"zr{k2}"])
            S.op("pool", lambda k2=k2, csl=csl: pool.tensor_tensor(out=ybg[rows, j, csl], in0=zr[k2][rows, :],
                                                                  in1=zT[rows, csl], op=ALU.mult),
                 reads=[f"zr{k2}", f"zT{pc}"], writes=[f"ybg{j}_{pc}_{hh}"])

    ws = phaseB_prep(0)
    for j in range(8):
        phaseB_proj(ws, run_pending_finB)
        if j + 1 < 8:
            ws = phaseB_prep(j + 1)
        phaseB_attn(j)
    run_pending_finB()

    S.barrier()
    st_pb.close()
    st_a.close()

    if DEBUG:
        st_d = ExitStack()
        dtmp = sb(st_d, "dtmp", [128, 8 * S_TOK], F32)
        S.op("dve", lambda: dve.tensor_copy(out=dtmp[:], in_=yag[:].rearrange("p a b -> p (a b)")), writes=["dtmp"])
        S.dma("sp", dbg_a[:], dtmp[:], reads=["dtmp"], writes=["dbg_a"])
        S.op("dve", lambda: dve.tensor_copy(out=dtmp[:], in_=ybg[:].rearrange("p a b -> p (a b)")), reads=["dbg_a"],
             writes=["dtmp"])
        S.dma("sp", dbg_b[:], dtmp[:], reads=["dtmp"], writes=["dbg_b"])
        S.barrier()
        st_d.close()

    st_c = ExitStack()
    merged = sb(st_c, "merged", [128, 8, S_TOK], BF16)
    wout = sb(st_c, "wout_sb", [128, 8, D], BF16)
    NWC = 8
    wc = [sb(st_c, f"wc{i}", [128, 1024], BF16) for i in range(NWC)]
    sg = [sb(st_c, f"sg{i}", [128, 512], F32) for i in range(4)]
    t12 = [sb(st_c, f"t12_{i}", [128, 512], F32) for i in range(4)]
    lng = sb(st_c, "lng_sb", [128, D], F32)
    lnb = sb(st_c, "lnb_sb", [128, D], F32)
    xtk = [sb(st_c, f"xtk{i}", [128, D], F32) for i in range(2)]
    yb_ = [sb(st_c, f"ybuf{i}", [128, D], F32) for i in range(2)]
    stats = sb(st_c, "stats", [128, 2, 6], F32)
    mv = sb(st_c, "mv", [128, 8], F32)

    S.dma("sp", lng[:], bass.AP(tensor=lng_d.tensor, offset=0, ap=[[0, 128], [1, D]]), writes=["lng"])
    S.dma("sp", lnb[:], bass.AP(tensor=lnb_d.tensor, offset=0, ap=[[0, 128], [1, D]]), writes=["lnb"])
    wcs = {"rr": 0}

    def load_c(src_ap):
        i = wcs["rr"]
        wcs["rr"] = (i + 1) % NWC
        S.dma("pool", wc[i][:], src_ap, writes=[f"wc{i}"])
        return i

    def prepC(dc):
        return (load_c(pa_d[dc]), load_c(pb_d[dc]), load_c(win_d[64 + dc]), load_c(win_d[72 + dc]))

    pring = {"rr": 0}

    def nbank():
        i = pring["rr"]
        pring["rr"] = (i + 1) % 8
        return i

    cw = prepC(0)
    it = 0
    for dc in range(8):
        wpa, wpb, wga, wgb = cw
        if dc + 1 < 8:
            cw = prepC(dc + 1)
        S.dma("pool", wout[:, dc, :], wout_d[dc * 128:(dc + 1) * 128, :], writes=[f"wout{dc}"])
        for tb in range(4):
            tsl = slice(tb * 512, (tb + 1) * 512)
            k2 = it % 2
            it += 1
            banks = {}
            for nm, wi, src, snames in (("ua", wpa, yag, [f"yag{kc}_{tb}" for kc in range(8)]),
                                        ("ga", wga, xT, XT_NAMES),
                                        ("ub", wpb, ybg, [f"ybg{kc}_{tb}_{hh}" for kc in range(8) for hh in range(2)]),
                                        ("gb", wgb, xT, XT_NAMES)):
                bi = nbank()
                banks[nm] = bi

                def mm(bi=bi, wi=wi, src=src, tsl=tsl):
                    ins = None
                    for kc in range(8):
                        ins = pe.matmul(bank(bi), lhsT=wc[wi][:, kc * 128:(kc + 1) * 128], rhs=src[:, kc, tsl],
                                        start=(kc == 0), stop=(kc == 7))
                    return ins

                S.op("pe", mm, reads=[f"wc{wi}"] + snames, writes=[pn(bi)])
            for gi, nm in enumerate(("ga", "gb")):
                bi = banks[nm]
                sgi = 2 * k2 + gi
                S.op("act", lambda bi=bi, sgi=sgi, gi=gi, dc=dc: act.activation(
                    out=sg[sgi][:], in_=bank(bi), func=AF.Sigmoid, bias=bg[:, gi * 8 + dc:gi * 8 + dc + 1]),
                    reads=[pn(bi), "bg"], writes=[f"sg{sgi}"])
            for gi, nm in enumerate(("ua", "ub")):
                bi = banks[nm]
                sgi = 2 * k2 + gi
                S.op("dve", lambda bi=bi, sgi=sgi: dve.tensor_tensor(out=t12[sgi][:], in0=bank(bi), in1=sg[sgi][:],
                                                                    op=ALU.mult),
                     reads=[pn(bi), f"sg{sgi}"], writes=[f"t12_{sgi}"])
            S.op("pool", lambda k2=k2, dc=dc, tsl=tsl: pool.tensor_tensor(out=merged[:, dc, tsl], in0=t12[2 * k2][:],
                                                                         in1=t12[2 * k2 + 1][:], op=ALU.add),
                 reads=[f"t12_{2 * k2}", f"t12_{2 * k2 + 1}"], writes=[f"mg{dc}_{tb}"])

    def out_stage1(tt):
        k2 = tt % 2
        tb = tt // 4
        S.dma("sp", xtk[k2][:], xtok_d[tt * 128:(tt + 1) * 128, :], writes=[f"xtk{k2}"])
        b0 = nbank()
        b1 = nbank()
        for half, bi in enumerate((b0, b1)):
            def mm(bi=bi, half=half, tt=tt):
                ins = None
                for kc in range(8):
                    ins = pe.matmul(bank(bi), lhsT=merged[:, kc, tt * 128:(tt + 1) * 128],
                                    rhs=wout[:, kc, half * 512:(half + 1) * 512], start=(kc == 0), stop=(kc == 7))
                return ins

            S.op("pe", mm, reads=[f"mg{kc}_{tb}" for kc in range(8)] + [f"wout{kc}" for kc in range(8)], writes=[pn(bi)])
        for half, bi in enumerate((b0, b1)):
            hs = slice(half * 512, (half + 1) * 512)
            S.op("dve", lambda bi=bi, hs=hs, k2=k2: dve.scalar_tensor_tensor(
                out=yb_[k2][:, hs], in0=xtk[k2][:, hs], scalar=ALPHA, in1=bank(bi), op0=ALU.mult, op1=ALU.add),
                reads=[pn(bi), f"xtk{k2}"], writes=[f"yb{k2}_{half}"])
            S.op("dve", lambda hs=hs, k2=k2, half=half: dve.bn_stats(out=stats[:, half, :], in_=yb_[k2][:, hs]),
                 reads=[f"yb{k2}_{half}"], writes=[f"stats{half}"])
        S.op("dve", lambda: dve.bn_aggr(out=mv[:, 0:2], in_=stats[:].rearrange("p a b -> p (a b)")),
             reads=["stats0", "stats1"], writes=["mv01"])
        S.op("act", lambda: act.activation(out=mv[:, 2:3], in_=mv[:, 1:2], func=AF.Ln, bias=eps_l), reads=["mv01", "eps_l"],
             writes=["mv2"])
        S.op("act", lambda: act.activation(out=mv[:, 3:4], in_=mv[:, 2:3], func=AF.Exp, scale=-0.5), reads=["mv2"],
             writes=["mv3"])
        ynames = [f"yb{k2}_0", f"yb{k2}_1"]
        S.op("dve", lambda: dve.tensor_scalar(out=mv[:, 4:5], in0=mv[:, 0:1], scalar1=mv[:, 3:4], scalar2=-1.0,
                                              op0=ALU.mult, op1=ALU.mult), reads=["mv01", "mv3"], writes=["mv4"])
        S.op("act", lambda k2=k2: act.activation(out=yb_[k2][:], in_=yb_[k2][:], func=AF.Identity, bias=mv[:, 4:5],
                                                 scale=mv[:, 3:4]), reads=ynames + ["mv3", "mv4"], writes=ynames)

    def out_stage2(tt):
        k2 = tt % 2
        ynames = [f"yb{k2}_0", f"yb{k2}_1"]
        S.op("dve", lambda k2=k2: dve.tensor_tensor(out=yb_[k2][:], in0=yb_[k2][:], in1=lng[:], op=ALU.mult),
             reads=ynames + ["lng"], writes=ynames)
        S.op("dve", lambda k2=k2: dve.tensor_tensor(out=yb_[k2][:], in0=yb_[k2][:], in1=lnb[:], op=ALU.add),
             reads=ynames + ["lnb"], writes=ynames)
        S.dma("sp", out_d[tt * 128:(tt + 1) * 128, :], yb_[k2][:], reads=ynames, writes=[f"out{tt}"])

    for tt in range(16):
        out_stage1(tt)
        if tt > 0:
            out_stage2(tt - 1)
    out_stage2(15)

    if DEBUG:
        S.barrier()
        S.dma("sp", dbg_m[:], merged[:].rearrange("p a b -> p (a b)"), writes=["dbg_m"])
    S.wait_all_dma("sp")
    S.barrier()
    st_c.close()
    es.close()
    return nc


_CACHE = {}


def _chunks(w, n):
    return np.ascontiguousarray(w.reshape(8, 128, n, 128).transpose(2, 1, 0, 3).reshape(n, 128, 1024))


def kernel(x, w_in, b_gate, w_proj_a, w_proj_b, w_out, da_lambda_q1, da_lambda_k1, da_lambda_q2, da_lambda_k2,
           da_subln_w, rel_bias, ln_g, ln_b):
    f = np.float32
    x = np.asarray(x, f)
    B = x.shape[0]
    if "nc" not in _CACHE:
        _CACHE["nc"] = build_nc()
        _CACHE["oh"] = _onehots()
    nc = _CACHE["nc"]
    oha, ohb = _CACHE["oh"]
    win_r = _chunks(np.asarray(w_in, f)[0], NCHUNK)
    pa_r = _chunks(np.asarray(w_proj_a, f)[0], 8)
    pb_r = _chunks(np.asarray(w_proj_b, f)[0], 8)
    wout = np.ascontiguousarray(np.asarray(w_out, f)[0])
    bg = np.ascontiguousarray(np.asarray(b_gate, f)[0].reshape(2, 8, 128).transpose(2, 0, 1).reshape(128, 16))
    lam = np.ascontiguousarray(np.concatenate([np.asarray(a, f).reshape(1, 64) for a in
                                               (da_lambda_q1, da_lambda_k1, da_lambda_q2, da_lambda_k2)], axis=0))
    subw = np.ascontiguousarray(np.asarray(da_subln_w, f).reshape(128, 1))
    relb = np.ascontiguousarray(np.asarray(rel_bias, f))
    lng = np.ascontiguousarray(np.asarray(ln_g, f).reshape(1, D))
    lnb = np.ascontiguousarray(np.asarray(ln_b, f).reshape(1, D))
    ident = np.eye(128, dtype=f)
    jmat = np.ascontiguousarray(np.eye(128, dtype=f)[::-1])
    shared = {"win": win_r, "pa": pa_r, "pb": pb_r, "wout": wout, "bg": bg, "lam": lam, "subw": subw, "relb": relb,
              "lng": lng, "lnb": lnb, "oha": oha, "ohb": ohb, "ident": ident, "jmat": jmat}
    in_maps = []
    for b in range(B):
        m = dict(shared)
        m["xT"] = np.ascontiguousarray(x[b].T)
        m["xtok"] = np.ascontiguousarray(x[b])
        in_maps.append(m)
    res = run_bass_kernel_spmd(nc, in_maps, core_ids=list(range(B)))
    _CACHE["last"] = res
    return np.stack([np.asarray(r["out"], f) for r in res.results], axis=0)
```

```python
import math
from contextlib import ExitStack

import numpy as np
import ml_dtypes

import concourse.bass as bass
import concourse.mybir as mybir
from concourse.bass_utils import run_bass_kernel_spmd

F32 = mybir.dt.float32
BF16 = mybir.dt.bfloat16
AF = mybir.ActivationFunctionType
ALU = mybir.AluOpType
AX = mybir.AxisListType

S_TOK = 2048
D = 1024
NCHUNK = 80
LAM_INIT = 0.2
ALPHA = 2.0 ** 0.25
SUBLN_EPS = 1e-5
LN_EPS = 1e-5
RA = 2560
TW = 2432
RB = 384
DILS = (1, 4, 16)
NEG = -30000.0

DEBUG = False


class _Buf:
    __slots__ = ("w", "r")

    def __init__(self):
        self.w = None
        self.r = {}


class Sched:
    def __init__(self, nc, n_dma=32):
        self.nc = nc
        self.engs = {"pe": nc.tensor, "act": nc.scalar, "dve": nc.vector, "pool": nc.gpsimd, "sp": nc.sync}
        self.sems = {}
        for e in ("pe", "act", "dve", "pool"):
            self.sems[e] = nc.alloc_semaphore(name=f"sem_{e}")
        self.n_dma = n_dma
        for i in range(n_dma):
            self.sems[("d", i)] = nc.alloc_semaphore(name=f"sem_dma{i}")
        self.dma_cnt = [0] * n_dma
        self.dma_rr = {"pool": 0, "sp": n_dma // 2}
        self.cnt = {e: 0 for e in ("pe", "act", "dve", "pool")}
        self.known = {e: {} for e in self.engs}
        self.bufs = {}
        self.n_waits = 0
        self.swdge_out = []
        self.swdge_limit = 6

    def _b(self, name):
        b = self.bufs.get(name)
        if b is None:
            b = self.bufs[name] = _Buf()
        return b

    def _deps(self, reads, writes):
        ev = {}

        def add(k, v):
            if ev.get(k, 0) < v:
                ev[k] = v

        for b in reads:
            if b.w is not None:
                add(*b.w)
        for b in writes:
            if b.w is not None:
                add(*b.w)
            for k, v in b.r.items():
                add(k, v)
        return ev

    def _wait(self, eng, ev):
        kn = self.known[eng]
        for k, v in ev.items():
            if eng == "pe" and k == "pe":
                continue
            if kn.get(k, 0) >= v:
                continue
            self.engs[eng].wait_ge(self.sems[k], v)
            kn[k] = v
            self.n_waits += 1

    def _commit(self, event, reads, writes):
        k, v = event
        for b in writes:
            b.w = event
            b.r = {}
        for b in reads:
            if b in writes:
                continue
            if b.r.get(k, 0) < v:
                b.r[k] = v

    def op(self, eng, fn, reads=(), writes=()):
        rb = [self._b(x) for x in reads]
        wb = [self._b(x) for x in writes]
        self._wait(eng, self._deps(rb, wb))
        ins = fn()
        self.cnt[eng] += 1
        ins.then_inc(self.sems[eng], 1)
        self._commit((eng, self.cnt[eng]), rb, wb)

    def dma(self, q, out, in_, reads=(), writes=()):
        rb = [self._b(x) for x in reads]
        wb = [self._b(x) for x in writes]
        self._wait(q, self._deps(rb, wb))
        if q == "pool" and len(self.swdge_out) >= self.swdge_limit:
            k0, v0 = self.swdge_out.pop(0)
            self._wait("pool", {k0: v0})
        half = self.n_dma // 2
        i = self.dma_rr[q]
        base = 0 if q == "pool" else half
        self.dma_rr[q] = base + (i - base + 1) % half
        ins = self.engs[q].dma_start(out=out, in_=in_)
        self.dma_cnt[i] += 16
        ins.then_inc(self.sems[("d", i)], 16)
        if q == "pool":
            self.swdge_out.append((("d", i), self.dma_cnt[i]))
        self._commit((("d", i), self.dma_cnt[i]), rb, wb)

    def barrier(self):
        ev = {e: self.cnt[e] for e in ("pe", "act", "dve", "pool") if self.cnt[e] > 0}
        for i in range(self.n_dma):
            if self.dma_cnt[i] > 0:
                ev[("d", i)] = self.dma_cnt[i]
        for e in self.engs:
            self._wait(e, ev)

    def wait_all_dma(self, eng):
        ev = {("d", i): self.dma_cnt[i] for i in range(self.n_dma) if self.dma_cnt[i] > 0}
        self._wait(eng, ev)


def _bucket(d):
    d = np.maximum(np.asarray(d), 0).astype(np.int32)
    distf = np.maximum(d, 1).astype(np.float32)
    large = 16 + (np.log(distf / np.float32(16)) / np.float32(math.log(2048 / 16)) * np.float32(16)).astype(np.int32)
    large = np.minimum(large, 31)
    return np.where(d < 16, d, large)


def _onehots():
    oha = np.zeros((33, RA), np.float32)
    r = np.arange(RA) - 511
    bk = _bucket(r)
    for i in range(RA):
        if r[i] < 0:
            oha[32, i] = 1.0
        else:
            oha[bk[i], i] = 1.0
    ohb = np.zeros((33, 3 * RB), np.float32)
    for di, d in enumerate(DILS):
        dl = np.arange(RB) - 127
        bk = _bucket(dl * d)
        for i in range(RB):
            if 0 <= dl[i] <= 128:
                ohb[bk[i], di * RB + i] = 1.0
            else:
                ohb[32, di * RB + i] = 1.0
    return oha, ohb


def build_nc():
    nc = bass.Bass("TRN2", target_bir_lowering=False)
    dt = nc.dram_tensor
    xT_d = dt("xT", [D, S_TOK], F32, kind="ExternalInput").ap()
    xtok_d = dt("xtok", [S_TOK, D], F32, kind="ExternalInput").ap()
    win_d = dt("win", [NCHUNK, 128, 1024], F32, kind="ExternalInput").ap()
    pa_d = dt("pa", [8, 128, 1024], F32, kind="ExternalInput").ap()
    pb_d = dt("pb", [8, 128, 1024], F32, kind="ExternalInput").ap()
    wout_d = dt("wout", [D, D], F32, kind="ExternalInput").ap()
    bg_d = dt("bg", [128, 16], F32, kind="ExternalInput").ap()
    lam_d = dt("lam", [4, 64], F32, kind="ExternalInput").ap()
    subw_d = dt("subw", [128, 1], F32, kind="ExternalInput").ap()
    relb_d = dt("relb", [32, 24], F32, kind="ExternalInput").ap()
    lng_d = dt("lng", [1, D], F32, kind="ExternalInput").ap()
    lnb_d = dt("lnb", [1, D], F32, kind="ExternalInput").ap()
    oha_d = dt("oha", [33, RA], F32, kind="ExternalInput").ap()
    ohb_d = dt("ohb", [33, 3 * RB], F32, kind="ExternalInput").ap()
    ident_d = dt("ident", [128, 128], F32, kind="ExternalInput").ap()
    jmat_d = dt("jmat", [128, 128], F32, kind="ExternalInput").ap()
    out_d = dt("out", [S_TOK, D], F32, kind="ExternalOutput").ap()
    sa_d = dt("sa_scr", [8, RA], BF16, kind="Internal").ap()
    sb_d = dt("sb_scr", [16, 3 * RB], BF16, kind="Internal").ap()
    if DEBUG:
        dbg_a = dt("dbg_a", [128, 8 * S_TOK], F32, kind="ExternalOutput").ap()
        dbg_b = dt("dbg_b", [128, 8 * S_TOK], F32, kind="ExternalOutput").ap()
        dbg_m = dt("dbg_m", [128, 8 * S_TOK], BF16, kind="ExternalOutput").ap()

    S = Sched(nc)
    pe, act, dve, pool = nc.tensor, nc.scalar, nc.vector, nc.gpsimd

    es = ExitStack()

    def sb(stack, name, shape, dtype):
        return stack.enter_context(nc.sbuf_tensor(name, shape, dtype))

    xT = sb(es, "xT_sb", [128, 8, S_TOK], BF16)
    yag = sb(es, "yag", [128, 8, S_TOK], BF16)
    ybg = sb(es, "ybg", [128, 8, S_TOK], BF16)
    ident = sb(es, "ident_sb", [128, 128], BF16)
    jmat = sb(es, "jmat_sb", [128, 128], BF16)
    ones_bf = sb(es, "ones_bf", [128, 128], BF16)
    ones_f = sb(es, "ones_f", [128, 128], F32)
    neglam = sb(es, "neglam", [128, 1], F32)
    subw = sb(es, "subw_sb", [128, 1], F32)
    bg = sb(es, "bg_sb", [128, 16], F32)
    smallf = sb(es, "smallf", [128, 16], F32)
    ps = es.enter_context(nc.psum_tensor("ps", [128, 4096], F32))
    psb = ps[:].bitcast(BF16)

    def bank(i, rows=slice(0, 128), c0=0, c1=512):
        return ps[rows, i * 512 + c0:i * 512 + c1]

    def bankbf(i):
        return psb[:, i * 1024:(i + 1) * 1024]

    def pn(i):
        return f"ps{i}"

    act_state = {"f": None}
    warm = sb(es, "act_warm", [128, 4], F32)
    S.op("dve", lambda: dve.memset(warm[:], 1.0), writes=["warm_in"])

    def act_fn(func):
        if act_state["f"] is not func:
            act_state["f"] = func
            S.op("act", lambda: act.activation(out=warm[:, 2:3], in_=warm[:, 0:1], func=func), reads=["warm_in"],
                 writes=["warm_out"])
        return func

    for kc in range(8):
        S.dma("pool", xT[:, kc, :], xT_d[kc * 128:(kc + 1) * 128, :], writes=[f"xT{kc}"])
    XT_NAMES = [f"xT{kc}" for kc in range(8)]
    S.dma("pool", ident[:], ident_d[:], writes=["ident"])
    S.dma("pool", jmat[:], jmat_d[:], writes=["jmat"])
    S.dma("sp", subw[:], subw_d[:], writes=["subw"])
    S.dma("sp", bg[:], bg_d[:], writes=["bg"])
    S.op("dve", lambda: dve.memset(ones_bf[:], 1.0), writes=["ones_bf"])
    S.op("dve", lambda: dve.memset(ones_f[:], 1.0), writes=["ones_f"])

    st_a = ExitStack()
    NW = 8
    wch = [sb(st_a, f"wch{i}", [128, 1024], BF16) for i in range(NW)]
    wstate = {"rr": 0}

    def load_chunk(src_ap):
        i = wstate["rr"]
        wstate["rr"] = (i + 1) % NW
        S.dma("pool", wch[i][:], src_ap, writes=[f"wch{i}"])
        return i

    qT = sb(st_a, "qT", [128, S_TOK], BF16)
    kT = sb(st_a, "kT", [128, S_TOK], BF16)
    aT = sb(st_a, "aT", [128, S_TOK], BF16)
    zT = sb(st_a, "zT", [128, S_TOK], BF16)
    NP = 14
    Pb = [sb(st_a, f"P{i}", [128, 512], BF16) for i in range(NP)]
    pstate = {"rr": 0}

    def next_p():
        i = pstate["rr"]
        pstate["rr"] = (i + 1) % NP
        return i

    tf = [sb(st_a, f"tf{i}", [128, 512], F32) for i in range(5)]
    sqb = sb(st_a, "sqb", [128, 512], BF16)

    st_s = ExitStack()
    tabx = sb(st_s, "tabx", [33, 24], F32)
    oha = sb(st_s, "oha_sb", [33, RA], F32)
    ohb = sb(st_s, "ohb_sb", [33, 3 * RB], F32)
    ega = sb(st_s, "ega", [24, RA], BF16)
    egb = sb(st_s, "egb", [24, 3 * RB], BF16)
    lamv = sb(st_s, "lamv", [128, 4, 64], F32)
    lamt = sb(st_s, "lamt", [128, 2, 64], F32)
    S.op("dve", lambda: dve.memset(tabx[32:33, :], NEG), writes=["tabx_m"])
    S.dma("sp", tabx[0:32, :], relb_d[:], writes=["tabx"])
    S.dma("sp", oha[:], oha_d[:], writes=["oha"])
    S.dma("sp", ohb[:], ohb_d[:], writes=["ohb"])
    S.dma("sp", lamv[:].rearrange("p a b -> p (a b)"),
          bass.AP(tensor=lam_d.tensor, offset=0, ap=[[0, 128], [1, 256]]), writes=["lamv"])
    ring = {"rr": 0}

    def rbank():
        i = ring["rr"]
        ring["rr"] = (i + 1) % 4
        return 4 + i

    def setup_g(oh, ohname, eg, egname, width):
        c = 0
        while c < width:
            n = min(512, width - c)
            bi = rbank()
            S.op("pe", lambda bi=bi, c=c, n=n: pe.matmul(bank(bi, slice(0, 24), 0, n), lhsT=tabx[0:33, :],
                                                         rhs=oh[0:33, c:c + n], start=True, stop=True),
                 reads=["tabx", "tabx_m", ohname], writes=[pn(bi)])
            S.op("act", lambda bi=bi, c=c, n=n: act.activation(out=eg[0:24, c:c + n], in_=bank(bi, slice(0, 24), 0, n),
                                                               func=AF.Exp),
                 reads=[pn(bi)], writes=[egname])
            c += n

    setup_g(oha, "oha", ega, "ega", RA)
    setup_g(ohb, "ohb", egb, "egb", 3 * RB)
    S.dma("sp", sa_d[:], ega[0:8, :], reads=["ega"], writes=["SA"])
    S.dma("sp", sb_d[:], egb[8:24, :], reads=["egb"], writes=["SB"])

    S.op("dve", lambda: dve.tensor_tensor(out=lamt[:, 0, :], in0=lamv[:, 0, :], in1=lamv[:, 1, :], op=ALU.mult),
         reads=["lamv"], writes=["lamt0"])
    S.op("dve", lambda: dve.tensor_tensor(out=lamt[:, 1, :], in0=lamv[:, 2, :], in1=lamv[:, 3, :], op=ALU.mult),
         reads=["lamv"], writes=["lamt1"])
    S.op("dve", lambda: dve.reduce_sum(out=smallf[:, 0:1], in_=lamt[:, 0, :], axis=AX.X), reads=["lamt0"], writes=["sm0"])
    S.op("dve", lambda: dve.reduce_sum(out=smallf[:, 1:2], in_=lamt[:, 1, :], axis=AX.X), reads=["lamt1"], writes=["sm1"])
    S.op("act", lambda: act.activation(out=smallf[:, 2:4], in_=smallf[:, 0:2], func=AF.Exp), reads=["sm0", "sm1"],
         writes=["sm23"])
    S.op("dve", lambda: dve.tensor_tensor(out=smallf[:, 4:5], in0=smallf[:, 3:4], in1=smallf[:, 2:3], op=ALU.subtract),
         reads=["sm23"], writes=["sm4"])
    S.op("dve", lambda: dve.tensor_scalar(out=neglam[:], in0=smallf[:, 4:5], scalar1=-LAM_INIT, scalar2=None, op0=ALU.add),
         reads=["sm4"], writes=["neglam"])
    S.op("dve", lambda: dve.tensor_scalar(out=smallf[:, 5:6], in0=subw[:], scalar1=1.0 - LAM_INIT, scalar2=None,
                                          op0=ALU.mult), reads=["subw"], writes=["subw_s"])
    S.op("dve", lambda: dve.memset(smallf[:, 6:7], SUBLN_EPS), writes=["eps_a"])
    S.op("dve", lambda: dve.memset(smallf[:, 7:8], LN_EPS), writes=["eps_l"])
    subw_s = smallf[:, 5:6]
    eps_a = smallf[:, 6:7]
    eps_l = smallf[:, 7:8]

    def proj(wi, dst, dname, kind):
        w = wch[wi]
        for tb in range(4):
            bi = rbank()

            def mm(bi=bi, tb=tb):
                ins = None
                for kc in range(8):
                    ins = pe.matmul(bank(bi), lhsT=w[:, kc * 128:(kc + 1) * 128], rhs=xT[:, kc, tb * 512:(tb + 1) * 512],
                                    start=(kc == 0), stop=(kc == 7))
                return ins

            S.op("pe", mm, reads=[f"wch{wi}"] + XT_NAMES, writes=[pn(bi)])
            o = dst[:, tb * 512:(tb + 1) * 512]
            if kind == "silu":
                S.op("act", lambda bi=bi, o=o: act.activation(out=o, in_=bank(bi), func=AF.Silu), reads=[pn(bi)],
                     writes=[f"{dname}{tb}"])
            elif kind == "act":
                S.op("act", lambda bi=bi, o=o: act.copy(out=o, in_=bank(bi)), reads=[pn(bi)], writes=[f"{dname}{tb}"])
            else:
                S.op("dve", lambda bi=bi, o=o: dve.tensor_copy(out=o, in_=bank(bi)), reads=[pn(bi)],
                     writes=[f"{dname}{tb}"])

    def names4(n):
        return [f"{n}{tb}" for tb in range(4)]

    st_pa = ExitStack()
    vtok = sb(st_pa, "vtok", [128, 16, 128], BF16)
    XT = sb(st_pa, "XTh", [128, TW], BF16)
    Ttab = sb(st_pa, "Ttab", [128, TW], BF16)

    def phaseA_prep(h):
        wq = load_chunk(win_d[h])
        wk = load_chunk(win_d[8 + h])
        wv = load_chunk(win_d[16 + h])
        wz = load_chunk(win_d[24 + h])
        S.dma("sp", XT[:], bass.AP(tensor=sa_d.tensor, offset=h * RA, ap=[[1, 128], [1, TW]]), reads=["SA"],
              writes=["XT"])
        return wq, wk, wv, wz

    def phaseA_proj(ws, hook=None):
        wq, wk, wv, wz = ws
        proj(wq, qT, "qT", "dve")
        proj(wk, kT, "kT", "dve")
        if hook is not None:
            hook()
        proj(wv, aT, "aT", "act")
        proj(wz, zT, "zT", "silu")
        for g in range(2):
            bi = rbank()

            def tr(bi=bi, g=g):
                ins = None
                for t8 in range(8):
                    ti = 8 * g + t8
                    ins = pe.transpose(bankbf(bi)[:, t8 * 128:(t8 + 1) * 128], aT[:, ti * 128:(ti + 1) * 128], ident[:])
                return ins

            S.op("pe", tr, reads=names4("aT") + ["ident"], writes=[pn(bi)])
            S.op("act", lambda bi=bi, g=g: act.copy(out=vtok[:, 8 * g:8 * g + 8, :].rearrange("p a b -> p (a b)"),
                                                   in_=bankbf(bi)), reads=[pn(bi)], writes=[f"vtok{g}"])
        c = 0
        while c < TW:
            n = min(512, TW - c)
            bi = rbank()
            S.op("pe", lambda bi=bi, c=c, n=n: pe.matmul(bank(bi, slice(0, 128), 0, n), lhsT=jmat[:], rhs=XT[:, c:c + n],
                                                         start=True, stop=True), reads=["jmat", "XT"], writes=[pn(bi)])
            S.op("dve", lambda bi=bi, c=c, n=n: dve.tensor_copy(out=Ttab[:, c:c + n], in_=bank(bi, slice(0, 128), 0, n)),
                 reads=[pn(bi)], writes=["Ttab"])
            c += n

    OB_, DB_ = (0, 1), (2, 3)
    mmc = {"n": 0}

    def phaseA_attn(h):
        for qb in range(4):
            nkt = 4 * qb + 4
            pending = []

            def flush(keep):
                while len(pending) > keep:
                    pending.pop(0)()

            for kt in range(nkt):
                j = kt - 4 * qb
                c0 = 128 * j if j > 0 else 0
                off = 128 * (4 * qb - kt) + 384
                for m in range(2):
                    rows = slice(64 * m, 64 * m + 64)
                    sbk = rbank()
                    S.op("pe", lambda sbk=sbk, rows=rows, kt=kt, c0=c0, qb=qb: pe.matmul(
                        bank(sbk, slice(0, 128), c0, 512), lhsT=kT[rows, kt * 128:(kt + 1) * 128],
                        rhs=qT[rows, qb * 512 + c0:(qb + 1) * 512], start=True, stop=True),
                        reads=names4("kT") + names4("qT"), writes=[pn(sbk)])
                    pi = next_p()
                    S.op("act", lambda sbk=sbk, pi=pi, c0=c0: act.activation(out=Pb[pi][:, c0:512],
                                                                             in_=bank(sbk, slice(0, 128), c0, 512),
                                                                             func=AF.Exp, scale=0.125),
                         reads=[pn(sbk)], writes=[f"P{pi}"])
                    mmc["n"] += 1
                    if True:
                        S.op("dve", lambda pi=pi, c0=c0, off=off: dve.tensor_tensor(
                            out=Pb[pi][:, c0:512], in0=Pb[pi][:, c0:512], in1=Ttab[:, off + c0:off + 512], op=ALU.mult),
                            reads=[f"P{pi}", "Ttab"], writes=[f"P{pi}"])
                    else:
                        S.op("pool", lambda pi=pi, c0=c0, off=off: pool.tensor_tensor(
                            out=Pb[pi][:, c0:512], in0=Pb[pi][:, c0:512], in1=Ttab[:, off + c0:off + 512], op=ALU.mult),
                            reads=[f"P{pi}", "Ttab"], writes=[f"P{pi}"])

                    def av(m=m, pi=pi, c0=c0, kt=kt, nkt=nkt):
                        def f():
                            pe.matmul(bank(OB_[m], slice(0, 128), c0, 512), lhsT=vtok[:, kt, :], rhs=Pb[pi][:, c0:512],
                                      start=(kt == 0), stop=(kt == nkt - 1))
                            return pe.matmul(bank(DB_[m], slice(0, 128), c0, 512), lhsT=ones_bf[:], rhs=Pb[pi][:, c0:512],
                                             start=(kt == 0), stop=(kt == nkt - 1))

                        S.op("pe", f, reads=[f"P{pi}", "vtok0", "vtok1", "ones_bf"], writes=[pn(OB_[m]), pn(DB_[m])])

                    pending.append(av)
                if kt % 2 == 1:
                    flush(4)
                if kt == 3:
                    run_pending_finA()
            flush(0)
            fin_part1()
            finA["p2"] = (lambda h=h, qb=qb: fin_part2(h, qb))

    finA = {"p2": None}

    def run_pending_finA():
        f = finA["p2"]
        if f is not None:
            finA["p2"] = None
            f()

    def fin_part1():
        r0, r1, o0, o1, tt = tf
        S.op("act", lambda: act.activation(out=r0[:], in_=bank(DB_[0]), func=AF.Ln), reads=[pn(DB_[0])], writes=["tf0"])
        S.op("act", lambda: act.activation(out=r1[:], in_=bank(DB_[1]), func=AF.Ln), reads=[pn(DB_[1])], writes=["tf1"])
        S.op("act", lambda: act.activation(out=r0[:], in_=r0[:], func=AF.Exp, scale=-1.0), reads=["tf0"], writes=["tf0"])
        S.op("act", lambda: act.activation(out=r1[:], in_=r1[:], func=AF.Exp, scale=-1.0), reads=["tf1"], writes=["tf1"])
        S.op("dve", lambda: dve.tensor_tensor(out=o0[:], in0=bank(OB_[0]), in1=r0[:], op=ALU.mult),
             reads=[pn(OB_[0]), "tf0"], writes=["tf2"])
        S.op("dve", lambda: dve.tensor_tensor(out=o1[:], in0=bank(OB_[1]), in1=r1[:], op=ALU.mult),
             reads=[pn(OB_[1]), "tf1"], writes=["tf3"])
        S.op("dve", lambda: dve.scalar_tensor_tensor(out=o0[:], in0=o1[:], scalar=neglam[:, 0:1], in1=o0[:],
                                                     op0=ALU.mult, op1=ALU.add),
             reads=["tf2", "tf3", "neglam"], writes=["tf2"])
        S.op("pool", lambda: pool.tensor_tensor(out=sqb[:], in0=o0[:], in1=o0[:], op=ALU.mult), reads=["tf2"],
             writes=["sqb"])

    def fin_part2(h, qb):
        r0, r1, o0, o1, tt = tf
        bi = rbank()
        S.op("pe", lambda bi=bi: pe.matmul(bank(bi), lhsT=ones_bf[:], rhs=sqb[:], start=True, stop=True),
             reads=["sqb", "ones_bf"], writes=[pn(bi)])
        S.op("act", lambda bi=bi: act.activation(out=r0[:], in_=bank(bi), func=AF.Ln, bias=eps_a, scale=1.0 / 128.0),
             reads=[pn(bi), "eps_a", "tf0"], writes=["tf0"])
        S.op("act", lambda: act.activation(out=r1[:], in_=r0[:], func=AF.Exp, scale=-0.5), reads=["tf0", "tf1"],
             writes=["tf1"])
        S.op("dve", lambda: dve.tensor_tensor(out=tt[:], in0=o0[:], in1=r1[:], op=ALU.mult), reads=["tf2", "tf1"],
             writes=["tf4"])
        S.op("dve", lambda h=h, qb=qb: dve.scalar_tensor_tensor(
            out=yag[:, h, qb * 512:(qb + 1) * 512], in0=tt[:], scalar=subw_s, in1=zT[:, qb * 512:(qb + 1) * 512],
            op0=ALU.mult, op1=ALU.mult), reads=["tf4", "subw_s", f"zT{qb}"], writes=[f"yag{h}_{qb}"])

    ws = phaseA_prep(0)
    for h in range(8):
        phaseA_proj(ws, run_pending_finA)
        if h + 1 < 8:
            ws = phaseA_prep(h + 1)
        phaseA_attn(h)
    run_pending_finA()

    S.barrier()
    st_pa.close()
    st_s.close()

    st_pb = ExitStack()
    q4 = sb(st_pb, "q4", [128, S_TOK], BF16)
    k4 = sb(st_pb, "k4", [128, S_TOK], BF16)
    q16 = sb(st_pb, "q16", [128, S_TOK], BF16)
    k16 = sb(st_pb, "k16", [128, S_TOK], BF16)
    Vd = sb(st_pb, "Vd", [128, 3, 16, 193], BF16)
    XB = sb(st_pb, "XB", [128, 6 * 256], BF16)
    Btab = sb(st_pb, "Btab", [128, 6, 256], BF16)
    accs = sb(st_pb, "accs", [128, S_TOK], F32)
    zr = [sb(st_pb, f"zr{i}", [128, 512], F32) for i in range(2)]

    S.op("pool", lambda: pool.memset(Vd[:].rearrange("p a b c -> p (a b c)"), 0.0), writes=["Vd"])
    S.op("pool", lambda: pool.memset(Vd[:, :, :, 64:66], 1.0), reads=[], writes=["Vd"])

    def phaseB_prep(j):
        wq = load_chunk(win_d[32 + j])
        wk = load_chunk(win_d[40 + j])
        wv = load_chunk(win_d[48 + j])
        wz = load_chunk(win_d[56 + j])
        S.dma("sp", XB[:].rearrange("p (a b c) -> p a b c", a=2, b=3),
              bass.AP(tensor=sb_d.tensor, offset=2 * j * 3 * RB, ap=[[1, 128], [3 * RB, 2], [RB, 3], [1, 256]]),
              reads=["SB"], writes=["XB"])
        return wq, wk, wv, wz

    qperm = {1: qT, 4: q4, 16: q16}
    kperm = {1: kT, 4: k4, 16: k16}
    sring = {"rr": 0}

    def phaseB_proj(ws, hook=None):
        wq, wk, wv, wz = ws
        proj(wq, qT, "qT", "dve")
        proj(wk, kT, "kT", "dve")
        if hook is not None:
            hook()
        proj(wv, aT, "aT", "act")
        proj(wz, zT, "zT", "silu")
        for d, qd, kd in ((4, q4, k4), (16, q16, k16)):
            if d == 4:
                S.op("dve", lambda d=d, qd=qd: dve.tensor_copy(out=qd[:].rearrange("p (r l) -> p r l", r=d),
                                                              in_=qT[:].rearrange("p (l r) -> p r l", r=d)),
                     reads=names4("qT"), writes=[f"q{d}"])
                S.op("dve", lambda d=d, kd=kd: dve.tensor_copy(out=kd[:].rearrange("p (r l) -> p r l", r=d),
                                                              in_=kT[:].rearrange("p (l r) -> p r l", r=d)),
                     reads=names4("kT"), writes=[f"k{d}"])
            else:
                S.op("act", lambda d=d, qd=qd: act.copy(out=qd[:].rearrange("p (r l) -> p r l", r=d),
                                                       in_=qT[:].rearrange("p (l r) -> p r l", r=d)),
                     reads=names4("qT"), writes=[f"q{d}"])
                S.op("act", lambda d=d, kd=kd: act.copy(out=kd[:].rearrange("p (r l) -> p r l", r=d),
                                                       in_=kT[:].rearrange("p (l r) -> p r l", r=d)),
                     reads=names4("kT"), writes=[f"k{d}"])
        for di, d in enumerate(DILS):
            av = aT[:].rearrange("p (l r) -> p r l", r=d)
            nb = (S_TOK // d) // 128
            for g in range(2):
                bi = rbank()

                def tr(bi=bi, g=g, av=av, nb=nb):
                    ins = None
                    for t8 in range(8):
                        ti = 8 * g + t8
                        r, n = ti // nb, ti % nb
                        ins = pe.transpose(bankbf(bi)[:, t8 * 128:(t8 + 1) * 128], av[:, r, n * 128:(n + 1) * 128],
                                           ident[:])
                    return ins

                S.op("pe", tr, reads=names4("aT") + ["ident"], writes=[pn(bi)])
                src = bankbf(bi).rearrange("p (a b) -> p a b", a=8)
                S.op("act", lambda bi=bi, g=g, di=di, src=src: act.copy(out=Vd[:, di, 8 * g:8 * g + 8, 0:64],
                                                                       in_=src[:, :, 0:64]),
                     reads=[pn(bi)], writes=["Vd"])
                S.op("dve", lambda bi=bi, g=g, di=di, src=src: dve.tensor_copy(out=Vd[:, di, 8 * g:8 * g + 8, 129:193],
                                                                              in_=src[:, :, 64:128]),
                     reads=[pn(bi)], writes=["Vd"])
        for c in range(3):
            bi = rbank()
            S.op("pe", lambda bi=bi, c=c: pe.matmul(bank(bi), lhsT=jmat[:], rhs=XB[:, c * 512:(c + 1) * 512], start=True,
                                                    stop=True), reads=["jmat", "XB"], writes=[pn(bi)])
            S.op("dve", lambda bi=bi, c=c: dve.tensor_copy(out=Btab[:, 2 * c:2 * c + 2, :].rearrange("p a b -> p (a b)"),
                                                          in_=bank(bi)), reads=[pn(bi)], writes=["Btab"])

    def phaseB_attn(j):
        for hh in range(2):
            rows = slice(64 * hh, 64 * hh + 64)
            started = set()
            tiles = []
            for di, d in enumerate(DILS):
                nb = (S_TOK // d) // 128
                for r in range(d):
                    for n in range(nb):
                        tiles.append((di, d, nb, r, n))
            pending = []

            def flush(keep):
                while len(pending) > keep:
                    pending.pop(0)()

            for gi in range(0, len(tiles), 2):
                grp = tiles[gi:gi + 2]
                di, d = grp[0][0], grp[0][1]
                sbk = rbank()

                def qk(grp=grp, sbk=sbk, rows=rows):
                    ins = None
                    for s, (di_, d_, nb, r, n) in enumerate(grp):
                        nq = 256 if n + 1 < nb else 128
                        col = (r * nb + n) * 128
                        ins = pe.matmul(bank(sbk, slice(0, 128), s * 256, s * 256 + nq),
                                        lhsT=kperm[d_][rows, col:col + 128], rhs=qperm[d_][rows, col:col + nq],
                                        start=True, stop=True)
                    return ins

                qn = names4("qT") if d == 1 else [f"q{d}"]
                kn = names4("kT") if d == 1 else [f"k{d}"]
                S.op("pe", qk, reads=qn + kn, writes=[pn(sbk)])
                pi = next_p()
                S.op("act", lambda pi=pi, sbk=sbk: act.activation(out=Pb[pi][:], in_=bank(sbk), func=AF.Exp, scale=0.125),
                     reads=[pn(sbk)], writes=[f"P{pi}"])
                bsl = Btab[:, hh * 3 + di, :]
                bap = bass.AP(tensor=bsl.tensor, offset=bsl.offset, ap=[list(bsl.ap[0]), [0, 2], [1, 256]])
                pv = Pb[pi][:].rearrange("p (a b) -> p a b", a=2)
                if True:
                    S.op("dve", lambda pv=pv, bap=bap: dve.tensor_tensor(out=pv, in0=pv, in1=bap, op=ALU.mult),
                         reads=[f"P{pi}", "Btab"], writes=[f"P{pi}"])
                else:
                    S.op("pool", lambda pv=pv, bap=bap: pool.tensor_tensor(out=pv, in0=pv, in1=bap, op=ALU.mult),
                         reads=[f"P{pi}", "Btab"], writes=[f"P{pi}"])

                def av(grp=grp, pi=pi, hh=hh):
                    wr = set()
                    plan = []
                    mrows = slice(0, 128)
                    for s, (di_, d_, nb, r, n) in enumerate(grp):
                        ti = r * nb + n
                        lhs = Vd[:, di_, ti, 0:128] if hh == 0 else Vd[:, di_, ti, 65:193]
                        for half in range(2):
                            nn = n + half
                            if nn >= nb:
                                continue
                            pc0 = s * 256 + half * 128
                            if d_ == 1:
                                b_ = nn // 4
                                plan.append((b_, bank(b_, mrows, (nn % 4) * 128, (nn % 4) * 128 + 128), lhs,
                                             Pb[pi][:, pc0:pc0 + 128]))
                            elif d_ == 4:
                                b_ = nn
                                o = bank(b_, mrows).rearrange("p (j d) -> p d j", d=4)[:, r, :]
                                plan.append((b_, o, lhs, Pb[pi][:, pc0:pc0 + 128]))
                            else:
                                for b_ in range(4):
                                    o = bank(b_, mrows).rearrange("p (j d) -> p d j", d=16)[:, r, :]
                                    plan.append((b_, o, lhs, Pb[pi][:, pc0 + 32 * b_:pc0 + 32 * b_ + 32]))
                    for b_, *_ in plan:
                        wr.add(b_)

                    def f():
                        ins = None
                        for b_, o, lhs, rhs in plan:
                            st = b_ not in started
                            started.add(b_)
                            ins = pe.matmul(o, lhsT=lhs, rhs=rhs, start=st, stop=False, skip_group_check=True)
                        return ins

                    S.op("pe", f, reads=[f"P{pi}", "Vd"], writes=[pn(b_) for b_ in sorted(wr)])

                pending.append(av)
                if (gi // 2) % 4 == 3:
                    flush(4)
                if gi == 6:
                    run_pending_finB()
            flush(0)
            finB_part1(hh)
            finB["p2"] = (lambda j=j, hh=hh: finB_part2(j, hh))

    finB = {"p2": None}

    def run_pending_finB():
        f = finB["p2"]
        if f is not None:
            finB["p2"] = None
            f()

    def finB_part1(hh):
        mrows = slice(0, 65) if hh == 0 else slice(0, 128)
        for pc in range(4):
            csl = slice(pc * 512, (pc + 1) * 512)
            if pc % 2 == 0:
                S.op("act", lambda pc=pc, csl=csl, mrows=mrows: act.copy(out=accs[mrows, csl], in_=bank(pc, mrows)),
                     reads=[pn(pc)], writes=[f"accs{pc}"])
            else:
                S.op("dve", lambda pc=pc, csl=csl, mrows=mrows: dve.tensor_copy(out=accs[mrows, csl], in_=bank(pc, mrows)),
                     reads=[pn(pc)], writes=[f"accs{pc}"])
        row = 64 if hh == 0 else 0
        rsl = slice(row, row + 1)
        an = [f"accs{pc}" for pc in range(4)]
        S.op("act", lambda rsl=rsl: act.activation(out=accs[rsl, :], in_=accs[rsl, :], func=AF.Ln), reads=an, writes=an)
        S.op("act", lambda rsl=rsl: act.activation(out=accs[rsl, :], in_=accs[rsl, :], func=AF.Exp, scale=-1.0), reads=an,
             writes=an)

    def finB_part2(j, hh):
        rows = slice(64 * hh, 64 * hh + 64)
        row = 64 if hh == 0 else 0
        rsl = slice(row, row + 1)
        mm_rows = slice(0, 64) if hh == 0 else slice(0, 128)
        mcols = 64 if hh == 0 else 128
        for pc in range(4):
            csl = slice(pc * 512, (pc + 1) * 512)
            bi = rbank()
            S.op("pe", lambda bi=bi, csl=csl: pe.matmul(bank(bi, mm_rows), lhsT=ones_f[rsl, 0:mcols], rhs=accs[rsl, csl],
                                                        start=True, stop=True),
                 reads=[f"accs{pc}", "ones_f"], writes=[pn(bi)])
            k2 = pc % 2
            S.op("dve", lambda bi=bi, k2=k2, csl=csl: dve.tensor_tensor(out=zr[k2][rows, :], in0=bank(bi, rows),
                                                                       in1=accs[rows, csl], op=ALU.mult),
                 reads=[pn(bi), f"accs{pc}"], writes=[f"zr{k2}"])
            S.op("pool", lambda k2=k2, csl=csl: pool.tensor_tensor(out=ybg[rows, j, csl], in0=zr[k2][rows, :],
                                                                  in1=zT[rows, csl], op=ALU.mult),
                 reads=[f"zr{k2}", f"zT{pc}"], writes=[f"ybg{j}_{pc}_{hh}"])

    ws = phaseB_prep(0)
    for j in range(8):
        phaseB_proj(ws, run_pending_finB)
        if j + 1 < 8:
            ws = phaseB_prep(j + 1)
        phaseB_attn(j)
    run_pending_finB()

    S.barrier()
    st_pb.close()
    st_a.close()

    if DEBUG:
        st_d = ExitStack()
        dtmp = sb(st_d, "dtmp", [128, 8 * S_TOK], F32)
        S.op("dve", lambda: dve.tensor_copy(out=dtmp[:], in_=yag[:].rearrange("p a b -> p (a b)")), writes=["dtmp"])
        S.dma("sp", dbg_a[:], dtmp[:], reads=["dtmp"], writes=["dbg_a"])
        S.op("dve", lambda: dve.tensor_copy(out=dtmp[:], in_=ybg[:].rearrange("p a b -> p (a b)")), reads=["dbg_a"],
             writes=["dtmp"])
        S.dma("sp", dbg_b[:], dtmp[:], reads=["dtmp"], writes=["dbg_b"])
        S.barrier()
        st_d.close()

    st_c = ExitStack()
    merged = sb(st_c, "merged", [128, 8, S_TOK], BF16)
    wout = sb(st_c, "wout_sb", [128, 8, D], BF16)
    NWC = 8
    wc = [sb(st_c, f"wc{i}", [128, 1024], BF16) for i in range(NWC)]
    sg = [sb(st_c, f"sg{i}", [128, 512], F32) for i in range(4)]
    t12 = [sb(st_c, f"t12_{i}", [128, 512], F32) for i in range(4)]
    lng = sb(st_c, "lng_sb", [128, D], F32)
    lnb = sb(st_c, "lnb_sb", [128, D], F32)
    xtk = [sb(st_c, f"xtk{i}", [128, D], F32) for i in range(2)]
    yb_ = [sb(st_c, f"ybuf{i}", [128, D], F32) for i in range(2)]
    stats = sb(st_c, "stats", [128, 2, 6], F32)
    mv = sb(st_c, "mv", [128, 8], F32)

    S.dma("sp", lng[:], bass.AP(tensor=lng_d.tensor, offset=0, ap=[[0, 128], [1, D]]), writes=["lng"])
    S.dma("sp", lnb[:], bass.AP(tensor=lnb_d.tensor, offset=0, ap=[[0, 128], [1, D]]), writes=["lnb"])
    wcs = {"rr": 0}

    def load_c(src_ap):
        i = wcs["rr"]
        wcs["rr"] = (i + 1) % NWC
        S.dma("pool", wc[i][:], src_ap, writes=[f"wc{i}"])
        return i

    def prepC(dc):
        return (load_c(pa_d[dc]), load_c(pb_d[dc]), load_c(win_d[64 + dc]), load_c(win_d[72 + dc]))

    pring = {"rr": 0}

    def nbank():
        i = pring["rr"]
        pring["rr"] = (i + 1) % 8
        return i

    cw = prepC(0)
    it = 0
    for dc in range(8):
        wpa, wpb, wga, wgb = cw
        if dc + 1 < 8:
            cw = prepC(dc + 1)
        S.dma("pool", wout[:, dc, :], wout_d[dc * 128:(dc + 1) * 128, :], writes=[f"wout{dc}"])
        for tb in range(4):
            tsl = slice(tb * 512, (tb + 1) * 512)
            k2 = it % 2
            it += 1
            banks = {}
            for nm, wi, src, snames in (("ua", wpa, yag, [f"yag{kc}_{tb}" for kc in range(8)]),
                                        ("ga", wga, xT, XT_NAMES),
                                        ("ub", wpb, ybg, [f"ybg{kc}_{tb}_{hh}" for kc in range(8) for hh in range(2)]),
                                        ("gb", wgb, xT, XT_NAMES)):
                bi = nbank()
                banks[nm] = bi

                def mm(bi=bi, wi=wi, src=src, tsl=tsl):
                    ins = None
                    for kc in range(8):
                        ins = pe.matmul(bank(bi), lhsT=wc[wi][:, kc * 128:(kc + 1) * 128], rhs=src[:, kc, tsl],
                                        start=(kc == 0), stop=(kc == 7))
                    return ins

                S.op("pe", mm, reads=[f"wc{wi}"] + snames, writes=[pn(bi)])
            for gi, nm in enumerate(("ga", "gb")):
                bi = banks[nm]
                sgi = 2 * k2 + gi
                S.op("act", lambda bi=bi, sgi=sgi, gi=gi, dc=dc: act.activation(
                    out=sg[sgi][:], in_=bank(bi), func=AF.Sigmoid, bias=bg[:, gi * 8 + dc:gi * 8 + dc + 1]),
                    reads=[pn(bi), "bg"], writes=[f"sg{sgi}"])
            for gi, nm in enumerate(("ua", "ub")):
                bi = banks[nm]
                sgi = 2 * k2 + gi
                S.op("dve", lambda bi=bi, sgi=sgi: dve.tensor_tensor(out=t12[sgi][:], in0=bank(bi), in1=sg[sgi][:],
                                                                    op=ALU.mult),
                     reads=[pn(bi), f"sg{sgi}"], writes=[f"t12_{sgi}"])
            S.op("pool", lambda k2=k2, dc=dc, tsl=tsl: pool.tensor_tensor(out=merged[:, dc, tsl], in0=t12[2 * k2][:],
                                                                         in1=t12[2 * k2 + 1][:], op=ALU.add),
                 reads=[f"t12_{2 * k2}", f"t12_{2 * k2 + 1}"], writes=[f"mg{dc}_{tb}"])

    def out_stage1(tt):
        k2 = tt % 2
        tb = tt // 4
        S.dma("sp", xtk[k2][:], xtok_d[tt * 128:(tt + 1) * 128, :], writes=[f"xtk{k2}"])
        b0 = nbank()
        b1 = nbank()
        for half, bi in enumerate((b0, b1)):
            def mm(bi=bi, half=half, tt=tt):
                ins = None
                for kc in range(8):
                    ins = pe.matmul(bank(bi), lhsT=merged[:, kc, tt * 128:(tt + 1) * 128],
                                    rhs=wout[:, kc, half * 512:(half + 1) * 512], start=(kc == 0), stop=(kc == 7))
                return ins

            S.op("pe", mm, reads=[f"mg{kc}_{tb}" for kc in range(8)] + [f"wout{kc}" for kc in range(8)], writes=[pn(bi)])
        for half, bi in enumerate((b0, b1)):
            hs = slice(half * 512, (half + 1) * 512)
            S.op("dve", lambda bi=bi, hs=hs, k2=k2: dve.scalar_tensor_tensor(
                out=yb_[k2][:, hs], in0=xtk[k2][:, hs], scalar=ALPHA, in1=bank(bi), op0=ALU.mult, op1=ALU.add),
                reads=[pn(bi), f"xtk{k2}"], writes=[f"yb{k2}_{half}"])
            S.op("dve", lambda hs=hs, k2=k2, half=half: dve.bn_stats(out=stats[:, half, :], in_=yb_[k2][:, hs]),
                 reads=[f"yb{k2}_{half}"], writes=[f"stats{half}"])
        S.op("dve", lambda: dve.bn_aggr(out=mv[:, 0:2], in_=stats[:].rearrange("p a b -> p (a b)")),
             reads=["stats0", "stats1"], writes=["mv01"])
        S.op("act", lambda: act.activation(out=mv[:, 2:3], in_=mv[:, 1:2], func=AF.Ln, bias=eps_l), reads=["mv01", "eps_l"],
             writes=["mv2"])
        S.op("act", lambda: act.activation(out=mv[:, 3:4], in_=mv[:, 2:3], func=AF.Exp, scale=-0.5), reads=["mv2"],
             writes=["mv3"])
        ynames = [f"yb{k2}_0", f"yb{k2}_1"]
        S.op("dve", lambda: dve.tensor_scalar(out=mv[:, 4:5], in0=mv[:, 0:1], scalar1=mv[:, 3:4], scalar2=-1.0,
                                              op0=ALU.mult, op1=ALU.mult), reads=["mv01", "mv3"], writes=["mv4"])
        S.op("act", lambda k2=k2: act.activation(out=yb_[k2][:], in_=yb_[k2][:], func=AF.Identity, bias=mv[:, 4:5],
                                                 scale=mv[:, 3:4]), reads=ynames + ["mv3", "mv4"], writes=ynames)

    def out_stage2(tt):
        k2 = tt % 2
        ynames = [f"yb{k2}_0", f"yb{k2}_1"]
        S.op("dve", lambda k2=k2: dve.tensor_tensor(out=yb_[k2][:], in0=yb_[k2][:], in1=lng[:], op=ALU.mult),
             reads=ynames + ["lng"], writes=ynames)
        S.op("dve", lambda k2=k2: dve.tensor_tensor(out=yb_[k2][:], in0=yb_[k2][:], in1=lnb[:], op=ALU.add),
             reads=ynames + ["lnb"], writes=ynames)
        S.dma("sp", out_d[tt * 128:(tt + 1) * 128, :], yb_[k2][:], reads=ynames, writes=[f"out{tt}"])

    for tt in range(16):
        out_stage1(tt)
        if tt > 0:
            out_stage2(tt - 1)
    out_stage2(15)

    if DEBUG:
        S.barrier()
        S.dma("sp", dbg_m[:], merged[:].rearrange("p a b -> p (a b)"), writes=["dbg_m"])
    S.wait_all_dma("sp")
    S.barrier()
    st_c.close()
    es.close()
    return nc


_CACHE = {}


def _chunks(w, n):
    return np.ascontiguousarray(w.reshape(8, 128, n, 128).transpose(2, 1, 0, 3).reshape(n, 128, 1024))


def kernel(x, w_in, b_gate, w_proj_a, w_proj_b, w_out, da_lambda_q1, da_lambda_k1, da_lambda_q2, da_lambda_k2,
           da_subln_w, rel_bias, ln_g, ln_b):
    f = np.float32
    x = np.asarray(x, f)
    B = x.shape[0]
    if "nc" not in _CACHE:
        _CACHE["nc"] = build_nc()
        _CACHE["oh"] = _onehots()
    nc = _CACHE["nc"]
    oha, ohb = _CACHE["oh"]
    win_r = _chunks(np.asarray(w_in, f)[0], NCHUNK)
    pa_r = _chunks(np.asarray(w_proj_a, f)[0], 8)
    pb_r = _chunks(np.asarray(w_proj_b, f)[0], 8)
    wout = np.ascontiguousarray(np.asarray(w_out, f)[0])
    bg = np.ascontiguousarray(np.asarray(b_gate, f)[0].reshape(2, 8, 128).transpose(2, 0, 1).reshape(128, 16))
    lam = np.ascontiguousarray(np.concatenate([np.asarray(a, f).reshape(1, 64) for a in
                                               (da_lambda_q1, da_lambda_k1, da_lambda_q2, da_lambda_k2)], axis=0))
    subw = np.ascontiguousarray(np.asarray(da_subln_w, f).reshape(128, 1))
    relb = np.ascontiguousarray(np.asarray(rel_bias, f))
    lng = np.ascontiguousarray(np.asarray(ln_g, f).reshape(1, D))
    lnb = np.ascontiguousarray(np.asarray(ln_b, f).reshape(1, D))
    ident = np.eye(128, dtype=f)
    jmat = np.ascontiguousarray(np.eye(128, dtype=f)[::-1])
    shared = {"win": win_r, "pa": pa_r, "pb": pb_r, "wout": wout, "bg": bg, "lam": lam, "subw": subw, "relb": relb,
              "lng": lng, "lnb": lnb, "oha": oha, "ohb": ohb, "ident": ident, "jmat": jmat}
    in_maps = []
    for b in range(B):
        m = dict(shared)
        m["xT"] = np.ascontiguousarray(x[b].T)
        m["xtok"] = np.ascontiguousarray(x[b])
        in_maps.append(m)
    res = run_bass_kernel_spmd(nc, in_maps, core_ids=list(range(B)))
    _CACHE["last"] = res
    return np.stack([np.asarray(r["out"], f) for r in res.results], axis=0)
```
